# Optimizing a Trainium2 kernel written in Bass

```python
import math
import jax
import jax.numpy as jnp
from jax import lax
import numpy as np

D_MODEL = 1024
BATCH = 16
SEQ = 4096
DEPTH = 2
DEC_BATCH = 16
DEC_SEQ = 2048
PAST_LEN = 128

PLE_DIM = 256
HG_HEADS = 4
HG_DK = 64
HG_DV = 64
HG_WIDTH = HG_HEADS * HG_DV
HG_CHUNK = 64
LRU_WIDTH = D_MODEL // 2
LRU_BLOCKS = 8
LRU_BLOCK = LRU_WIDTH // LRU_BLOCKS
LRU_C = 8.0
CONV_W = 4
DA_HEADS = 4
DA_HEAD_QK = 32
DA_HEAD_V = 2 * DA_HEAD_QK
DA_WIDTH = DA_HEADS * DA_HEAD_V
ROPE_THETA = 500000.0
ROPE_DIM = DA_HEAD_QK // 4
Q_BLOCK = 128
D_MIX = HG_WIDTH + LRU_WIDTH + DA_WIDTH
RMS_EPS = 1e-6
SPLIT_SIZES = (HG_HEADS * HG_DK, HG_HEADS * HG_DK, HG_HEADS * HG_DK, HG_WIDTH, HG_WIDTH,
               LRU_WIDTH, LRU_WIDTH,
               2 * DA_HEADS * DA_HEAD_QK, 2 * DA_HEADS * DA_HEAD_QK, DA_WIDTH, DA_WIDTH)
D_IN = sum(SPLIT_SIZES)

kernel_name = 'hybrid_bidir_hgrn2_rglru_diffattn'


def rmsnorm(x, g):
    x32 = x.astype(jnp.float32)
    y = x32 * lax.rsqrt(jnp.mean(x32 * x32, axis=-1, keepdims=True) + RMS_EPS)
    return (y * g.astype(jnp.float32)).astype(x.dtype)


def hgrn2_scan(q, logf, k, v):
    B, S, H, K = q.shape
    V = v.shape[-1]
    C = HG_CHUNK
    N = S // C

    def chunks(t):
        return t.reshape(B, N, C, H, t.shape[-1]).transpose(1, 0, 3, 2, 4)

    qc, kc, vc = chunks(q), chunks(k), chunks(v)
    cc = jnp.cumsum(chunks(logf), axis=3)
    causal = jnp.tril(jnp.ones((C, C), dtype=bool))[:, :, None]

    def step(state, inp):
        qb, cb, kb, vb = inp
        diff = cb[:, :, :, None, :] - cb[:, :, None, :, :]
        decay = jnp.where(causal, jnp.exp(jnp.where(causal, diff, 0.0)), 0.0)
        scores = jnp.einsum('bhik,bhjk,bhijk->bhij', qb, kb, decay)
        o = (jnp.einsum('bhij,bhjv->bhiv', scores, vb)
             + jnp.einsum('bhik,bhkv->bhiv', qb * jnp.exp(cb), state))
        c_last = cb[:, :, -1:, :]
        new_state = (jnp.exp(c_last)[:, :, 0, :, None] * state
                     + jnp.einsum('bhjk,bhjv->bhkv', kb * jnp.exp(c_last - cb), vb))
        return new_state, o

    init = jnp.zeros((B, H, K, V), dtype=jnp.float32)
    _, o = lax.scan(step, init, (qc, cc, kc, vc))
    return o.transpose(1, 0, 3, 2, 4).reshape(B, S, H, V)


def hgrn2_branch(zq, zf_fwd, zf_bwd, zi, zg, lb, norm_g):
    B, S, _ = zq.shape

    def heads(t):
        return t.astype(jnp.float32).reshape(B, S, HG_HEADS, -1)

    q = jax.nn.silu(heads(zq))
    v = heads(zi)

    def gates(zf, lb_d):
        zf = heads(zf)
        lb_d = lb_d.reshape(HG_HEADS, HG_DK)
        f = lb_d + (1.0 - lb_d) * jax.nn.sigmoid(zf)
        k = (1.0 - lb_d) * jax.nn.sigmoid(-zf)
        return jnp.log(f), k

    logf_f, k_f = gates(zf_fwd, lb[0])
    logf_b, k_b = gates(zf_bwd, lb[1])

    def flip(t):
        return jnp.flip(t, axis=1)

    o = (hgrn2_scan(q, logf_f, k_f, v)
         + flip(hgrn2_scan(flip(q), flip(logf_b), flip(k_b), flip(v))))
    o = rmsnorm(o, norm_g).reshape(B, S, HG_WIDTH)
    return o.astype(zg.dtype) * jax.nn.silu(zg)


def _linear_combine(e1, e2):
    a1, b1 = e1
    a2, b2 = e2
    return a1 * a2, a2 * b1 + b2


def rglru_branch(zx, zg, conv_w, conv_b, wa, ba, wx, bx, lam):
    B, S, W = zx.shape
    left = CONV_W // 2
    xp = jnp.pad(zx, ((0, 0), (left, CONV_W - 1 - left), (0, 0)))
    u = conv_b
    for j in range(CONV_W):
        u = u + xp[:, j:j + S] * conv_w[j]
    u32 = u.astype(jnp.float32)
    ub = u32.reshape(B, S, LRU_BLOCKS, LRU_BLOCK)

    def direction(d, reverse):
        r = jax.nn.sigmoid(jnp.einsum('bsnc,nce->bsne', ub, wa[d].astype(jnp.float32)).reshape(B, S, W)
                           + ba[d].astype(jnp.float32))
        i = jax.nn.sigmoid(jnp.einsum('bsnc,nce->bsne', ub, wx[d].astype(jnp.float32)).reshape(B, S, W)
                           + bx[d].astype(jnp.float32))
        log_a = -LRU_C * jax.nn.softplus(-lam[d].astype(jnp.float32)) * r
        a = jnp.exp(log_a)
        b = jnp.sqrt(-jnp.expm1(2.0 * log_a)) * (i * u32)
        _, h = lax.associative_scan(_linear_combine, (a, b), axis=1, reverse=reverse)
        return h

    h = direction(0, False) + direction(1, True)
    return h.astype(zx.dtype) * jax.nn.silu(zg)


def rope_partial(t, pos):
    half = ROPE_DIM // 2
    inv = ROPE_THETA ** (-jnp.arange(half, dtype=jnp.float32) * 2.0 / ROPE_DIM)
    ang = pos.astype(jnp.float32)[:, None] * inv[None, :]
    cos = jnp.cos(ang)[None, :, None, :]
    sin = jnp.sin(ang)[None, :, None, :]
    t32 = t.astype(jnp.float32)
    t1, t2 = t32[..., :half], t32[..., half:ROPE_DIM]
    rot = jnp.concatenate([t1 * cos - t2 * sin, t2 * cos + t1 * sin], axis=-1)
    return jnp.concatenate([rot.astype(t.dtype), t[..., ROPE_DIM:]], axis=-1)


def diff_attn_branch(zq, zk, zv, zg, lq1, lk1, lq2, lk2, norm_g, lam_init):
    B, S, _ = zq.shape
    pos = jnp.arange(S)
    q = rope_partial(zq.reshape(B, S, 2 * DA_HEADS, DA_HEAD_QK), pos) * (DA_HEAD_QK ** -0.5)
    k = rope_partial(zk.reshape(B, S, 2 * DA_HEADS, DA_HEAD_QK), pos)
    v = zv.reshape(B, S, DA_HEADS, DA_HEAD_V)
    lam = (jnp.exp(jnp.sum(lq1.astype(jnp.float32) * lk1.astype(jnp.float32)))
           - jnp.exp(jnp.sum(lq2.astype(jnp.float32) * lk2.astype(jnp.float32))) + lam_init)
    nb = S // Q_BLOCK
    qb = q.reshape(B, nb, Q_BLOCK, 2 * DA_HEADS, DA_HEAD_QK).transpose(1, 0, 3, 2, 4)
    kt = k.transpose(0, 2, 1, 3)
    vt = v.transpose(0, 2, 1, 3).astype(jnp.float32)

    def block(qblk):
        s = jnp.einsum('bhqd,bhkd->bhqk', qblk, kt).astype(jnp.float32)
        p = jax.nn.softmax(s, axis=-1).reshape(B, DA_HEADS, 2, Q_BLOCK, S)
        w = p[:, :, 0] - lam * p[:, :, 1]
        return jnp.einsum('bhqk,bhkv->bhqv', w, vt)

    o = lax.map(block, qb)
    o = o.transpose(1, 0, 3, 2, 4).reshape(B, S, DA_HEADS, DA_HEAD_V)
    o = rmsnorm(o, norm_g) * (1.0 - lam_init)
    return o.reshape(B, S, DA_WIDTH).astype(zg.dtype) * jax.nn.silu(zg)


def trunk(x, p, norm_g, w_in, w_out, hg_lb, hg_norm, lru_conv_w, lru_conv_b, lru_wa, lru_ba,
          lru_wx, lru_bx, lru_lam, da_lq1, da_lk1, da_lq2, da_lk2, da_norm, ple_w, ple_gate_w,
          final_norm):
    lb_all = jnp.cumsum(jax.nn.softmax(hg_lb.astype(jnp.float32), axis=0), axis=0)
    lb_all = lb_all - lb_all[:1]
    split_at = np.cumsum(SPLIT_SIZES)[:-1].tolist()
    h = x
    for l in range(DEPTH):
        hn = rmsnorm(h, norm_g[l])
        z = hn @ w_in[l]
        (hq, hf_f, hf_b, hi, hg, lx, lg, dq, dk, dv, dg) = jnp.split(z, split_at, axis=-1)
        o_hg = hgrn2_branch(hq, hf_f, hf_b, hi, hg, lb_all[l], hg_norm[l])
        o_lru = rglru_branch(lx, lg, lru_conv_w[l], lru_conv_b[l], lru_wa[l], lru_ba[l],
                             lru_wx[l], lru_bx[l], lru_lam[l])
        lam_init = 0.8 - 0.6 * math.exp(-0.3 * l)
        o_da = diff_attn_branch(dq, dk, dv, dg, da_lq1[l], da_lk1[l], da_lq2[l], da_lk2[l],
                                da_norm[l], lam_init)
        o = jnp.concatenate([o_hg, o_lru, o_da], axis=-1)
        h = h + o @ w_out[l]
        gate = jax.nn.sigmoid(h @ ple_gate_w[l])
        h = h + gate * (p[l] @ ple_w[l])
    return rmsnorm(h, final_norm)


def setup_inputs(seed: int = 0) -> dict:
    key = jax.random.key(seed)
    ks = jax.random.split(key, 24)
    f32 = jnp.float32

    def nrm(k, shape, scale):
        return jax.random.normal(k, shape, dtype=f32) * scale

    a0 = jax.random.uniform(ks[15], (DEPTH, 2, LRU_WIDTH), dtype=f32, minval=0.9, maxval=0.999)
    s0 = a0 ** (1.0 / LRU_C)
    return {
        'x_prompt': nrm(ks[0], (BATCH, SEQ, D_MODEL), 1.0),
        'x_sample': nrm(ks[1], (DEC_BATCH, DEC_SEQ, D_MODEL), 1.0),
        'p_prompt': nrm(ks[2], (DEPTH, BATCH, SEQ, PLE_DIM), 1.0),
        'p_sample': nrm(ks[3], (DEPTH, DEC_BATCH, DEC_SEQ, PLE_DIM), 1.0),
        'norm_g': 1.0 + nrm(ks[4], (DEPTH, D_MODEL), 0.05),
        'w_in': nrm(ks[5], (DEPTH, D_MODEL, D_IN), D_MODEL ** -0.5),
        'w_out': nrm(ks[6], (DEPTH, D_MIX, D_MODEL), D_MIX ** -0.5),
        'hg_lb': nrm(ks[7], (DEPTH, 2, HG_HEADS * HG_DK), 0.5),
        'hg_norm': 1.0 + nrm(ks[8], (DEPTH, HG_DV), 0.05),
        'lru_conv_w': nrm(ks[9], (DEPTH, CONV_W, LRU_WIDTH), CONV_W ** -0.5),
        'lru_conv_b': nrm(ks[10], (DEPTH, LRU_WIDTH), 0.01),
        'lru_wa': nrm(ks[11], (DEPTH, 2, LRU_BLOCKS, LRU_BLOCK, LRU_BLOCK), LRU_BLOCK ** -0.5),
        'lru_ba': nrm(ks[12], (DEPTH, 2, LRU_WIDTH), 0.01),
        'lru_wx': nrm(ks[13], (DEPTH, 2, LRU_BLOCKS, LRU_BLOCK, LRU_BLOCK), LRU_BLOCK ** -0.5),
        'lru_bx': nrm(ks[14], (DEPTH, 2, LRU_WIDTH), 0.01),
        'lru_lam': jnp.log(s0) - jnp.log1p(-s0),
        'da_lq1': nrm(ks[16], (DEPTH, DA_HEAD_QK), 0.1),
        'da_lk1': nrm(ks[17], (DEPTH, DA_HEAD_QK), 0.1),
        'da_lq2': nrm(ks[18], (DEPTH, DA_HEAD_QK), 0.1),
        'da_lk2': nrm(ks[19], (DEPTH, DA_HEAD_QK), 0.1),
        'da_norm': 1.0 + nrm(ks[20], (DEPTH, DA_HEAD_V), 0.05),
        'ple_w': nrm(ks[21], (DEPTH, PLE_DIM, D_MODEL), PLE_DIM ** -0.5),
        'ple_gate_w': nrm(ks[22], (DEPTH, D_MODEL, D_MODEL), D_MODEL ** -0.5),
        'final_norm': 1.0 + nrm(ks[23], (D_MODEL,), 0.05),
    }


def reference(x_prompt, x_sample, p_prompt, p_sample, norm_g, w_in, w_out, hg_lb, hg_norm,
              lru_conv_w, lru_conv_b, lru_wa, lru_ba, lru_wx, lru_bx, lru_lam, da_lq1, da_lk1,
              da_lq2, da_lk2, da_norm, ple_w, ple_gate_w, final_norm):
    y_prompt = trunk(x_prompt, p_prompt, norm_g, w_in, w_out, hg_lb, hg_norm, lru_conv_w,
                     lru_conv_b, lru_wa, lru_ba, lru_wx, lru_bx, lru_lam, da_lq1, da_lk1,
                     da_lq2, da_lk2, da_norm, ple_w, ple_gate_w, final_norm)
    y_sample = trunk(x_sample, p_sample, norm_g, w_in, w_out, hg_lb, hg_norm, lru_conv_w,
                     lru_conv_b, lru_wa, lru_ba, lru_wx, lru_bx, lru_lam, da_lq1, da_lk1,
                     da_lq2, da_lk2, da_norm, ple_w, ple_gate_w, final_norm)
    return (y_prompt, y_sample)
```

```python
import numpy as np
import ml_dtypes
from contextlib import ExitStack
import concourse.bass as bass
import concourse.mybir as mybir
from concourse.bass_utils import run_bass_kernel_spmd

F32 = mybir.dt.float32
BF16 = mybir.dt.bfloat16
AF = mybir.ActivationFunctionType
ALU = mybir.AluOpType

D = 1024
DIN = 3328
PLE = 256
DEPTH = 2
NFM = 26
EPS = 1e-6
ENGS = ("pe", "act", "dve", "pool", "sp")
N_DMA_SEMS = 12
PHASES = ("w", "0", "1", "h", "l", "d", "3", "L2")
FUSE_LRU = False
HG_STOP = 99


class Buf:
    __slots__ = ("name", "w", "rs", "rd")

    def __init__(self, name=""):
        self.name = name
        self.w = None
        self.rs = {}
        self.rd = []


class Ev:
    __slots__ = ("eng", "fn", "deps", "need_inc", "semkey", "semval", "is_dma", "prev_dma")

    def __init__(self, eng, fn, is_dma=False):
        self.eng = eng
        self.fn = fn
        self.deps = ()
        self.need_inc = False
        self.semkey = None
        self.semval = 0
        self.is_dma = is_dma
        self.prev_dma = None


class Prog:
    def __init__(self, nc):
        self.nc = nc
        self.q = {e: [] for e in ENGS}
        self.dma_rr = {e: 0 for e in ENGS}
        self.dma_cnt = {}
        self.dma_last = {}
        self.all_bufs = []
        self.last_ev = {e: None for e in ENGS}

    def buf(self, name=""):
        b = Buf(name)
        self.all_bufs.append(b)
        return b

    def bufs(self, n, name=""):
        return [self.buf(f"{name}{i}") for i in range(n)]

    def _record(self, ev, reads, writes):
        deps = set()
        for b in reads:
            if b.w is not None:
                deps.add(b.w)
        for b in writes:
            if b.w is not None:
                deps.add(b.w)
            deps.update(b.rs.values())
            deps.update(b.rd)
        deps.discard(ev)
        ev.deps = tuple(deps)
        for b in reads:
            if ev.is_dma:
                b.rd.append(ev)
            else:
                b.rs[ev.eng] = ev
        for b in writes:
            b.w = ev
            b.rs = {}
            b.rd = []
        self.q[ev.eng].append(ev)
        if not ev.is_dma:
            self.last_ev[ev.eng] = ev

    def op(self, eng, fn, reads=(), writes=()):
        ev = Ev(eng, fn)
        self._record(ev, reads, writes)
        return ev

    def dma(self, eng, fn, reads=(), writes=()):
        ev = Ev(eng, fn, is_dma=True)
        k = self.dma_rr[eng]
        self.dma_rr[eng] = (k + 1) % N_DMA_SEMS
        key = (eng, k)
        ev.semkey = key
        self.dma_cnt[key] = self.dma_cnt.get(key, 0) + 16
        ev.semval = self.dma_cnt[key]
        ev.prev_dma = self.dma_last.get(key)
        self.dma_last[key] = ev
        self._record(ev, reads, writes)
        return ev

    def barrier(self):
        evs = [e for e in self.last_ev.values() if e is not None]
        evs += list(self.dma_last.values())
        for eng in ENGS:
            ev = Ev(eng, None)
            ev.deps = tuple(evs)
            self.q[eng].append(ev)
        for b in self.all_bufs:
            b.w = None
            b.rs = {}
            b.rd = []

    def emit(self):
        nc = self.nc
        for e in ENGS:
            for ev in self.q[e]:
                for d in ev.deps:
                    if d.is_dma or d.fn is None:
                        continue
                    if d.eng == "pe" and ev.eng == "pe" and not ev.is_dma and ev.fn is not None:
                        continue
                    d.need_inc = True
        tail = []
        for e in ENGS:
            evs = [x for x in self.q[e] if not x.is_dma and x.fn is not None]
            if evs:
                evs[-1].need_inc = True
                tail.append(evs[-1])
        tail += list(self.dma_last.values())
        counts = {}
        for e in ENGS:
            c = 0
            for ev in self.q[e]:
                if ev.is_dma or ev.fn is None:
                    continue
                if ev.need_inc:
                    c += 1
                    ev.semkey = e
                    ev.semval = c
            counts[e] = (c, len(self.q[e]))
        self.counts = counts
        with ExitStack() as es:
            sems = {}
            for e in ENGS:
                sems[e] = es.enter_context(nc.semaphore(f"s_{e}"))
            for key in sorted(self.dma_cnt.keys()):
                sems[key] = es.enter_context(nc.semaphore(f"d_{key[0]}{key[1]}"))
            block = es.enter_context(nc.Block())
            engmap = {"pe": "tensor", "act": "scalar", "dve": "vector", "pool": "gpsimd", "sp": "sync"}

            def run_queue(e, engine):
                seen = {}

                def wait_for(d):
                    if d.semkey is None or d.semval == 0:
                        return
                    if seen.get(d.semkey, 0) >= d.semval:
                        return
                    engine.wait_ge(sems[d.semkey], d.semval)
                    seen[d.semkey] = d.semval

                for ev in self.q[e]:
                    for d in ev.deps:
                        if (not d.is_dma) and d.eng == "pe" and e == "pe" and not ev.is_dma and ev.fn is not None:
                            continue
                        wait_for(d)
                    if ev.is_dma and ev.prev_dma is not None:
                        wait_for(ev.prev_dma)
                    if ev.fn is None:
                        continue
                    ins = ev.fn(engine)
                    if ev.is_dma:
                        ins.then_inc(sems[ev.semkey], 16)
                    elif ev.need_inc:
                        ins.then_inc(sems[ev.semkey], 1)
                if e == "sp":
                    for d in tail:
                        wait_for(d)

            for e in ENGS:
                dec = getattr(block, engmap[e])

                def mk(e=e):
                    def f(engine):
                        run_queue(e, engine)
                    return f
                dec(mk())


def mm(P, out, lhsT, rhs, start, stop, reads, writes, tp=None):
    if tp is None:
        return P.op("pe", lambda e: e.matmul(out, lhsT=lhsT, rhs=rhs, start=start, stop=stop), reads, writes)
    return P.op("pe", lambda e: e.matmul(out, lhsT=lhsT, rhs=rhs, start=start, stop=stop, tile_position=tp),
                reads, writes)


def tr(P, out, in_, ident, reads, writes):
    return P.op("pe", lambda e: e.transpose(out, in_, ident), reads, writes)


def act(P, out, in_, func, reads, writes, bias=None, scale=None):
    kw = {}
    if bias is not None:
        kw["bias"] = bias
    if scale is not None:
        kw["scale"] = scale
    return P.op("act", lambda e: e.activation(out=out, in_=in_, func=func, **kw), reads, writes)


def tt(P, eng, out, in0, in1, op, reads, writes):
    return P.op(eng, lambda e: e.tensor_tensor(out=out, in0=in0, in1=in1, op=op), reads, writes)


def ts(P, eng, out, in0, s1, s2, op0, op1, reads, writes):
    if op1 is None:
        return P.op(eng, lambda e: e.tensor_scalar(out=out, in0=in0, scalar1=s1, scalar2=None, op0=op0),
                    reads, writes)
    return P.op(eng, lambda e: e.tensor_scalar(out=out, in0=in0, scalar1=s1, scalar2=s2, op0=op0, op1=op1),
                reads, writes)


def stt(P, out, in0, scalar, in1, op0, op1, reads, writes):
    return P.op("dve", lambda e: e.scalar_tensor_tensor(out=out, in0=in0, scalar=scalar, in1=in1, op0=op0, op1=op1),
                reads, writes)


def cp(P, eng, out, in_, reads, writes):
    if eng == "act":
        return P.op("act", lambda e: e.copy(out=out, in_=in_), reads, writes)
    return P.op(eng, lambda e: e.tensor_copy(out=out, in_=in_), reads, writes)


def mset(P, eng, ap, val, writes):
    return P.op(eng, lambda e: e.memset(ap, val), (), writes)


def scan(P, out, d0, d1, reads, writes):
    return P.op("dve", lambda e: e.tensor_tensor_scan(out=out, data0=d0, data1=d1, initial=0.0,
                                                      op0=ALU.mult, op1=ALU.add), reads, writes)


def dma(P, q, out, in_, reads, writes):
    return P.dma(q, lambda e: e.dma_start(out=out, in_=in_), reads, writes)


class Arena:
    def __init__(self, t, nwords):
        self.t = t
        self.n = nwords
        self.off = 0

    def reset(self):
        self.off = 0

    def alloc(self, nelem, dtype):
        words = nelem if dtype == F32 else (nelem + 1) // 2
        a = self.t[:, self.off:self.off + words]
        self.off += words
        assert self.off <= self.n, f"arena overflow {self.off} > {self.n}"
        return a if dtype == F32 else a.bitcast(BF16)


LQ = "sp"
SQ = "pool"


def build_program(seq_lens, debug=False):
    nc = bass.Bass("TRN2", target_bir_lowering=False)
    SMAX = max(seq_lens)
    NSEQ = len(seq_lens)
    dk = "ExternalOutput" if debug else "Internal"

    def din(name, shape, dt=F32):
        return nc.dram_tensor(name, list(shape), dt, kind="ExternalInput").ap()

    def dscr(name, shape, dt, kind=None):
        return nc.dram_tensor(name, list(shape), dt, kind=kind or "Internal").ap()

    xs = [din(f"x{i}", [S, D]) for i, S in enumerate(seq_lens)]
    ps_in = [din(f"p{i}", [DEPTH, S, PLE]) for i, S in enumerate(seq_lens)]
    ys = [nc.dram_tensor(f"y{i}", [S, D], F32, kind="ExternalOutput").ap() for i, S in enumerate(seq_lens)]
    W = {}
    for name, shape in [("norm_g", [DEPTH, D]), ("w_in", [DEPTH, D, DIN]), ("w_out", [DEPTH, D, D]),
                        ("hg_lb", [DEPTH, 2, 256]), ("hg_norm", [DEPTH, 64]), ("lru_conv_w", [DEPTH, 4, 512]),
                        ("lru_conv_b", [DEPTH, 512]), ("lru_wa", [DEPTH, 2, 8, 64, 64]), ("lru_ba", [DEPTH, 2, 512]),
                        ("lru_wx", [DEPTH, 2, 8, 64, 64]), ("lru_bx", [DEPTH, 2, 512]), ("lru_lam", [DEPTH, 2, 512]),
                        ("da_lq1", [DEPTH, 32]), ("da_lk1", [DEPTH, 32]), ("da_lq2", [DEPTH, 32]),
                        ("da_lk2", [DEPTH, 32]), ("da_norm", [DEPTH, 64]), ("ple_w", [DEPTH, PLE, D]),
                        ("ple_gate_w", [DEPTH, D, D]), ("final_norm", [D])]:
        W[name] = din(name, shape)
    c_ident = din("c_ident", [128, 128])
    c_cos = din("c_cos", [128, SMAX])
    c_sin = din("c_sin", [128, SMAX])
    c_maskf = din("c_maskf", [128, 128])
    c_maskb = din("c_maskb", [128, 128])
    c_bones = din("c_bones", [128, 128])

    WIN_d = dscr("WIN_d", [DEPTH, 128, 8, NFM * 128], BF16)
    WV_d = dscr("WV_d", [DEPTH, 128, 8, 512], BF16)
    WOUT_d = dscr("WOUT_d", [DEPTH, 128, 8, D], BF16)
    WG_d = dscr("WG_d", [DEPTH, 128, 8, D], BF16)
    WPLE_d = dscr("WPLE_d", [DEPTH, 128, 2, D], BF16)
    hT_d = dscr("hT_d", [8, 128, SMAX], F32, dk)
    qh_d = dscr("qh_d", [256, SMAX], BF16, dk)
    sgf_d = dscr("sgf_d", [256, SMAX], F32, dk)
    sgb_d = dscr("sgb_d", [256, SMAX], F32, dk)
    gh_d = dscr("gh_d", [256, SMAX], BF16, dk)
    lx_d = dscr("lx_d", [512, SMAX], F32, dk)
    gl_d = dscr("gl_d", [512, SMAX], BF16, dk)
    dq_d = dscr("dq_d", [256, SMAX], BF16, dk)
    dk_d = dscr("dk_d", [256, SMAX], BF16, dk)
    gd_d = dscr("gd_d", [256, SMAX], BF16, dk)
    vh_d = dscr("vh_d", [SMAX, 256], BF16, dk)
    vd_d = dscr("vd_d", [SMAX, 256], BF16, dk)
    oT_d = dscr("oT_d", [D, SMAX], BF16, dk)

    P = Prog(nc)
    AR_WORDS = 46 * 1024
    CA_WORDS = 4608
    with ExitStack() as es:
        arena_t = es.enter_context(nc.sbuf_tensor("arena", [128, AR_WORDS], F32))
        cst_t = es.enter_context(nc.sbuf_tensor("cst", [128, CA_WORDS], F32))
        PS = [es.enter_context(nc.psum_tensor(f"ps{i}", [128, 512], F32)) for i in range(8)]
        bPS = P.bufs(8, "ps")
        AR = Arena(arena_t, AR_WORDS)
        CA = Arena(cst_t, CA_WORDS)
        bC = P.buf("consts")
        IDENT = CA.alloc(128, F32)
        IDENTB = CA.alloc(128, BF16)
        ONESB = CA.alloc(128, BF16)
        BONES = CA.alloc(128, BF16)
        MASKF = CA.alloc(128, BF16)
        MASKB = CA.alloc(128, BF16)
        NEGHALF = CA.alloc(512, F32)
        HALF = CA.alloc(512, F32)
        CTMP = CA.alloc(128, F32)
        EPSB = CA.alloc(1, F32)
        HM = CA.alloc(4, F32)

        dma(P, LQ, IDENT, c_ident, [], [bC])
        cp(P, "dve", IDENTB, IDENT, [bC], [bC])
        mset(P, "dve", ONESB, 1.0, [bC])
        mset(P, "dve", NEGHALF, -0.5, [bC])
        mset(P, "dve", HALF, 0.5, [bC])
        mset(P, "dve", EPSB, EPS, [bC])
        mset(P, "dve", HM, 0.0, [bC])
        for j in range(4):
            mset(P, "dve", HM[32 * j:32 * j + 32, j:j + 1], 1.0, [bC])
        for src, dst in ((c_bones, BONES), (c_maskf, MASKF), (c_maskb, MASKB)):
            dma(P, LQ, CTMP, src, [bC], [bC])
            cp(P, "dve", dst, CTMP, [bC], [bC])

        SB = []
        bCw = P.buf("cw")

        def nb_():
            b = P.buf("sm")
            SB.append(b)
            return [b]

        def load_cols(dst, src1d):
            C = dst.shape[1]
            for c in range(C):
                dma(P, LQ, dst[:, c:c + 1], src1d[c * 128:(c + 1) * 128].rearrange("(p o) -> p o", o=1), [], nb_())

        PR = []
        fg = CA.alloc(8, F32)
        load_cols(fg, W["final_norm"])
        AR.reset()
        bdfs = [AR.alloc(16 * 128, F32).rearrange("p (g d c e) -> p g d c e", g=2, d=2, c=4) for _ in range(DEPTH)]
        lqks = [AR.alloc(128, F32).rearrange("p (a d) -> p a d", a=4) for _ in range(DEPTH)]
        bBDFm = P.bufs(DEPTH, "bdfm")
        lbr = AR.alloc(8, F32).rearrange("p (l d c) -> p l d c", l=DEPTH, d=2)
        for ll in range(DEPTH):
            for d in range(2):
                load_cols(lbr[:, ll, d, :], W["hg_lb"][ll, d])
        for l in range(DEPTH):
            pr = {}
            bdf, lqk = bdfs[l], lqks[l]
            pr["gcol"] = CA.alloc(8, F32)
            load_cols(pr["gcol"], W["norm_g"][l])
            pr["gneg"] = CA.alloc(8, F32)
            ts(P, "dve", pr["gneg"], pr["gcol"], -1.0, None, ALU.mult, None, SB + [bCw], [bCw])
            cw = CA.alloc(16, F32).rearrange("p (c j) -> p c j", c=4)
            for j in range(4):
                for c in range(4):
                    dma(P, LQ, cw[:, c, j:j + 1],
                        W["lru_conv_w"][l, j, c * 128:(c + 1) * 128].rearrange("(p o) -> p o", o=1), [], nb_())
            pr["cw"] = cw
            pr["cb"] = CA.alloc(4, F32)
            load_cols(pr["cb"], W["lru_conv_b"][l])
            for nm, key in (("lru_ba", "bab"), ("lru_bx", "bxb"), ("lru_lam", "coef")):
                t = CA.alloc(8, F32).rearrange("p (d c) -> p d c", d=2)
                for d in range(2):
                    load_cols(t[:, d, :], W[nm][l, d])
                pr[key] = t
            cf = pr["coef"].rearrange("p d c -> p (d c)")
            act(P, cf, cf, AF.Exp, SB + [bCw], [bCw], scale=-1.0)
            act(P, cf, cf, AF.Ln, SB + [bCw], [bCw], bias=1.0)
            ts(P, "dve", cf, cf, -8.0, None, ALU.mult, None, SB + [bCw], [bCw])
            for key, src in (("coef2", "coef"), ("nbab", "bab"), ("nbxb", "bxb")):
                t = CA.alloc(8, F32).rearrange("p (d c) -> p d c", d=2)
                ts(P, "dve", t.rearrange("p d c -> p (d c)"), pr[src].rearrange("p d c -> p (d c)"),
                   2.0 if key == "coef2" else -1.0, None, ALU.mult, None, SB + [bCw], [bCw])
                pr[key] = t
            lbd = CA.alloc(4, F32).rearrange("p (d c) -> p d c", d=2)
            c1 = CA.alloc(4, F32).rearrange("p (d c) -> p d c", d=2)
            c1n = CA.alloc(4, F32).rearrange("p (d c) -> p d c", d=2)
            if l == 0:
                mset(P, "dve", lbd, 0.0, [bCw])
            else:
                tt(P, "dve", lbd, lbr[:, 0], lbr[:, 1], ALU.subtract, SB + [bCw], [bCw])
                act(P, lbd, lbd, AF.Exp, SB + [bCw], [bCw])
                ts(P, "dve", lbd, lbd, 1.0, None, ALU.add, None, SB + [bCw], [bCw])
                P.op("dve", (lambda a: (lambda e: e.reciprocal(out=a, in_=a)))(lbd), SB + [bCw], [bCw])
            ts(P, "dve", c1, lbd, -1.0, 1.0, ALU.mult, ALU.add, SB + [bCw], [bCw])
            ts(P, "dve", c1n, c1, -1.0, None, ALU.mult, None, SB + [bCw], [bCw])
            pr["lbd"], pr["c1"], pr["c1n"] = lbd, c1, c1n
            gn = CA.alloc(1, F32)
            dma(P, LQ, gn[0:64, :], W["hg_norm"][l].rearrange("(p o) -> p o", o=1), [], nb_())
            dma(P, LQ, gn[64:128, :], W["hg_norm"][l].rearrange("(p o) -> p o", o=1), [], nb_())
            pr["gn"] = gn
            lam_init = 0.8 - 0.6 * float(np.exp(-0.3 * l))
            lsm = CA.alloc(4, F32)
            for a, nm in enumerate(("da_lq1", "da_lk1", "da_lq2", "da_lk2")):
                dma(P, LQ, lqk[:, a, :], W[nm][l:l + 1, :].partition_broadcast(128), [], nb_())
            tt(P, "dve", lqk[:, 0, :], lqk[:, 0, :], lqk[:, 1, :], ALU.mult, SB + [bCw], [bCw])
            tt(P, "dve", lqk[:, 2, :], lqk[:, 2, :], lqk[:, 3, :], ALU.mult, SB + [bCw], [bCw])
            P.op("dve", (lambda o, i: (lambda e: e.reduce_sum(out=o, in_=i, axis=mybir.AxisListType.X)))(
                lsm[:, 0:1], lqk[:, 0, :]), SB + [bCw], [bCw])
            P.op("dve", (lambda o, i: (lambda e: e.reduce_sum(out=o, in_=i, axis=mybir.AxisListType.X)))(
                lsm[:, 1:2], lqk[:, 2, :]), SB + [bCw], [bCw])
            act(P, lsm[:, 0:2], lsm[:, 0:2], AF.Exp, SB + [bCw], [bCw])
            tt(P, "dve", lsm[:, 2:3], lsm[:, 1:2], lsm[:, 0:1], ALU.subtract, SB + [bCw], [bCw])
            ts(P, "dve", lsm[:, 3:4], lsm[:, 2:3], -lam_init, None, ALU.add, None, SB + [bCw], [bCw])
            pr["neglam"] = lsm[0:64, 3:4]
            dn = CA.alloc(1, F32)
            dma(P, LQ, dn[0:64, :], W["da_norm"][l].rearrange("(p o) -> p o", o=1), [], nb_())
            ts(P, "dve", dn[0:64, :], dn[0:64, :], 1.0 - lam_init, None, ALU.mult, None, SB + [bCw], [bCw])
            pr["dn"] = dn
            bdb = CA.alloc(16 * 128, BF16).rearrange("p (g d c e) -> p g d c e", g=2, d=2, c=4)
            mset(P, "pool", bdf, 0.0, [bBDFm[l]])
            for g, wname in enumerate(("lru_wa", "lru_wx")):
                for d in range(2):
                    for b in range(2):
                        src = W[wname][l, d].rearrange("(c b) ci e -> b ci c e", b=2)[b]
                        dma(P, LQ, bdf[64 * b:64 * b + 64, g, d, :, 64 * b:64 * b + 64], src, [bBDFm[l]], nb_())
            cp(P, "pool", bdb, bdf, SB + [bCw], [bCw])
            pr["bdb"] = bdb
            PR.append(pr)
        P.barrier()

        def prep_weights():
            AR.reset()
            stg = [AR.alloc(DIN, F32) for _ in range(2)]
            bstg = P.bufs(2, "wstg")
            ob = [AR.alloc(NFM * 128 + 512, BF16) for _ in range(2)]
            bob = P.bufs(2, "wob")
            bg = bC
            it = 0
            for l in range(DEPTH):
                for kc in range(8):
                    s = stg[it % 2]
                    o = ob[it % 2]
                    bs, bo = bstg[it % 2], bob[it % 2]
                    eng = "dve"
                    it += 1
                    dma(P, LQ, s, W["w_in"][l, kc * 128:(kc + 1) * 128, :], [], [bs])
                    g = PR[l]["gcol"][:, kc:kc + 1]
                    for (s0, s1, d0) in ((0, 768, 0), (1024, 1280, 768), (1280, 2304, 1024), (2304, 2816, 2048),
                                         (3072, 3328, 2560)):
                        ts(P, eng, o[:, d0:d0 + (s1 - s0)], s[:, s0:s1], g, None, ALU.mult, None, [bs, bg], [bo])
                    mset(P, eng, o[:, 2816:3328], 0.0, [bo])
                    sv = s[:, 2304:2816].rearrange("p (h d) -> p h d", d=32)
                    dv = o[:, 2816:3328].rearrange("p (h d) -> p h d", d=32)
                    ts(P, eng, dv[:, :, 0:4], sv[:, :, 4:8], PR[l]["gneg"][:, kc:kc + 1], None, ALU.mult, None, [bs, bg], [bo])
                    ts(P, eng, dv[:, :, 4:8], sv[:, :, 0:4], g, None, ALU.mult, None, [bs, bg], [bo])
                    ts(P, eng, o[:, 3328:3584], s[:, 768:1024], g, None, ALU.mult, None, [bs, bg], [bo])
                    ts(P, eng, o[:, 3584:3840], s[:, 2816:3072], g, None, ALU.mult, None, [bs, bg], [bo])
                    dma(P, SQ, WIN_d[l, :, kc, :], o[:, 0:3328], [bo], [])
                    dma(P, SQ, WV_d[l, :, kc, :], o[:, 3328:3840], [bo], [])
            for l in range(DEPTH):
                for (src, dst, nk) in ((W["w_out"], WOUT_d, 8), (W["ple_gate_w"], WG_d, 8), (W["ple_w"], WPLE_d, 2)):
                    for kc in range(nk):
                        s = stg[it % 2]
                        o = ob[it % 2]
                        bs, bo = bstg[it % 2], bob[it % 2]
                        eng = "dve" if it % 2 == 0 else "act"
                        it += 1
                        dma(P, LQ, s[:, 0:D], src[l, kc * 128:(kc + 1) * 128, :], [], [bs])
                        cp(P, eng, o[:, 0:D], s[:, 0:D], [bs], [bo])
                        dma(P, SQ, dst[l, :, kc, :], o[:, 0:D], [bo], [])
            P.barrier()

        def rstd_from_ss(ss_ps, n, npart, rst, brst, bss, inv_n):
            act(P, rst, ss_ps, AF.Ln, [bss], [brst], bias=EPSB[0:npart, :], scale=inv_n)
            act(P, rst, rst, AF.Exp, [brst], [brst], scale=-0.5)

        def phase0(si):
            S = seq_lens[si]
            AR.reset()
            xt = [AR.alloc(4 * D, F32) for _ in range(2)]
            bx = P.bufs(2, "xt")
            ht = [AR.alloc(8 * 512, F32) for _ in range(2)]
            bh = P.bufs(2, "ht")
            for ti in range(S // 512):
                X = xt[ti % 2].rearrange("p (s f) -> p s f", s=4)
                H = ht[ti % 2].rearrange("p (c t) -> p c t", c=8)
                dma(P, LQ, X, xs[si][ti * 512:(ti + 1) * 512, :].rearrange("(s p) f -> p s f", p=128), [],
                    [bx[ti % 2]])
                for c in range(8):
                    bank = c
                    for s in range(4):
                        tr(P, PS[bank][:, s * 128:(s + 1) * 128], X[:, s, c * 128:(c + 1) * 128], IDENT,
                           [bx[ti % 2], bC], [bPS[bank]])
                    cp(P, "dve" if c % 2 == 0 else "act", H[:, c, :], PS[bank][:, :], [bPS[bank]], [bh[ti % 2]])
                dma(P, SQ, hT_d[:, :, ti * 512:(ti + 1) * 512].rearrange("c p t -> p c t"), H, [bh[ti % 2]], [])
            P.barrier()

        def phase1(si, l):
            S = seq_lens[si]
            AR.reset()
            WIN = AR.alloc(8 * NFM * 128, BF16).rearrange("p (k f) -> p k f", k=8)
            WV = AR.alloc(8 * 512, BF16).rearrange("p (k f) -> p k f", k=8)
            bWg = P.bufs(NFM, "win")
            bW = P.buf("wv")

            def load_w1():
                for (f0, f1) in ((0, 2), (2, 6), (6, 12), (12, 18), (18, 26)):
                    dma(P, LQ, WIN[:, :, f0 * 128:f1 * 128], WIN_d[l, :, :, f0 * 128:f1 * 128], [], bWg[f0:f1])
                dma(P, LQ, WV, WV_d[l], [], [bW])
            ht = [AR.alloc(8 * 512, F32) for _ in range(2)]
            bh = P.bufs(2, "ht")
            sq = AR.alloc(8 * 512, BF16)
            bsq = P.buf("sq")
            hn = [AR.alloc(8 * 512, BF16) for _ in range(2)]
            bhn = P.bufs(2, "hn")
            rst = AR.alloc(512, F32)
            brst = P.buf("rst")
            cs = [AR.alloc(512, F32) for _ in range(2)]
            sn = [AR.alloc(512, F32) for _ in range(2)]
            bcs = P.bufs(2, "cs")
            NST = 6
            st = [AR.alloc(512, F32) for _ in range(NST)]
            bst = P.bufs(NST, "st")
            r1 = AR.alloc(512, F32)
            r2 = AR.alloc(512, F32)
            br = P.buf("r")
            sti = [0]
            bank_i = [0]

            def next_bank():
                b = 1 + bank_i[0] % 7
                bank_i[0] += 1
                return b

            def next_st():
                k = sti[0] % NST
                sti[0] += 1
                return st[k], bst[k]

            NTI = S // 512

            def pre(ti):
                t0 = ti * 512
                H = ht[ti % 2].rearrange("p (c t) -> p c t", c=8)
                HN = hn[ti % 2].rearrange("p (c t) -> p c t", c=8)
                SQv = sq.rearrange("p (c t) -> p c t", c=8)
                dma(P, LQ, H, hT_d[:, :, t0:t0 + 512].rearrange("c p t -> p c t"), [], [bh[ti % 2]])
                dma(P, LQ, cs[ti % 2], c_cos[:, t0:t0 + 512], [], [bcs[ti % 2]])
                dma(P, LQ, sn[ti % 2], c_sin[:, t0:t0 + 512], [], [bcs[ti % 2]])
                tt(P, "pool", SQv, H, H, ALU.mult, [bh[ti % 2]], [bsq])

            def pre_b(ti):
                H = ht[ti % 2].rearrange("p (c t) -> p c t", c=8)
                HN = hn[ti % 2].rearrange("p (c t) -> p c t", c=8)
                SQv = sq.rearrange("p (c t) -> p c t", c=8)
                for c in range(8):
                    mm(P, PS[0][:, :], ONESB, SQv[:, c, :], c == 0, c == 7, [bC, bsq], [bPS[0]])
                rstd_from_ss(PS[0][:, :], 512, 128, rst, brst, bPS[0], 1.0 / D)
                rb = bass.AP(rst.tensor, rst.offset, [list(rst.ap[0]), [0, 8], [1, 512]])
                tt(P, "dve", HN, H, rb, ALU.mult, [bh[ti % 2], brst], [bhn[ti % 2]])

            pre(0)
            pre_b(0)
            load_w1()
            for ti in range(NTI):
                t0 = ti * 512
                HN = hn[ti % 2].rearrange("p (c t) -> p c t", c=8)
                for fc in list(range(0, 22)):
                    if fc == 2 and ti + 1 < NTI:
                        pre(ti + 1)
                    if fc == 9 and ti + 1 < NTI:
                        pre_b(ti + 1)
                    b = next_bank()
                    for kc in range(8):
                        mm(P, PS[b][:, :], WIN[:, kc, fc * 128:(fc + 1) * 128], HN[:, kc, :], kc == 0, kc == 7,
                           [bWg[fc], bhn[ti % 2]], [bPS[b]])
                    if fc in (0, 1):
                        o, bo = next_st()
                        ob = o.bitcast(BF16)[:, 0:512]
                        act(P, ob, PS[b][:, :], AF.Silu, [bPS[b]], [bo])
                        dma(P, SQ, qh_d[fc * 128:(fc + 1) * 128, t0:t0 + 512], ob, [bo], [])
                    elif fc in (2, 3, 4, 5):
                        o, bo = next_st()
                        act(P, o, PS[b][:, :], AF.Sigmoid, [bPS[b]], [bo])
                        dst = sgf_d if fc < 4 else sgb_d
                        r0 = (fc % 2) * 128
                        dma(P, SQ, dst[r0:r0 + 128, t0:t0 + 512], o, [bo], [])
                    elif fc in (6, 7) or 12 <= fc <= 15 or fc in (20, 21):
                        o, bo = next_st()
                        ob = o.bitcast(BF16)[:, 0:512]
                        act(P, ob, PS[b][:, :], AF.Silu, [bPS[b]], [bo])
                        if fc in (6, 7):
                            dst, r0 = gh_d, (fc - 6) * 128
                        elif fc in (20, 21):
                            dst, r0 = gd_d, (fc - 20) * 128
                        else:
                            dst, r0 = gl_d, (fc - 12) * 128
                        dma(P, SQ, dst[r0:r0 + 128, t0:t0 + 512], ob, [bo], [])
                    elif 8 <= fc <= 11:
                        o, bo = next_st()
                        cp(P, "dve", o, PS[b][:, :], [bPS[b]], [bo])
                        dma(P, SQ, lx_d[(fc - 8) * 128:(fc - 7) * 128, t0:t0 + 512], o, [bo], [])
                    else:
                        b2 = next_bank()
                        for kc in range(8):
                            mm(P, PS[b2][:, :], WIN[:, kc, (fc + 6) * 128:(fc + 7) * 128], HN[:, kc, :], kc == 0,
                               kc == 7, [bWg[fc + 6], bhn[ti % 2]], [bPS[b2]])
                        o, bo = next_st()
                        ob = o.bitcast(BF16)[:, 0:512]
                        tt(P, "dve", r1, PS[b][:, :], cs[ti % 2], ALU.mult, [bPS[b], bcs[ti % 2]], [br])
                        tt(P, "dve", r2, PS[b2][:, :], sn[ti % 2], ALU.mult, [bPS[b2], bcs[ti % 2]], [br])
                        tt(P, "pool", ob, r1, r2, ALU.add, [br], [bo])
                        dst = dq_d if fc < 18 else dk_d
                        r0 = (fc % 2) * 128
                        dma(P, SQ, dst[r0:r0 + 128, t0:t0 + 512], ob, [bo], [])
                for s in range(4):
                    b = next_bank()
                    for kc in range(8):
                        mm(P, PS[b][:, :], HN[:, kc, s * 128:(s + 1) * 128], WV[:, kc, :], kc == 0, kc == 7,
                           [bW, bhn[ti % 2]], [bPS[b]])
                    o, bo = next_st()
                    ob = o.bitcast(BF16)[:, 0:512]
                    cp(P, "act", ob, PS[b][:, :], [bPS[b]], [bo])
                    dma(P, SQ, vh_d[t0 + s * 128:t0 + (s + 1) * 128, :], ob[:, 0:256], [bo], [])
                    dma(P, SQ, vd_d[t0 + s * 128:t0 + (s + 1) * 128, :], ob[:, 256:512], [bo], [])
            P.barrier()

        def phase_lru(si, l):
            S = seq_lens[si]
            TT = min(1024, S // 4)
            NTT = S // TT
            GW = min(512, TT)
            AR.reset()
            pr = PR[l]
            cw, cb, bab, bxb, coef, bdb = pr["cw"], pr["cb"], pr["bab"], pr["bxb"], pr["coef"], pr["bdb"]
            LX = [AR.alloc(S + 4, F32) for _ in range(2)]
            GL = [AR.alloc(S, BF16) for _ in range(2)]
            OB = [AR.alloc(S, BF16) for _ in range(2)]
            bLX, bGL, bOB = P.bufs(2, "lx"), P.bufs(2, "gl"), P.bufs(2, "ob")
            U32 = AR.alloc(S, F32)
            UB = AR.alloc(S, BF16)
            HS = AR.alloc(S, F32)
            bU, bUB, bHS = P.bufs(NTT, "u32"), P.bufs(NTT, "ub"), P.bufs(NTT, "hs")
            A_ = [AR.alloc(TT, F32) for _ in range(3)]
            I_ = [AR.alloc(TT, F32) for _ in range(3)]
            T_ = [AR.alloc(TT, F32) for _ in range(2)]
            H2 = [AR.alloc(TT, F32) for _ in range(2)]
            bA, bI, bT, bH2 = P.bufs(3, "la"), P.bufs(3, "li"), P.bufs(2, "lt"), P.bufs(2, "lh")
            bank_i = [0]

            def nbank():
                b = bank_i[0] % 8
                bank_i[0] += 1
                return b

            descs = []
            for c in range(4):
                passes = [(0, True), (1, False)] if c % 2 == 0 else [(1, False), (0, True)]
                for pi, (d, asc) in enumerate(passes):
                    tiles = list(range(NTT)) if asc else list(range(NTT - 1, -1, -1))
                    for idx, tt_ in enumerate(tiles):
                        descs.append(dict(c=c, pi=pi, d=d, tt=tt_, first=(idx == 0), lastt=(idx == NTT - 1),
                                          k3=len(descs) % 3, k=len(descs) % 2, banks=[]))
            prevd = [None]

            def stage_a1(ds):
                c, pi, d, tt_ = ds["c"], ds["pi"], ds["d"], ds["tt"]
                lx, blx = LX[c % 2], bLX[c % 2]
                gl, bgl = GL[c % 2], bGL[c % 2]
                a0 = tt_ * TT
                u32 = U32[:, a0:a0 + TT]
                ub = UB[:, a0:a0 + TT]
                if ds["first"] and pi == 0:
                    mset(P, "pool", lx[:, 0:2], 0.0, [blx])
                    mset(P, "pool", lx[:, S + 2:S + 4], 0.0, [blx])
                    dma(P, LQ, lx[:, 2:S + 2], lx_d[c * 128:(c + 1) * 128, 0:S], [], [blx])
                    dma(P, LQ, gl, gl_d[c * 128:(c + 1) * 128, 0:S], [], [bgl])
                if pi == 0:
                    ts(P, "dve", u32, lx[:, a0:a0 + TT], cw[:, c, 0:1], cb[:, c:c + 1], ALU.mult, ALU.add,
                       [blx, bC], [bU[tt_]])
                    for j in range(1, 4):
                        stt(P, u32, lx[:, a0 + j:a0 + j + TT], cw[:, c, j:j + 1], u32, ALU.mult, ALU.add,
                            [blx, bC, bU[tt_]], [bU[tt_]])
                    cp(P, "act", ub, u32, [bU[tt_]], [bUB[tt_]])
                for t0 in range(0, TT, GW):
                    b1, b2 = nbank(), nbank()
                    ds["banks"].append((t0, b1, b2))
                    mm(P, PS[b1][:, 0:GW], bdb[:, 0, d, c, :], ub[:, t0:t0 + GW], True, True, [bC, bUB[tt_]],
                       [bPS[b1]])
                    mm(P, PS[b2][:, 0:GW], bdb[:, 1, d, c, :], ub[:, t0:t0 + GW], True, True, [bC, bUB[tt_]],
                       [bPS[b2]])

            def stage_a2(ds):
                c, d, k3 = ds["c"], ds["d"], ds["k3"]
                for (t0, b1, b2) in ds["banks"]:
                    act(P, A_[k3][:, t0:t0 + GW], PS[b1][:, 0:GW], AF.Sigmoid, [bPS[b1], bC], [bA[k3]],
                        bias=bab[:, d, c:c + 1])
                    act(P, I_[k3][:, t0:t0 + GW], PS[b2][:, 0:GW], AF.Sigmoid, [bPS[b2], bC], [bI[k3]],
                        bias=bxb[:, d, c:c + 1])

            def stage_b_act(ds):
                c, d, k3, k = ds["c"], ds["d"], ds["k3"], ds["k"]
                act(P, A_[k3], A_[k3], AF.Exp, [bA[k3], bC], [bA[k3]], scale=coef[:, d, c:c + 1])
                act(P, T_[k], A_[k3], AF.Square, [bA[k3]], [bT[k]])
                act(P, T_[k], T_[k], AF.Sqrt, [bT[k]], [bT[k]], scale=-1.0, bias=1.0)

            def stage_b_rest(ds):
                c, pi, d, tt_, k3, k = ds["c"], ds["pi"], ds["d"], ds["tt"], ds["k3"], ds["k"]
                gl, bgl = GL[c % 2], bGL[c % 2]
                ob, bob = OB[c % 2], bOB[c % 2]
                a0 = tt_ * TT
                u32 = U32[:, a0:a0 + TT]
                tt(P, "dve", I_[k3], I_[k3], u32, ALU.mult, [bI[k3], bU[tt_]], [bI[k3]])
                tt(P, "dve", I_[k3], I_[k3], T_[k], ALU.mult, [bI[k3], bT[k]], [bI[k3]])
                if pi == 0:
                    dest, bdest = HS[:, a0:a0 + TT], bHS[tt_]
                else:
                    dest, bdest = H2[k], bH2[k]
                rd = [bA[k3], bI[k3]]
                if ds["first"]:
                    init = 0.0
                else:
                    pdest, pb = prevd[0]
                    init = pdest[:, TT - 1:TT] if d == 0 else pdest[:, 0:1]
                    rd = rd + [pb]
                if d == 0:
                    P.op("dve", (lambda o_, a_, b_, i_: (lambda e: e.tensor_tensor_scan(
                        out=o_, data0=a_, data1=b_, initial=i_, op0=ALU.mult, op1=ALU.add)))(
                        dest, A_[k3], I_[k3], init), rd, [bdest])
                else:
                    P.op("dve", (lambda o_, a_, b_, i_: (lambda e: e.tensor_tensor_scan(
                        out=o_, data0=a_, data1=b_, initial=i_, op0=ALU.mult, op1=ALU.add)))(
                        dest[:, ::-1], A_[k3][:, ::-1], I_[k3][:, ::-1], init), rd, [bdest])
                prevd[0] = (dest, bdest)
                if pi == 1:
                    tt(P, "pool", T_[k], H2[k], HS[:, a0:a0 + TT], ALU.add, [bH2[k], bHS[tt_]], [bT[k]])
                    tt(P, "dve", ob[:, a0:a0 + TT], T_[k], gl[:, a0:a0 + TT], ALU.mult, [bT[k], bgl], [bob])
                    if ds["lastt"]:
                        dma(P, SQ, oT_d[256 + c * 128:256 + (c + 1) * 128, 0:S], ob, [bob], [])

            nd = len(descs)
            for j in range(min(2, nd)):
                stage_a1(descs[j])
                stage_a2(descs[j])
            for i in range(nd):
                if i + 2 < nd:
                    stage_a1(descs[i + 2])
                stage_b_act(descs[i])
                if i + 2 < nd:
                    stage_a2(descs[i + 2])
                stage_b_rest(descs[i])
            P.barrier()

        def lru_generator(si, l):
            S = seq_lens[si]
            TT = 512
            NTT = S // TT
            pr = PR[l]
            cw, cb, coef, coef2, bdb = pr["cw"], pr["cb"], pr["coef"], pr["coef2"], pr["bdb"]
            nbab, nbxb = pr["nbab"], pr["nbxb"]
            LX = AR.alloc(S + 4, F32)
            GL = AR.alloc(S, BF16)
            OB = AR.alloc(S, BF16)
            U32 = AR.alloc(S, F32)
            UB = AR.alloc(S, BF16)
            HS = AR.alloc(S, F32)
            bLX, bGL, bOB = P.buf("lx"), P.buf("gl"), P.buf("ob")
            bU, bUB, bHS = P.bufs(NTT, "u32"), P.bufs(NTT, "ub"), P.bufs(NTT, "hs")
            NS = 2
            ER = [AR.alloc(TT, F32) for _ in range(NS)]
            EI = [AR.alloc(TT, F32) for _ in range(NS)]
            A_ = [AR.alloc(TT, F32) for _ in range(NS)]
            T_ = [AR.alloc(TT, F32) for _ in range(NS)]
            H2 = [AR.alloc(TT, F32) for _ in range(NS)]
            bER, bEI, bA, bT, bH2 = (P.bufs(NS, "ler"), P.bufs(NS, "lei"), P.bufs(NS, "la"), P.bufs(NS, "lt"),
                                     P.bufs(NS, "lh"))
            descs = []
            for c in range(4):
                passes = [(0, True), (1, False)] if c % 2 == 0 else [(1, False), (0, True)]
                for pi, (d, asc) in enumerate(passes):
                    tiles = list(range(NTT)) if asc else list(range(NTT - 1, -1, -1))
                    for idx, tt_ in enumerate(tiles):
                        descs.append(dict(c=c, pi=pi, d=d, tt=tt_, first=(idx == 0), lastt=(idx == NTT - 1),
                                          k=len(descs) % NS))

            def load_lx(c):
                mset(P, "pool", LX[:, 0:2], 0.0, [bLX])
                mset(P, "pool", LX[:, S + 2:S + 4], 0.0, [bLX])
                dma(P, LQ, LX[:, 2:S + 2], lx_d[c * 128:(c + 1) * 128, 0:S], [], [bLX])

            def stage_a(ds):
                c, pi, d, tt_, k = ds["c"], ds["pi"], ds["d"], ds["tt"], ds["k"]
                a0 = tt_ * TT
                u32 = U32[:, a0:a0 + TT]
                ub = UB[:, a0:a0 + TT]
                if ds["first"] and pi == 0:
                    if c == 0:
                        load_lx(0)
                if ds["first"] and pi == 1 and c + 1 < 4:
                    load_lx(c + 1)
                if pi == 0:
                    ts(P, "dve", u32, LX[:, a0:a0 + TT], cw[:, c, 0:1], cb[:, c:c + 1], ALU.mult, ALU.add,
                       [bLX, bC], [bU[tt_]])
                    for j in range(1, 4):
                        stt(P, u32, LX[:, a0 + j:a0 + j + TT], cw[:, c, j:j + 1], u32, ALU.mult, ALU.add,
                            [bLX, bC, bU[tt_]], [bU[tt_]])
                    yield
                    cp(P, "act", ub, u32, [bU[tt_]], [bUB[tt_]])
                    yield
                mm(P, PS[6][:, :], bdb[:, 0, d, c, :], ub, True, True, [bC, bUB[tt_]], [bPS[6]])
                mm(P, PS[7][:, :], bdb[:, 1, d, c, :], ub, True, True, [bC, bUB[tt_]], [bPS[7]])
                yield
                act(P, ER[k], PS[6][:, :], AF.Exp, [bPS[6], bC], [bER[k]], bias=nbab[:, d, c:c + 1], scale=-1.0)
                act(P, EI[k], PS[7][:, :], AF.Exp, [bPS[7], bC], [bEI[k]], bias=nbxb[:, d, c:c + 1], scale=-1.0)
                yield
                act(P, ER[k], ER[k], AF.Ln, [bER[k]], [bER[k]], bias=1.0)
                act(P, EI[k], EI[k], AF.Ln, [bEI[k]], [bEI[k]], bias=1.0)
                yield
                act(P, ER[k], ER[k], AF.Exp, [bER[k]], [bER[k]], scale=-1.0)
                act(P, EI[k], EI[k], AF.Exp, [bEI[k]], [bEI[k]], scale=-1.0)
                yield

            prevd = [None]

            def stage_b(ds):
                c, pi, d, tt_, k = ds["c"], ds["pi"], ds["d"], ds["tt"], ds["k"]
                a0 = tt_ * TT
                u32 = U32[:, a0:a0 + TT]
                if pi == 0 and ds["lastt"]:
                    dma(P, LQ, GL, gl_d[c * 128:(c + 1) * 128, 0:S], [], [bGL])
                act(P, A_[k], ER[k], AF.Exp, [bER[k], bC], [bA[k]], scale=coef[:, d, c:c + 1])
                act(P, T_[k], ER[k], AF.Exp, [bER[k], bC], [bT[k]], scale=coef2[:, d, c:c + 1])
                yield
                act(P, T_[k], T_[k], AF.Ln, [bT[k]], [bT[k]], scale=-1.0, bias=1.0)
                act(P, T_[k], T_[k], AF.Exp, [bT[k]], [bT[k]], scale=0.5)
                tt(P, "dve", EI[k], EI[k], u32, ALU.mult, [bEI[k], bU[tt_]], [bEI[k]])
                yield
                tt(P, "dve", EI[k], EI[k], T_[k], ALU.mult, [bEI[k], bT[k]], [bEI[k]])
                if pi == 0:
                    dest, bdest = HS[:, a0:a0 + TT], bHS[tt_]
                else:
                    dest, bdest = H2[k], bH2[k]
                rd = [bA[k], bEI[k]]
                if ds["first"]:
                    init = 0.0
                else:
                    pdest, pb = prevd[0]
                    init = pdest[:, TT - 1:TT] if d == 0 else pdest[:, 0:1]
                    rd = rd + [pb]
                if d == 0:
                    P.op("dve", (lambda o_, a_, b_, i_: (lambda e: e.tensor_tensor_scan(
                        out=o_, data0=a_, data1=b_, initial=i_, op0=ALU.mult, op1=ALU.add)))(
                        dest, A_[k], EI[k], init), rd, [bdest])
                else:
                    P.op("dve", (lambda o_, a_, b_, i_: (lambda e: e.tensor_tensor_scan(
                        out=o_, data0=a_, data1=b_, initial=i_, op0=ALU.mult, op1=ALU.add)))(
                        dest[:, ::-1], A_[k][:, ::-1], EI[k][:, ::-1], init), rd, [bdest])
                prevd[0] = (dest, bdest)
                yield
                if pi == 1:
                    tt(P, "pool", T_[k], H2[k], HS[:, a0:a0 + TT], ALU.add, [bH2[k], bHS[tt_]], [bT[k]])
                    tt(P, "dve", OB[:, a0:a0 + TT], T_[k], GL[:, a0:a0 + TT], ALU.mult, [bT[k], bGL], [bOB])
                    if ds["lastt"]:
                        dma(P, SQ, oT_d[256 + c * 128:256 + (c + 1) * 128, 0:S], OB, [bOB], [])
                    yield

            nd = len(descs)
            for x in stage_a(descs[0]):
                yield
            for i in range(nd):
                if i + 1 < nd:
                    for x in stage_a(descs[i + 1]):
                        yield
                for x in stage_b(descs[i]):
                    yield

        def phase_da(si, l):
            S = seq_lens[si]
            NB = S // 128
            lam_init = 0.8 - 0.6 * float(np.exp(-0.3 * l))
            AR.reset()
            pr = PR[l]
            neglam, dn = pr["neglam"], pr["dn"]
            bsm = bC
            QT = AR.alloc(2 * S, BF16).rearrange("p (c t) -> p c t", c=2)
            KT = AR.alloc(2 * S, BF16).rearrange("p (c t) -> p c t", c=2)
            VA = AR.alloc(NB * 512, BF16).rearrange("p (n h f) -> p n h f", n=NB, h=4)
            bQ, bK, bV, bVA = P.bufs(4, "da")
            mset(P, "dve", VA[:, :, :, 64:128], 1.0, [bVA])
            for h_ in range(4):
                dma(P, LQ, VA[:, :, h_, 0:64],
                    vd_d[0:S, 64 * h_:64 * h_ + 64].rearrange("(n p) f -> p n f", p=128), [], [bVA])
            dma(P, LQ, QT, dq_d[:, 0:S].rearrange("(c p) t -> p c t", p=128), [], [bQ])
            dma(P, LQ, KT, dk_d[:, 0:S].rearrange("(c p) t -> p c t", p=128), [], [bK])
            lru_gen = lru_generator(si, l) if FUSE_LRU else None
            NPT = 4
            PT = [AR.alloc(512, BF16) for _ in range(NPT)]
            bPT = P.bufs(NPT, "pt")
            RZ = AR.alloc(512, F32)
            OS = AR.alloc(512, F32)
            ON = [AR.alloc(512, F32) for _ in range(2)]
            DD = AR.alloc(512, F32)
            D2 = AR.alloc(512, BF16)
            RS = AR.alloc(512, F32)
            GD = [AR.alloc(512, BF16) for _ in range(2)]
            OO = [AR.alloc(512, BF16) for _ in range(2)]
            bRZ, bOS, bDD, bD2, bRS = P.bufs(5, "dae")
            bON = P.bufs(2, "on")
            bGD = P.bufs(2, "gd")
            bOO = P.bufs(2, "oo")
            scale = 32.0 ** -0.5
            QM = [AR.alloc(512, BF16) for _ in range(3)]
            bQM = P.bufs(3, "qm")
            NQC = S // 512
            LOOK = 2
            steps = []
            for h in range(4):
                for qc in range(NQC):
                    for s_ in range(2):
                        for kb in range(NB):
                            steps.append((h, qc, s_, kb))
            n = len(steps)
            deferred = []
            gidx = {}

            def epilogue_a(h, qc, s_):
                acc = 3 + s_
                P.op("dve", (lambda a: (lambda e: e.reciprocal(out=RZ[64:128, :], in_=PS[a][64:128, :])))(acc),
                     [bPS[acc]], [bRZ])
                cp(P, "dve", OS[0:64, :], PS[acc][0:64, :], [bPS[acc]], [bOS])

            def epilogue_b1(h, qc, s_):
                mm(P, PS[5][0:64, :], IDENT[64:128, 64:128], RZ[64:128, :], True, True, [bC, bRZ], [bPS[5]])
                tt(P, "dve", ON[s_][0:64, :], OS[0:64, :], PS[5][0:64, :], ALU.mult, [bOS, bPS[5]], [bON[s_]])

            def epilogue_b2(h, qc, s_):
                stt(P, DD[0:64, :], ON[1][0:64, :], neglam, ON[0][0:64, :], ALU.mult, ALU.add,
                    [bON[0], bON[1], bsm], [bDD])
                tt(P, "dve", D2[0:64, :], DD[0:64, :], DD[0:64, :], ALU.mult, [bDD], [bD2])

            def epilogue_b3(h, qc, s_):
                mm(P, PS[5][0:64, :], ONESB[0:64, 0:64], D2[0:64, :], True, True, [bC, bD2], [bPS[5]])

            def epilogue_b4(h, qc, s_):
                q0 = qc * 512
                k = gidx[(h, qc)]
                gdt, bgd = GD[k % 2], bGD[k % 2]
                oot, boo = OO[k % 2], bOO[k % 2]
                rstd_from_ss(PS[5][0:64, :], 512, 64, RS[0:64, :], bRS, bPS[5], 1.0 / 64)
                stt(P, DD[0:64, :], DD[0:64, :], dn[0:64, :], RS[0:64, :], ALU.mult, ALU.mult, [bDD, bsm, bRS],
                    [bDD])
                tt(P, "dve", oot[0:64, :], DD[0:64, :], gdt[0:64, :], ALU.mult, [bDD, bgd], [boo])
                dma(P, SQ, oT_d[768 + 64 * h:768 + 64 * h + 64, q0:q0 + 512], oot[0:64, :], [boo], [])

            groups = []
            for h in range(4):
                for qc in range(NQC):
                    for s_ in range(2):
                        groups.append((h, qc, s_))
            qm_of = {}

            def make_qm(g):
                if g >= len(groups):
                    return
                h, qc, s_ = groups[g]
                hd = 2 * h + s_
                qm, bqm = QM[g % 3], bQM[g % 3]
                qm_of[(h, qc, s_)] = (qm, bqm)
                ts(P, "dve", qm, QT[:, hd // 4, qc * 512:(qc + 1) * 512], HM[:, hd % 4:hd % 4 + 1], None, ALU.mult,
                   None, [bQ, bC], [bqm])

            d1 = max(1, min(8, NB // 4))
            if lru_gen is not None:
                n_y = 4 * 2 * (S // 512) * 10 + 16
                stride = max(1, int(0.92 * n) // n_y)
            make_qm(0)
            for i in range(n + LOOK):
                if lru_gen is not None and i % stride == 0:
                    next(lru_gen, None)
                if i < n:
                    h, qc, s_, kb = steps[i]
                    hd = 2 * h + s_
                    ch = hd // 4
                    q0 = qc * 512
                    if kb == 0:
                        if s_ == 0:
                            k = len(gidx)
                            gidx[(h, qc)] = k
                            dma(P, LQ, GD[k % 2][0:64, :], gd_d[64 * h:64 * h + 64, q0:q0 + 512], [], [bGD[k % 2]])
                        make_qm(i // NB + 1)
                    qm, bqm = qm_of[(h, qc, s_)]
                    sb = i % 3
                    mm(P, PS[sb][:, :], KT[:, ch, kb * 128:(kb + 1) * 128], qm, True, True, [bK, bqm], [bPS[sb]])
                j = i - LOOK
                if j >= 0:
                    h, qc, s_, kb = steps[j]
                    sb = j % 3
                    acc = 3 + s_
                    pt, bpt = PT[j % NPT], bPT[j % NPT]
                    act(P, pt, PS[sb][:, :], AF.Exp, [bPS[sb]], [bpt], scale=scale)
                    mm(P, PS[acc][:, :], VA[:, kb, h, :], pt, kb == 0, kb == NB - 1, [bVA, bpt], [bPS[acc]])
                    deferred.sort(key=lambda t: t[0])
                    while deferred and deferred[0][0] <= j:
                        _, fn_, args = deferred.pop(0)
                        fn_(*args)
                    if kb == NB - 1:
                        epilogue_a(h, qc, s_)
                        deferred.append((j + d1, epilogue_b1, (h, qc, s_)))
                        if s_ == 1:
                            deferred.append((j + 2 * d1, epilogue_b2, (h, qc, s_)))
                            deferred.append((j + 3 * d1, epilogue_b3, (h, qc, s_)))
                            deferred.append((j + 4 * d1, epilogue_b4, (h, qc, s_)))
            deferred.sort(key=lambda t: t[0])
            while deferred:
                _, fn_, args = deferred.pop(0)
                fn_(*args)
            if lru_gen is not None:
                for _ in lru_gen:
                    pass
            P.barrier()

        def phase_hg(si, l):
            S = seq_lens[si]
            NT = S // 128
            N = S // 32
            AR.reset()
            pr = PR[l]
            lbd, c1, c1n, gn = pr["lbd"], pr["c1"], pr["c1n"], pr["gn"]
            bsm = bC
            RM = AR.alloc(S + 32, BF16)
            bRM = P.buf("rm")
            mset(P, "dve", RM, 1.0, [bRM])
            mset(P, "dve", RM.rearrange("p (n j) -> p n j", j=32)[:, :, 0:1], 0.0, [bRM])
            RMf = RM[:, 0:S]
            RMb = RM[:, 1:S + 1]
            VH = AR.alloc(NT * 256, BF16).rearrange("p (n f) -> p n f", n=NT)
            bVH = P.buf("vh")
            dma(P, LQ, VH, vh_d[0:S, :].rearrange("(n p) f -> p n f", p=128), [], [bVH])
            B1 = AR.alloc(S, F32)
            B2 = AR.alloc(S, F32)
            B3 = AR.alloc(S, F32)
            B4 = AR.alloc(S, F32)
            B5 = AR.alloc(S, F32)
            TQ = min(1024, S)
            NTQ = S // TQ
            bB1, bB2, bB3, bB4, bB5 = [P.bufs(NTQ, f"hgB{k}") for k in range(5)]
            QH = AR.alloc(S, BF16)
            QTt = AR.alloc(S, BF16)
            KTt = AR.alloc(S, BF16)
            KH = QH
            KHT = B3.bitcast(BF16)[:, 0:S].rearrange("p (n f) -> p n f", n=NT)
            bQH, bQT, bKT = [P.bufs(NTQ, f"hgq{k}") for k in range(3)]
            bKH = bQH
            bKHT = bB3
            DEC = AR.alloc(N, F32)
            bDEC = P.buf("dec")
            b4b = B4.bitcast(BF16)
            assert (N + 1) * 64 <= 4 * S
            SS = bass.AP(b4b.tensor, b4b.offset, [list(b4b.ap[0]), [64, N + 1], [1, 64]])
            bSSl = bB4 + bB5
            OACC = AR.alloc(S, F32)
            bOA = P.buf("oacc")
            NG = NT // 4
            AT = [[AR.alloc(512, BF16) for _ in range(2)] for _ in range(NG)]
            bAT = [P.bufs(2, "at") for _ in range(NG)]
            OTMP = [AR.alloc(512, F32) for _ in range(2)]
            bOTMP = P.bufs(2, "otmp")
            GH = AR.alloc(512, BF16)
            SQo = AR.alloc(512, BF16)
            RSo = AR.alloc(512, F32)
            OOo = AR.alloc(512, BF16)
            bGH, bSQo, bRSo, bOOo = P.bufs(4, "hgo")
            U32 = bass.AP(B1.tensor, B1.offset, [list(B1.ap[0]), [64, N], [1, 64]]) if N * 64 <= 2 * S else None
            assert U32 is not None
            bU = bB1 + bB2
            for hp in range(2):
                for d in range(2):
                    sg_d = sgf_d if d == 0 else sgb_d
                    RMd = RMf if d == 0 else RMb
                    mask = MASKF if d == 0 else MASKB
                    NCQ = TQ // 32
                    last = 31 if d == 0 else 0
                    sls = [slice(q * TQ, (q + 1) * TQ) for q in range(NTQ)]
                    QS = list(enumerate(sls))
                    for q, sl in QS:
                        dma(P, LQ, B1[:, sl], sg_d[hp * 128:(hp + 1) * 128, sl], [], [bB1[q]])
                        dma(P, LQ, QH[:, sl], qh_d[hp * 128:(hp + 1) * 128, sl], [], [bQH[q]])
                    for q, sl in QS:
                        ts(P, "dve", B2[:, sl], B1[:, sl], c1[:, d, hp:hp + 1], lbd[:, d, hp:hp + 1], ALU.mult,
                           ALU.add, [bB1[q], bsm], [bB2[q]])
                    for q, sl in QS:
                        act(P, B2[:, sl], B2[:, sl], AF.Ln, [bB2[q]], [bB2[q]])
                    for q, sl in QS:
                        act(P, B1[:, sl], B1[:, sl], AF.Identity, [bB1[q], bsm], [bB1[q]], bias=c1[:, d, hp:hp + 1],
                            scale=c1n[:, d, hp:hp + 1])
                    for q, sl in QS:
                        if d == 0:
                            scan(P, B3[:, sl], RMd[:, sl], B2[:, sl], [bRM, bB2[q]], [bB3[q]])
                        else:
                            scan(P, B3[:, sl][:, ::-1], RMd[:, sl][:, ::-1], B2[:, sl][:, ::-1], [bRM, bB2[q]],
                                 [bB3[q]])
                    for q, sl in QS:
                        act(P, B4[:, sl], B3[:, sl], AF.Exp, [bB3[q]], [bB4[q]])
                    for q, sl in QS:
                        act(P, B5[:, sl], B3[:, sl], AF.Exp, [bB3[q]], [bB5[q]], scale=-1.0)
                    for q, sl in QS:
                        tt(P, "dve", QTt[:, sl], QH[:, sl], B4[:, sl], ALU.mult, [bQH[q], bB4[q]], [bQT[q]])
                    for q, sl in QS:
                        tt(P, "dve", KTt[:, sl], B1[:, sl], B5[:, sl], ALU.mult, [bB1[q], bB5[q]], [bKT[q]])
                    for q, sl in QS:
                        clb = bass.AP(B3.tensor, B3.offset + q * TQ + last, [list(B3.ap[0]), [32, NCQ], [0, 32]])
                        tt(P, "dve", B5[:, sl].rearrange("p (n j) -> p n j", j=32), clb,
                           B3[:, sl].rearrange("p (n j) -> p n j", j=32), ALU.subtract, [bB3[q], bB5[q]], [bB5[q]])
                    for q, sl in QS:
                        act(P, B5[:, sl], B5[:, sl], AF.Exp, [bB5[q]], [bB5[q]])
                    for q, sl in QS:
                        tt(P, "dve", KH[:, sl], B1[:, sl], B5[:, sl], ALU.mult, [bB1[q], bB5[q]], [bKH[q]])
                    e4 = bass.AP(B4.tensor, B4.offset + last, [list(B4.ap[0]), [32, N]])
                    cp(P, "dve", DEC, e4, bB4, [bDEC])
                    if HG_STOP <= 1:
                        continue
                    for T0 in range(0, NT, 4):
                        b = (T0 // 4) % 2
                        psb = PS[b].bitcast(BF16)
                        for t4 in range(4):
                            T = T0 + t4
                            tr(P, psb[:, t4 * 128:(t4 + 1) * 128], KH[:, T * 128:(T + 1) * 128], IDENTB,
                               [bKH[(T * 128) // TQ], bC], [bPS[b]])
                        cp(P, "act", KHT[:, T0:T0 + 4, :], psb[:, 0:512].rearrange("p (t f) -> p t f", t=4),
                           [bPS[b]], bKHT)
                    if HG_STOP <= 2:
                        continue
                    TG = 8
                    for T0 in range(0, NT, TG):
                        nt = min(TG, NT - T0)
                        for tt_ in range(nt):
                            T = T0 + tt_
                            for m in range(4):
                                for e_ in range(2):
                                    mm(P, PS[2 + m][64 * e_:64 * e_ + 64, tt_ * 64:(tt_ + 1) * 64],
                                       KHT[32 * m:32 * m + 32, T, 64 * e_:64 * e_ + 64],
                                       VH[32 * m:32 * m + 32, T, 64 * (2 * hp + e_):64 * (2 * hp + e_) + 64],
                                       True, True, bKHT + [bVH], [bPS[2 + m]], tp=(32 * m, 64 * e_))
                        for m in range(4):
                            udst = bass.AP(B1.tensor, B1.offset + (4 * T0 + m) * 64,
                                           [list(B1.ap[0]), [256, nt], [1, 64]])
                            cp(P, "act", udst,
                               PS[2 + m][:, 0:nt * 64].rearrange("p (t v) -> p t v", v=64), [bPS[2 + m]], bU)
                    if HG_STOP <= 3:
                        continue
                    if d == 0:
                        mset(P, "pool", SS[:, 0, :], 0.0, bSSl)
                    else:
                        mset(P, "pool", SS[:, N, :], 0.0, bSSl)
                    def emit_scan(v):
                        u_v = bass.AP(B1.tensor, B1.offset + v, [list(B1.ap[0]), [64, N]])
                        if d == 0:
                            o_v = bass.AP(SS.tensor, SS.offset + 64 + v, [list(SS.ap[0]), [64, N]])
                            scan(P, o_v, DEC, u_v, [bDEC] + bU, bSSl)
                        else:
                            o_v = bass.AP(SS.tensor, SS.offset + v, [list(SS.ap[0]), [64, N]])
                            scan(P, o_v[:, ::-1], DEC[:, ::-1], u_v[:, ::-1], [bDEC] + bU, bSSl)

                    vper = (64 + NG - 1) // NG
                    vnext = 0
                    for G in range(NG):
                        for e_ in range(2):
                            sb = (2 * G + e_) % 4
                            for t4 in range(4):
                                T = G * 4 + t4
                                qi = (T * 128) // TQ
                                mm(P, PS[sb][:, t4 * 128:(t4 + 1) * 128], KTt[64 * e_:64 * e_ + 64, T * 128:(T + 1) * 128],
                                   QTt[64 * e_:64 * e_ + 64, T * 128:(T + 1) * 128], True, True, [bKT[qi], bQT[qi]],
                                   [bPS[sb]])
                        for _ in range(vper):
                            if vnext < 64:
                                emit_scan(vnext)
                                vnext += 1
                        for e_ in range(2):
                            sb = (2 * G + e_) % 4
                            at, bat = AT[G][e_], bAT[G][e_]
                            mb = bass.AP(mask.tensor, mask.offset, [list(mask.ap[0]), [0, 4], [1, 128]])
                            cp(P, "act", at, PS[sb][:, :], [bPS[sb]], [bat])
                            tt(P, "pool", at.rearrange("p (a i) -> p a i", a=4),
                               at.rearrange("p (a i) -> p a i", a=4), mb, ALU.mult, [bat, bC], [bat])
                    while vnext < 64:
                        emit_scan(vnext)
                        vnext += 1
                    for G in range(NG):
                        for e_ in range(2):
                            ob = 6 + e_
                            at, bat = AT[G][e_], bAT[G][e_]
                            for t4 in range(4):
                                T = G * 4 + t4
                                qi = (T * 128) // TQ
                                vcol = 64 * (2 * hp + e_)
                                mm(P, PS[ob][64 * e_:64 * e_ + 64, t4 * 128:(t4 + 1) * 128],
                                   VH[:, T, vcol:vcol + 64], at[:, t4 * 128:(t4 + 1) * 128], True, False,
                                   [bVH, bat], [bPS[ob]], tp=(0, 64 * e_))
                                for m in range(4):
                                    n = 4 * T + m
                                    sidx = n if d == 0 else n + 1
                                    mm(P, PS[ob][64 * e_:64 * e_ + 64, t4 * 128 + 32 * m:t4 * 128 + 32 * m + 32],
                                       SS[64 * e_:64 * e_ + 64, sidx, :],
                                       QTt[64 * e_:64 * e_ + 64, T * 128 + 32 * m:T * 128 + 32 * m + 32],
                                       False, m == 3, bSSl + [bQT[qi]], [bPS[ob]], tp=(64 * e_, 64 * e_))
                        for e_ in range(2):
                            ob = 6 + e_
                            oa = OACC[64 * e_:64 * e_ + 64, G * 512:(G + 1) * 512]
                            if d == 0:
                                cp(P, "act", oa, PS[ob][64 * e_:64 * e_ + 64, :], [bPS[ob]], [bOA])
                            else:
                                otmp, botmp = OTMP[e_], bOTMP[e_]
                                cp(P, "act", otmp[64 * e_:64 * e_ + 64, :], PS[ob][64 * e_:64 * e_ + 64, :], [bPS[ob]],
                                   [botmp])
                                tt(P, "pool", oa, oa, otmp[64 * e_:64 * e_ + 64, :], ALU.add, [bOA, botmp], [bOA])
                if HG_STOP <= 5:
                    continue
                for G in range(S // 512):
                    oa = OACC[:, G * 512:(G + 1) * 512]
                    dma(P, LQ, GH, gh_d[hp * 128:(hp + 1) * 128, G * 512:(G + 1) * 512], [], [bGH])
                    tt(P, "pool", SQo, oa, oa, ALU.mult, [bOA], [bSQo])
                    mm(P, PS[0][:, :], BONES, SQo, True, True, [bC, bSQo], [bPS[0]])
                    rstd_from_ss(PS[0][:, :], 512, 128, RSo, bRSo, bPS[0], 1.0 / 64)
                    stt(P, RSo, oa, gn[:, 0:1], RSo, ALU.mult, ALU.mult, [bOA, bsm, bRSo], [bRSo])
                    tt(P, "dve", OOo, RSo, GH, ALU.mult, [bRSo, bGH], [bOOo])
                    dma(P, SQ, oT_d[hp * 128:(hp + 1) * 128, G * 512:(G + 1) * 512], OOo, [bOOo], [])
            P.barrier()

        def phase3(si, l):
            S = seq_lens[si]
            last = (l == DEPTH - 1)
            NTI = S // 512
            AR.reset()
            WO = AR.alloc(8 * D, BF16).rearrange("p (k f) -> p k f", k=8)
            WGt = AR.alloc(8 * D, BF16).rearrange("p (k f) -> p k f", k=8)
            WP = AR.alloc(2 * D, BF16).rearrange("p (k f) -> p k f", k=2)
            bW = P.buf("w3")
            bWG3 = P.buf("w3g")
            dma(P, LQ, WO, WOUT_d[l], [], [bW])
            NH = 3 if last else 2
            ot = [AR.alloc(8 * 512, BF16) for _ in range(2)]
            ht = [AR.alloc(8 * 512, F32) for _ in range(NH)]
            pt = [AR.alloc(4 * PLE, F32) for _ in range(2)]
            bot = P.bufs(2, "ot")
            bht = [P.bufs(8, f"ht3_{k}") for k in range(NH)]
            bpt = P.bufs(2, "pt3")
            pTs = [AR.alloc(2 * 512, BF16).rearrange("p (k t) -> p k t", k=2) for _ in range(2)]
            bpT = P.bufs(2, "pT")
            hnbs = [AR.alloc(8 * 512, BF16).rearrange("p (c t) -> p c t", c=8) for _ in range(2)]
            bhnb = P.bufs(2, "hnb")
            gt = [AR.alloc(512, F32) for _ in range(2)]
            bgt = P.bufs(2, "gt")
            sq = AR.alloc(8 * 512, BF16).rearrange("p (c t) -> p c t", c=8) if last else None
            bsq = P.buf("sq3")
            rst = AR.alloc(512, F32)
            brst = P.buf("rst3")
            yt = AR.alloc(4 * D, F32).rearrange("p (s f) -> p s f", s=4) if last else None
            byt = P.buf("yt")
            bi = [0]

            def nb():
                b = 1 + bi[0] % 7
                bi[0] += 1
                return b

            def stage_wo(ti):
                t0 = ti * 512
                O = ot[ti % 2].rearrange("p (c t) -> p c t", c=8)
                H = ht[ti % NH].rearrange("p (c t) -> p c t", c=8)
                bH = bht[ti % NH]
                PT_ = pt[ti % 2].rearrange("p (s f) -> p s f", s=4)
                pT = pTs[ti % 2]
                hnb = hnbs[ti % 2]
                dma(P, LQ, O, oT_d[:, t0:t0 + 512].rearrange("(c p) t -> p c t", p=128), [], [bot[ti % 2]])
                dma(P, LQ, H, hT_d[:, :, t0:t0 + 512].rearrange("c p t -> p c t"), [], bH)
                dma(P, LQ, PT_, ps_in[si][l, t0:t0 + 512, :].rearrange("(s p) f -> p s f", p=128), [],
                    [bpt[ti % 2]])
                for pc in range(2):
                    b = nb()
                    for s_ in range(4):
                        tr(P, PS[b][:, s_ * 128:(s_ + 1) * 128], PT_[:, s_, pc * 128:(pc + 1) * 128], IDENT,
                           [bpt[ti % 2], bC], [bPS[b]])
                    cp(P, "act", pT[:, pc, :], PS[b][:, :], [bPS[b]], [bpT[ti % 2]])
                for c in range(8):
                    b = nb()
                    for kc in range(8):
                        mm(P, PS[b][:, :], WO[:, kc, c * 128:(c + 1) * 128], O[:, kc, :], kc == 0, kc == 7,
                           [bW, bot[ti % 2]], [bPS[b]])
                    tt(P, "dve", H[:, c, :], H[:, c, :], PS[b][:, :], ALU.add, [bH[c], bPS[b]], [bH[c]])
                    cp(P, "act", hnb[:, c, :], H[:, c, :], [bH[c]], [bhnb[ti % 2]])

            def stage_gate(ti):
                t0 = ti * 512
                H = ht[ti % NH].rearrange("p (c t) -> p c t", c=8)
                bH = bht[ti % NH]
                pT = pTs[ti % 2]
                hnb = hnbs[ti % 2]
                for c in range(8):
                    b = nb()
                    for kc in range(8):
                        mm(P, PS[b][:, :], WGt[:, kc, c * 128:(c + 1) * 128], hnb[:, kc, :], kc == 0, kc == 7,
                           [bWG3, bhnb[ti % 2]], [bPS[b]])
                    g, bg = gt[c % 2], bgt[c % 2]
                    act(P, g, PS[b][:, :], AF.Sigmoid, [bPS[b]], [bg])
                    b2 = nb()
                    for pc in range(2):
                        mm(P, PS[b2][:, :], WP[:, pc, c * 128:(c + 1) * 128], pT[:, pc, :], pc == 0, pc == 1,
                           [bWG3, bpT[ti % 2]], [bPS[b2]])
                    tt(P, "dve", g, g, PS[b2][:, :], ALU.mult, [bg, bPS[b2]], [bg])
                    tt(P, "dve", H[:, c, :], H[:, c, :], g, ALU.add, [bH[c], bg], [bH[c]])
                if not last:
                    dma(P, SQ, hT_d[:, :, t0:t0 + 512].rearrange("c p t -> p c t"), H, bH, [])

            def stage_fina(ti):
                H = ht[ti % NH].rearrange("p (c t) -> p c t", c=8)
                bH = bht[ti % NH]
                tt(P, "pool", sq, H, H, ALU.mult, bH, [bsq])
                for c in range(8):
                    mm(P, PS[0][:, :], ONESB, sq[:, c, :], c == 0, c == 7, [bC, bsq], [bPS[0]])
                rstd_from_ss(PS[0][:, :], 512, 128, rst, brst, bPS[0], 1.0 / D)
                for c in range(8):
                    stt(P, H[:, c, :], H[:, c, :], fg[:, c:c + 1], rst, ALU.mult, ALU.mult,
                        [bH[c], bC, brst], [bH[c]])

            def stage_finb(ti):
                t0 = ti * 512
                H = ht[ti % NH].rearrange("p (c t) -> p c t", c=8)
                bH = bht[ti % NH]
                for s_ in range(4):
                    for half in range(2):
                        b = nb()
                        for cc in range(4):
                            c = half * 4 + cc
                            tr(P, PS[b][:, cc * 128:(cc + 1) * 128], H[:, c, s_ * 128:(s_ + 1) * 128], IDENT,
                               [bH[c], bC], [bPS[b]])
                        cp(P, "act" if half == 0 else "dve", yt[:, s_, half * 512:(half + 1) * 512], PS[b][:, :],
                           [bPS[b]], [byt])
                dma(P, SQ, ys[si][t0:t0 + 512, :].rearrange("(s p) f -> p s f", p=128), yt, [byt], [])

            for k in range(NTI + 2):
                if last and 0 <= k - 2 < NTI:
                    stage_fina(k - 2)
                if k < NTI:
                    stage_wo(k)
                if k == 0:
                    dma(P, LQ, WGt, WG_d[l], [], [bWG3])
                    dma(P, LQ, WP, WPLE_d[l], [], [bWG3])
                if last and 0 <= k - 2 < NTI:
                    stage_finb(k - 2)
                if 0 <= k - 1 < NTI:
                    stage_gate(k - 1)
            P.barrier()

        ph = PHASES
        if "w" in ph:
            prep_weights()
        for si in range(NSEQ):
            if "0" in ph:
                phase0(si)
            for l in range(DEPTH if "L2" in ph else 1):
                if "1" in ph:
                    phase1(si, l)
                if "h" in ph:
                    phase_hg(si, l)
                if "l" in ph and not FUSE_LRU:
                    phase_lru(si, l)
                if "d" in ph:
                    phase_da(si, l)
                if "3" in ph:
                    phase3(si, l)
        P.emit()
    return nc, P


def _consts(smax):
    ident = np.eye(128, dtype=np.float32)
    half = 4
    inv = (500000.0 ** (-np.arange(half, dtype=np.float32) * 2.0 / 8)).astype(np.float32)
    pos = np.arange(smax, dtype=np.float32)
    ang = pos[None, :] * inv[:, None]
    cos = np.ones((128, smax), np.float32)
    sin = np.zeros((128, smax), np.float32)
    for p in range(128):
        d = p % 32
        if d < 8:
            cos[p] = np.cos(ang[d % 4])
            sin[p] = np.sin(ang[d % 4])
    j = np.arange(128)[:, None]
    i = np.arange(128)[None, :]
    same = (j // 32) == (i // 32)
    maskf = (same & (j <= i)).astype(np.float32)
    maskb = (same & (j >= i)).astype(np.float32)
    bones = ((j // 64) == (i // 64)).astype(np.float32)
    return {"c_ident": ident, "c_cos": cos, "c_sin": sin, "c_maskf": maskf, "c_maskb": maskb, "c_bones": bones}


_WNAMES = ("norm_g", "w_in", "w_out", "hg_lb", "hg_norm", "lru_conv_w", "lru_conv_b", "lru_wa", "lru_ba", "lru_wx",
           "lru_bx", "lru_lam", "da_lq1", "da_lk1", "da_lq2", "da_lk2", "da_norm", "ple_w", "ple_gate_w",
           "final_norm")


def run_cores(seq_lens, per_core_x, per_core_p, weights, debug=False):
    nc, P = build_program(seq_lens, debug=debug)
    consts = _consts(max(seq_lens))
    in_maps = []
    for c in range(len(per_core_x)):
        m = {}
        for i in range(len(seq_lens)):
            m[f"x{i}"] = np.ascontiguousarray(per_core_x[c][i], dtype=np.float32)
            m[f"p{i}"] = np.ascontiguousarray(per_core_p[c][i], dtype=np.float32)
        for k in _WNAMES:
            m[k] = np.ascontiguousarray(weights[k], dtype=np.float32)
        m.update(consts)
        in_maps.append(m)
    res = run_bass_kernel_spmd(nc, in_maps, core_ids=list(range(len(per_core_x))))
    return res.results


def kernel(**inputs):
    n = 8
    xp = np.asarray(inputs["x_prompt"])
    xsm = np.asarray(inputs["x_sample"])
    pp = np.asarray(inputs["p_prompt"])
    psm = np.asarray(inputs["p_sample"])
    B, S1, _ = xp.shape
    B2, S2, _ = xsm.shape
    bp, bs = B // n, B2 // n
    seq_lens = [S1] * bp + [S2] * bs
    pcx, pcp = [], []
    for c in range(n):
        x_list = [xp[c * bp + i] for i in range(bp)] + [xsm[c * bs + i] for i in range(bs)]
        p_list = [pp[:, c * bp + i] for i in range(bp)] + [psm[:, c * bs + i] for i in range(bs)]
        pcx.append(x_list)
        pcp.append(p_list)
    weights = {k: np.asarray(inputs[k]) for k in _WNAMES}
    results = run_cores(seq_lens, pcx, pcp, weights)
    yp = np.empty((B, S1, D), np.float32)
    ysm = np.empty((B2, S2, D), np.float32)
    for c in range(n):
        r = results[c]
        for i in range(bp):
            yp[c * bp + i] = r[f"y{i}"]
        for i in range(bs):
            ysm[c * bs + i] = r[f"y{bp + i}"]
    return (yp, ysm)
```

```python
import numpy as np
import ml_dtypes
from contextlib import ExitStack
import concourse.bass as bass
import concourse.mybir as mybir
from concourse.bass_utils import run_bass_kernel_spmd

F32 = mybir.dt.float32
BF16 = mybir.dt.bfloat16
AF = mybir.ActivationFunctionType
ALU = mybir.AluOpType

D = 1024
DIN = 3328
PLE = 256
DEPTH = 2
NFM = 26
EPS = 1e-6
ENGS = ("pe", "act", "dve", "pool", "sp")
N_DMA_SEMS = 12
PHASES = ("w", "0", "1", "h", "l", "d", "3", "L2")
FUSE_LRU = False
HG_STOP = 99


class Buf:
    __slots__ = ("name", "w", "rs", "rd")

    def __init__(self, name=""):
        self.name = name
        self.w = None
        self.rs = {}
        self.rd = []


class Ev:
    __slots__ = ("eng", "fn", "deps", "need_inc", "semkey", "semval", "is_dma", "prev_dma")

    def __init__(self, eng, fn, is_dma=False):
        self.eng = eng
        self.fn = fn
        self.deps = ()
        self.need_inc = False
        self.semkey = None
        self.semval = 0
        self.is_dma = is_dma
        self.prev_dma = None


class Prog:
    def __init__(self, nc):
        self.nc = nc
        self.q = {e: [] for e in ENGS}
        self.dma_rr = {e: 0 for e in ENGS}
        self.dma_cnt = {}
        self.dma_last = {}
        self.all_bufs = []
        self.last_ev = {e: None for e in ENGS}

    def buf(self, name=""):
        b = Buf(name)
        self.all_bufs.append(b)
        return b

    def bufs(self, n, name=""):
        return [self.buf(f"{name}{i}") for i in range(n)]

    def _record(self, ev, reads, writes):
        deps = set()
        for b in reads:
            if b.w is not None:
                deps.add(b.w)
        for b in writes:
            if b.w is not None:
                deps.add(b.w)
            deps.update(b.rs.values())
            deps.update(b.rd)
        deps.discard(ev)
        ev.deps = tuple(deps)
        for b in reads:
            if ev.is_dma:
                b.rd.append(ev)
            else:
                b.rs[ev.eng] = ev
        for b in writes:
            b.w = ev
            b.rs = {}
            b.rd = []
        self.q[ev.eng].append(ev)
        if not ev.is_dma:
            self.last_ev[ev.eng] = ev

    def op(self, eng, fn, reads=(), writes=()):
        ev = Ev(eng, fn)
        self._record(ev, reads, writes)
        return ev

    def dma(self, eng, fn, reads=(), writes=()):
        ev = Ev(eng, fn, is_dma=True)
        k = self.dma_rr[eng]
        self.dma_rr[eng] = (k + 1) % N_DMA_SEMS
        key = (eng, k)
        ev.semkey = key
        self.dma_cnt[key] = self.dma_cnt.get(key, 0) + 16
        ev.semval = self.dma_cnt[key]
        ev.prev_dma = self.dma_last.get(key)
        self.dma_last[key] = ev
        self._record(ev, reads, writes)
        return ev

    def barrier(self):
        evs = [e for e in self.last_ev.values() if e is not None]
        evs += list(self.dma_last.values())
        for eng in ENGS:
            ev = Ev(eng, None)
            ev.deps = tuple(evs)
            self.q[eng].append(ev)
        for b in self.all_bufs:
            b.w = None
            b.rs = {}
            b.rd = []

    def emit(self):
        nc = self.nc
        for e in ENGS:
            for ev in self.q[e]:
                for d in ev.deps:
                    if d.is_dma or d.fn is None:
                        continue
                    if d.eng == "pe" and ev.eng == "pe" and not ev.is_dma and ev.fn is not None:
                        continue
                    d.need_inc = True
        tail = []
        for e in ENGS:
            evs = [x for x in self.q[e] if not x.is_dma and x.fn is not None]
            if evs:
                evs[-1].need_inc = True
                tail.append(evs[-1])
        tail += list(self.dma_last.values())
        counts = {}
        for e in ENGS:
            c = 0
            for ev in self.q[e]:
                if ev.is_dma or ev.fn is None:
                    continue
                if ev.need_inc:
                    c += 1
                    ev.semkey = e
                    ev.semval = c
            counts[e] = (c, len(self.q[e]))
        self.counts = counts
        with ExitStack() as es:
            sems = {}
            for e in ENGS:
                sems[e] = es.enter_context(nc.semaphore(f"s_{e}"))
            for key in sorted(self.dma_cnt.keys()):
                sems[key] = es.enter_context(nc.semaphore(f"d_{key[0]}{key[1]}"))
            block = es.enter_context(nc.Block())
            engmap = {"pe": "tensor", "act": "scalar", "dve": "vector", "pool": "gpsimd", "sp": "sync"}

            def run_queue(e, engine):
                seen = {}

                def wait_for(d):
                    if d.semkey is None or d.semval == 0:
                        return
                    if seen.get(d.semkey, 0) >= d.semval:
                        return
                    engine.wait_ge(sems[d.semkey], d.semval)
                    seen[d.semkey] = d.semval

                for ev in self.q[e]:
                    for d in ev.deps:
                        if (not d.is_dma) and d.eng == "pe" and e == "pe" and not ev.is_dma and ev.fn is not None:
                            continue
                        wait_for(d)
                    if ev.is_dma and ev.prev_dma is not None:
                        wait_for(ev.prev_dma)
                    if ev.fn is None:
                        continue
                    ins = ev.fn(engine)
                    if ev.is_dma:
                        ins.then_inc(sems[ev.semkey], 16)
                    elif ev.need_inc:
                        ins.then_inc(sems[ev.semkey], 1)
                if e == "sp":
                    for d in tail:
                        wait_for(d)

            for e in ENGS:
                dec = getattr(block, engmap[e])

                def mk(e=e):
                    def f(engine):
                        run_queue(e, engine)
                    return f
                dec(mk())


def mm(P, out, lhsT, rhs, start, stop, reads, writes, tp=None):
    if tp is None:
        return P.op("pe", lambda e: e.matmul(out, lhsT=lhsT, rhs=rhs, start=start, stop=stop), reads, writes)
    return P.op("pe", lambda e: e.matmul(out, lhsT=lhsT, rhs=rhs, start=start, stop=stop, tile_position=tp),
                reads, writes)


def tr(P, out, in_, ident, reads, writes):
    return P.op("pe", lambda e: e.transpose(out, in_, ident), reads, writes)


def act(P, out, in_, func, reads, writes, bias=None, scale=None):
    kw = {}
    if bias is not None:
        kw["bias"] = bias
    if scale is not None:
        kw["scale"] = scale
    return P.op("act", lambda e: e.activation(out=out, in_=in_, func=func, **kw), reads, writes)


def tt(P, eng, out, in0, in1, op, reads, writes):
    return P.op(eng, lambda e: e.tensor_tensor(out=out, in0=in0, in1=in1, op=op), reads, writes)


def ts(P, eng, out, in0, s1, s2, op0, op1, reads, writes):
    if op1 is None:
        return P.op(eng, lambda e: e.tensor_scalar(out=out, in0=in0, scalar1=s1, scalar2=None, op0=op0),
                    reads, writes)
    return P.op(eng, lambda e: e.tensor_scalar(out=out, in0=in0, scalar1=s1, scalar2=s2, op0=op0, op1=op1),
                reads, writes)


def stt(P, out, in0, scalar, in1, op0, op1, reads, writes):
    return P.op("dve", lambda e: e.scalar_tensor_tensor(out=out, in0=in0, scalar=scalar, in1=in1, op0=op0, op1=op1),
                reads, writes)


def cp(P, eng, out, in_, reads, writes):
    if eng == "act":
        return P.op("act", lambda e: e.copy(out=out, in_=in_), reads, writes)
    return P.op(eng, lambda e: e.tensor_copy(out=out, in_=in_), reads, writes)


def mset(P, eng, ap, val, writes):
    return P.op(eng, lambda e: e.memset(ap, val), (), writes)


def scan(P, out, d0, d1, reads, writes):
    return P.op("dve", lambda e: e.tensor_tensor_scan(out=out, data0=d0, data1=d1, initial=0.0,
                                                      op0=ALU.mult, op1=ALU.add), reads, writes)


def dma(P, q, out, in_, reads, writes):
    return P.dma(q, lambda e: e.dma_start(out=out, in_=in_), reads, writes)


class Arena:
    def __init__(self, t, nwords):
        self.t = t
        self.n = nwords
        self.off = 0

    def reset(self):
        self.off = 0

    def alloc(self, nelem, dtype):
        words = nelem if dtype == F32 else (nelem + 1) // 2
        a = self.t[:, self.off:self.off + words]
        self.off += words
        assert self.off <= self.n, f"arena overflow {self.off} > {self.n}"
        return a if dtype == F32 else a.bitcast(BF16)


LQ = "sp"
SQ = "pool"


def build_program(seq_lens, debug=False):
    nc = bass.Bass("TRN2", target_bir_lowering=False)
    SMAX = max(seq_lens)
    NSEQ = len(seq_lens)
    dk = "ExternalOutput" if debug else "Internal"

    def din(name, shape, dt=F32):
        return nc.dram_tensor(name, list(shape), dt, kind="ExternalInput").ap()

    def dscr(name, shape, dt, kind=None):
        return nc.dram_tensor(name, list(shape), dt, kind=kind or "Internal").ap()

    xs = [din(f"x{i}", [S, D]) for i, S in enumerate(seq_lens)]
    ps_in = [din(f"p{i}", [DEPTH, S, PLE]) for i, S in enumerate(seq_lens)]
    ys = [nc.dram_tensor(f"y{i}", [S, D], F32, kind="ExternalOutput").ap() for i, S in enumerate(seq_lens)]
    W = {}
    for name, shape in [("norm_g", [DEPTH, D]), ("w_in", [DEPTH, D, DIN]), ("w_out", [DEPTH, D, D]),
                        ("hg_lb", [DEPTH, 2, 256]), ("hg_norm", [DEPTH, 64]), ("lru_conv_w", [DEPTH, 4, 512]),
                        ("lru_conv_b", [DEPTH, 512]), ("lru_wa", [DEPTH, 2, 8, 64, 64]), ("lru_ba", [DEPTH, 2, 512]),
                        ("lru_wx", [DEPTH, 2, 8, 64, 64]), ("lru_bx", [DEPTH, 2, 512]), ("lru_lam", [DEPTH, 2, 512]),
                        ("da_lq1", [DEPTH, 32]), ("da_lk1", [DEPTH, 32]), ("da_lq2", [DEPTH, 32]),
                        ("da_lk2", [DEPTH, 32]), ("da_norm", [DEPTH, 64]), ("ple_w", [DEPTH, PLE, D]),
                        ("ple_gate_w", [DEPTH, D, D]), ("final_norm", [D])]:
        W[name] = din(name, shape)
    c_ident = din("c_ident", [128, 128])
    c_cos = din("c_cos", [128, SMAX])
    c_sin = din("c_sin", [128, SMAX])
    c_maskf = din("c_maskf", [128, 128])
    c_maskb = din("c_maskb", [128, 128])
    c_bones = din("c_bones", [128, 128])

    WIN_d = dscr("WIN_d", [DEPTH, 128, 8, NFM * 128], BF16)
    WV_d = dscr("WV_d", [DEPTH, 128, 8, 512], BF16)
    WOUT_d = dscr("WOUT_d", [DEPTH, 128, 8, D], BF16)
    WG_d = dscr("WG_d", [DEPTH, 128, 8, D], BF16)
    WPLE_d = dscr("WPLE_d", [DEPTH, 128, 2, D], BF16)
    hT_d = dscr("hT_d", [8, 128, SMAX], F32, dk)
    qh_d = dscr("qh_d", [256, SMAX], BF16, dk)
    sgf_d = dscr("sgf_d", [256, SMAX], F32, dk)
    sgb_d = dscr("sgb_d", [256, SMAX], F32, dk)
    gh_d = dscr("gh_d", [256, SMAX], BF16, dk)
    lx_d = dscr("lx_d", [512, SMAX], F32, dk)
    gl_d = dscr("gl_d", [512, SMAX], BF16, dk)
    dq_d = dscr("dq_d", [256, SMAX], BF16, dk)
    dk_d = dscr("dk_d", [256, SMAX], BF16, dk)
    gd_d = dscr("gd_d", [256, SMAX], BF16, dk)
    vh_d = dscr("vh_d", [SMAX, 256], BF16, dk)
    vd_d = dscr("vd_d", [SMAX, 256], BF16, dk)
    oT_d = dscr("oT_d", [D, SMAX], BF16, dk)

    P = Prog(nc)
    AR_WORDS = 46 * 1024
    CA_WORDS = 4608
    with ExitStack() as es:
        arena_t = es.enter_context(nc.sbuf_tensor("arena", [128, AR_WORDS], F32))
        cst_t = es.enter_context(nc.sbuf_tensor("cst", [128, CA_WORDS], F32))
        PS = [es.enter_context(nc.psum_tensor(f"ps{i}", [128, 512], F32)) for i in range(8)]
        bPS = P.bufs(8, "ps")
        AR = Arena(arena_t, AR_WORDS)
        CA = Arena(cst_t, CA_WORDS)
        bC = P.buf("consts")
        IDENT = CA.alloc(128, F32)
        IDENTB = CA.alloc(128, BF16)
        ONESB = CA.alloc(128, BF16)
        BONES = CA.alloc(128, BF16)
        MASKF = CA.alloc(128, BF16)
        MASKB = CA.alloc(128, BF16)
        NEGHALF = CA.alloc(512, F32)
        HALF = CA.alloc(512, F32)
        CTMP = CA.alloc(128, F32)
        EPSB = CA.alloc(1, F32)
        HM = CA.alloc(4, F32)

        dma(P, LQ, IDENT, c_ident, [], [bC])
        cp(P, "dve", IDENTB, IDENT, [bC], [bC])
        mset(P, "dve", ONESB, 1.0, [bC])
        mset(P, "dve", NEGHALF, -0.5, [bC])
        mset(P, "dve", HALF, 0.5, [bC])
        mset(P, "dve", EPSB, EPS, [bC])
        mset(P, "dve", HM, 0.0, [bC])
        for j in range(4):
            mset(P, "dve", HM[32 * j:32 * j + 32, j:j + 1], 1.0, [bC])
        for src, dst in ((c_bones, BONES), (c_maskf, MASKF), (c_maskb, MASKB)):
            dma(P, LQ, CTMP, src, [bC], [bC])
            cp(P, "dve", dst, CTMP, [bC], [bC])

        SB = []
        bCw = P.buf("cw")

        def nb_():
            b = P.buf("sm")
            SB.append(b)
            return [b]

        def load_cols(dst, src1d):
            C = dst.shape[1]
            for c in range(C):
                dma(P, LQ, dst[:, c:c + 1], src1d[c * 128:(c + 1) * 128].rearrange("(p o) -> p o", o=1), [], nb_())

        PR = []
        fg = CA.alloc(8, F32)
        load_cols(fg, W["final_norm"])
        AR.reset()
        bdfs = [AR.alloc(16 * 128, F32).rearrange("p (g d c e) -> p g d c e", g=2, d=2, c=4) for _ in range(DEPTH)]
        lqks = [AR.alloc(128, F32).rearrange("p (a d) -> p a d", a=4) for _ in range(DEPTH)]
        bBDFm = P.bufs(DEPTH, "bdfm")
        lbr = AR.alloc(8, F32).rearrange("p (l d c) -> p l d c", l=DEPTH, d=2)
        for ll in range(DEPTH):
            for d in range(2):
                load_cols(lbr[:, ll, d, :], W["hg_lb"][ll, d])
        for l in range(DEPTH):
            pr = {}
            bdf, lqk = bdfs[l], lqks[l]
            pr["gcol"] = CA.alloc(8, F32)
            load_cols(pr["gcol"], W["norm_g"][l])
            pr["gneg"] = CA.alloc(8, F32)
            ts(P, "dve", pr["gneg"], pr["gcol"], -1.0, None, ALU.mult, None, SB + [bCw], [bCw])
            cw = CA.alloc(16, F32).rearrange("p (c j) -> p c j", c=4)
            for j in range(4):
                for c in range(4):
                    dma(P, LQ, cw[:, c, j:j + 1],
                        W["lru_conv_w"][l, j, c * 128:(c + 1) * 128].rearrange("(p o) -> p o", o=1), [], nb_())
            pr["cw"] = cw
            pr["cb"] = CA.alloc(4, F32)
            load_cols(pr["cb"], W["lru_conv_b"][l])
            for nm, key in (("lru_ba", "bab"), ("lru_bx", "bxb"), ("lru_lam", "coef")):
                t = CA.alloc(8, F32).rearrange("p (d c) -> p d c", d=2)
                for d in range(2):
                    load_cols(t[:, d, :], W[nm][l, d])
                pr[key] = t
            cf = pr["coef"].rearrange("p d c -> p (d c)")
            act(P, cf, cf, AF.Exp, SB + [bCw], [bCw], scale=-1.0)
            act(P, cf, cf, AF.Ln, SB + [bCw], [bCw], bias=1.0)
            ts(P, "dve", cf, cf, -8.0, None, ALU.mult, None, SB + [bCw], [bCw])
            for key, src in (("coef2", "coef"), ("nbab", "bab"), ("nbxb", "bxb")):
                t = CA.alloc(8, F32).rearrange("p (d c) -> p d c", d=2)
                ts(P, "dve", t.rearrange("p d c -> p (d c)"), pr[src].rearrange("p d c -> p (d c)"),
                   2.0 if key == "coef2" else -1.0, None, ALU.mult, None, SB + [bCw], [bCw])
                pr[key] = t
            lbd = CA.alloc(4, F32).rearrange("p (d c) -> p d c", d=2)
            c1 = CA.alloc(4, F32).rearrange("p (d c) -> p d c", d=2)
            c1n = CA.alloc(4, F32).rearrange("p (d c) -> p d c", d=2)
            if l == 0:
                mset(P, "dve", lbd, 0.0, [bCw])
            else:
                tt(P, "dve", lbd, lbr[:, 0], lbr[:, 1], ALU.subtract, SB + [bCw], [bCw])
                act(P, lbd, lbd, AF.Exp, SB + [bCw], [bCw])
                ts(P, "dve", lbd, lbd, 1.0, None, ALU.add, None, SB + [bCw], [bCw])
                P.op("dve", (lambda a: (lambda e: e.reciprocal(out=a, in_=a)))(lbd), SB + [bCw], [bCw])
            ts(P, "dve", c1, lbd, -1.0, 1.0, ALU.mult, ALU.add, SB + [bCw], [bCw])
            ts(P, "dve", c1n, c1, -1.0, None, ALU.mult, None, SB + [bCw], [bCw])
            pr["lbd"], pr["c1"], pr["c1n"] = lbd, c1, c1n
            gn = CA.alloc(1, F32)
            dma(P, LQ, gn[0:64, :], W["hg_norm"][l].rearrange("(p o) -> p o", o=1), [], nb_())
            dma(P, LQ, gn[64:128, :], W["hg_norm"][l].rearrange("(p o) -> p o", o=1), [], nb_())
            pr["gn"] = gn
            lam_init = 0.8 - 0.6 * float(np.exp(-0.3 * l))
            lsm = CA.alloc(4, F32)
            for a, nm in enumerate(("da_lq1", "da_lk1", "da_lq2", "da_lk2")):
                dma(P, LQ, lqk[:, a, :], W[nm][l:l + 1, :].partition_broadcast(128), [], nb_())
            tt(P, "dve", lqk[:, 0, :], lqk[:, 0, :], lqk[:, 1, :], ALU.mult, SB + [bCw], [bCw])
            tt(P, "dve", lqk[:, 2, :], lqk[:, 2, :], lqk[:, 3, :], ALU.mult, SB + [bCw], [bCw])
            P.op("dve", (lambda o, i: (lambda e: e.reduce_sum(out=o, in_=i, axis=mybir.AxisListType.X)))(
                lsm[:, 0:1], lqk[:, 0, :]), SB + [bCw], [bCw])
            P.op("dve", (lambda o, i: (lambda e: e.reduce_sum(out=o, in_=i, axis=mybir.AxisListType.X)))(
                lsm[:, 1:2], lqk[:, 2, :]), SB + [bCw], [bCw])
            act(P, lsm[:, 0:2], lsm[:, 0:2], AF.Exp, SB + [bCw], [bCw])
            tt(P, "dve", lsm[:, 2:3], lsm[:, 1:2], lsm[:, 0:1], ALU.subtract, SB + [bCw], [bCw])
            ts(P, "dve", lsm[:, 3:4], lsm[:, 2:3], -lam_init, None, ALU.add, None, SB + [bCw], [bCw])
            pr["neglam"] = lsm[0:64, 3:4]
            dn = CA.alloc(1, F32)
            dma(P, LQ, dn[0:64, :], W["da_norm"][l].rearrange("(p o) -> p o", o=1), [], nb_())
            ts(P, "dve", dn[0:64, :], dn[0:64, :], 1.0 - lam_init, None, ALU.mult, None, SB + [bCw], [bCw])
            pr["dn"] = dn
            bdb = CA.alloc(16 * 128, BF16).rearrange("p (g d c e) -> p g d c e", g=2, d=2, c=4)
            mset(P, "pool", bdf, 0.0, [bBDFm[l]])
            for g, wname in enumerate(("lru_wa", "lru_wx")):
                for d in range(2):
                    for b in range(2):
                        src = W[wname][l, d].rearrange("(c b) ci e -> b ci c e", b=2)[b]
                        dma(P, LQ, bdf[64 * b:64 * b + 64, g, d, :, 64 * b:64 * b + 64], src, [bBDFm[l]], nb_())
            cp(P, "pool", bdb, bdf, SB + [bCw], [bCw])
            pr["bdb"] = bdb
            PR.append(pr)
        P.barrier()

        def prep_weights():
            AR.reset()
            stg = [AR.alloc(DIN, F32) for _ in range(2)]
            bstg = P.bufs(2, "wstg")
            ob = [AR.alloc(NFM * 128 + 512, BF16) for _ in range(2)]
            bob = P.bufs(2, "wob")
            bg = bC
            it = 0
            for l in range(DEPTH):
                for kc in range(8):
                    s = stg[it % 2]
                    o = ob[it % 2]
                    bs, bo = bstg[it % 2], bob[it % 2]
                    eng = "dve"
                    it += 1
                    dma(P, LQ, s, W["w_in"][l, kc * 128:(kc + 1) * 128, :], [], [bs])
                    g = PR[l]["gcol"][:, kc:kc + 1]
                    for (s0, s1, d0) in ((0, 768, 0), (1024, 1280, 768), (1280, 2304, 1024), (2304, 2816, 2048),
                                         (3072, 3328, 2560)):
                        ts(P, eng, o[:, d0:d0 + (s1 - s0)], s[:, s0:s1], g, None, ALU.mult, None, [bs, bg], [bo])
                    mset(P, eng, o[:, 2816:3328], 0.0, [bo])
                    sv = s[:, 2304:2816].rearrange("p (h d) -> p h d", d=32)
                    dv = o[:, 2816:3328].rearrange("p (h d) -> p h d", d=32)
                    ts(P, eng, dv[:, :, 0:4], sv[:, :, 4:8], PR[l]["gneg"][:, kc:kc + 1], None, ALU.mult, None, [bs, bg], [bo])
                    ts(P, eng, dv[:, :, 4:8], sv[:, :, 0:4], g, None, ALU.mult, None, [bs, bg], [bo])
                    ts(P, eng, o[:, 3328:3584], s[:, 768:1024], g, None, ALU.mult, None, [bs, bg], [bo])
                    ts(P, eng, o[:, 3584:3840], s[:, 2816:3072], g, None, ALU.mult, None, [bs, bg], [bo])
                    dma(P, SQ, WIN_d[l, :, kc, :], o[:, 0:3328], [bo], [])
                    dma(P, SQ, WV_d[l, :, kc, :], o[:, 3328:3840], [bo], [])
            for l in range(DEPTH):
                for (src, dst, nk) in ((W["w_out"], WOUT_d, 8), (W["ple_gate_w"], WG_d, 8), (W["ple_w"], WPLE_d, 2)):
                    for kc in range(nk):
                        s = stg[it % 2]
                        o = ob[it % 2]
                        bs, bo = bstg[it % 2], bob[it % 2]
                        eng = "dve" if it % 2 == 0 else "act"
                        it += 1
                        dma(P, LQ, s[:, 0:D], src[l, kc * 128:(kc + 1) * 128, :], [], [bs])
                        cp(P, eng, o[:, 0:D], s[:, 0:D], [bs], [bo])
                        dma(P, SQ, dst[l, :, kc, :], o[:, 0:D], [bo], [])
            P.barrier()

        def rstd_from_ss(ss_ps, n, npart, rst, brst, bss, inv_n):
            act(P, rst, ss_ps, AF.Ln, [bss], [brst], bias=EPSB[0:npart, :], scale=inv_n)
            act(P, rst, rst, AF.Exp, [brst], [brst], scale=-0.5)

        def phase0(si):
            S = seq_lens[si]
            AR.reset()
            xt = [AR.alloc(4 * D, F32) for _ in range(2)]
            bx = P.bufs(2, "xt")
            ht = [AR.alloc(8 * 512, F32) for _ in range(2)]
            bh = P.bufs(2, "ht")
            for ti in range(S // 512):
                X = xt[ti % 2].rearrange("p (s f) -> p s f", s=4)
                H = ht[ti % 2].rearrange("p (c t) -> p c t", c=8)
                dma(P, LQ, X, xs[si][ti * 512:(ti + 1) * 512, :].rearrange("(s p) f -> p s f", p=128), [],
                    [bx[ti % 2]])
                for c in range(8):
                    bank = c
                    for s in range(4):
                        tr(P, PS[bank][:, s * 128:(s + 1) * 128], X[:, s, c * 128:(c + 1) * 128], IDENT,
                           [bx[ti % 2], bC], [bPS[bank]])
                    cp(P, "dve" if c % 2 == 0 else "act", H[:, c, :], PS[bank][:, :], [bPS[bank]], [bh[ti % 2]])
                dma(P, SQ, hT_d[:, :, ti * 512:(ti + 1) * 512].rearrange("c p t -> p c t"), H, [bh[ti % 2]], [])
            P.barrier()

        def phase1(si, l):
            S = seq_lens[si]
            AR.reset()
            WIN = AR.alloc(8 * NFM * 128, BF16).rearrange("p (k f) -> p k f", k=8)
            WV = AR.alloc(8 * 512, BF16).rearrange("p (k f) -> p k f", k=8)
            bWg = P.bufs(NFM, "win")
            bW = P.buf("wv")

            def load_w1():
                for (f0, f1) in ((0, 2), (2, 6), (6, 12), (12, 18), (18, 26)):
                    dma(P, LQ, WIN[:, :, f0 * 128:f1 * 128], WIN_d[l, :, :, f0 * 128:f1 * 128], [], bWg[f0:f1])
                dma(P, LQ, WV, WV_d[l], [], [bW])
            ht = [AR.alloc(8 * 512, F32) for _ in range(2)]
            bh = P.bufs(2, "ht")
            sq = AR.alloc(8 * 512, BF16)
            bsq = P.buf("sq")
            hn = [AR.alloc(8 * 512, BF16) for _ in range(2)]
            bhn = P.bufs(2, "hn")
            rst = AR.alloc(512, F32)
            brst = P.buf("rst")
            cs = [AR.alloc(512, F32) for _ in range(2)]
            sn = [AR.alloc(512, F32) for _ in range(2)]
            bcs = P.bufs(2, "cs")
            NST = 6
            st = [AR.alloc(512, F32) for _ in range(NST)]
            bst = P.bufs(NST, "st")
            r1 = AR.alloc(512, F32)
            r2 = AR.alloc(512, F32)
            br = P.buf("r")
            sti = [0]
            bank_i = [0]

            def next_bank():
                b = 1 + bank_i[0] % 7
                bank_i[0] += 1
                return b

            def next_st():
                k = sti[0] % NST
                sti[0] += 1
                return st[k], bst[k]

            NTI = S // 512

            def pre(ti):
                t0 = ti * 512
                H = ht[ti % 2].rearrange("p (c t) -> p c t", c=8)
                HN = hn[ti % 2].rearrange("p (c t) -> p c t", c=8)
                SQv = sq.rearrange("p (c t) -> p c t", c=8)
                dma(P, LQ, H, hT_d[:, :, t0:t0 + 512].rearrange("c p t -> p c t"), [], [bh[ti % 2]])
                dma(P, LQ, cs[ti % 2], c_cos[:, t0:t0 + 512], [], [bcs[ti % 2]])
                dma(P, LQ, sn[ti % 2], c_sin[:, t0:t0 + 512], [], [bcs[ti % 2]])
                tt(P, "pool", SQv, H, H, ALU.mult, [bh[ti % 2]], [bsq])

            def pre_b(ti):
                H = ht[ti % 2].rearrange("p (c t) -> p c t", c=8)
                HN = hn[ti % 2].rearrange("p (c t) -> p c t", c=8)
                SQv = sq.rearrange("p (c t) -> p c t", c=8)
                for c in range(8):
                    mm(P, PS[0][:, :], ONESB, SQv[:, c, :], c == 0, c == 7, [bC, bsq], [bPS[0]])
                rstd_from_ss(PS[0][:, :], 512, 128, rst, brst, bPS[0], 1.0 / D)
                rb = bass.AP(rst.tensor, rst.offset, [list(rst.ap[0]), [0, 8], [1, 512]])
                tt(P, "dve", HN, H, rb, ALU.mult, [bh[ti % 2], brst], [bhn[ti % 2]])

            pre(0)
            pre_b(0)
            load_w1()
            for ti in range(NTI):
                t0 = ti * 512
                HN = hn[ti % 2].rearrange("p (c t) -> p c t", c=8)
                for fc in list(range(0, 22)):
                    if fc == 2 and ti + 1 < NTI:
                        pre(ti + 1)
                    if fc == 9 and ti + 1 < NTI:
                        pre_b(ti + 1)
                    b = next_bank()
                    for kc in range(8):
                        mm(P, PS[b][:, :], WIN[:, kc, fc * 128:(fc + 1) * 128], HN[:, kc, :], kc == 0, kc == 7,
                           [bWg[fc], bhn[ti % 2]], [bPS[b]])
                    if fc in (0, 1):
                        o, bo = next_st()
                        ob = o.bitcast(BF16)[:, 0:512]
                        act(P, ob, PS[b][:, :], AF.Silu, [bPS[b]], [bo])
                        dma(P, SQ, qh_d[fc * 128:(fc + 1) * 128, t0:t0 + 512], ob, [bo], [])
                    elif fc in (2, 3, 4, 5):
                        o, bo = next_st()
                        act(P, o, PS[b][:, :], AF.Sigmoid, [bPS[b]], [bo])
                        dst = sgf_d if fc < 4 else sgb_d
                        r0 = (fc % 2) * 128
                        dma(P, SQ, dst[r0:r0 + 128, t0:t0 + 512], o, [bo], [])
                    elif fc in (6, 7) or 12 <= fc <= 15 or fc in (20, 21):
                        o, bo = next_st()
                        ob = o.bitcast(BF16)[:, 0:512]
                        act(P, ob, PS[b][:, :], AF.Silu, [bPS[b]], [bo])
                        if fc in (6, 7):
                            dst, r0 = gh_d, (fc - 6) * 128
                        elif fc in (20, 21):
                            dst, r0 = gd_d, (fc - 20) * 128
                        else:
                            dst, r0 = gl_d, (fc - 12) * 128
                        dma(P, SQ, dst[r0:r0 + 128, t0:t0 + 512], ob, [bo], [])
                    elif 8 <= fc <= 11:
                        o, bo = next_st()
                        cp(P, "dve", o, PS[b][:, :], [bPS[b]], [bo])
                        dma(P, SQ, lx_d[(fc - 8) * 128:(fc - 7) * 128, t0:t0 + 512], o, [bo], [])
                    else:
                        b2 = next_bank()
                        for kc in range(8):
                            mm(P, PS[b2][:, :], WIN[:, kc, (fc + 6) * 128:(fc + 7) * 128], HN[:, kc, :], kc == 0,
                               kc == 7, [bWg[fc + 6], bhn[ti % 2]], [bPS[b2]])
                        o, bo = next_st()
                        ob = o.bitcast(BF16)[:, 0:512]
                        tt(P, "dve", r1, PS[b][:, :], cs[ti % 2], ALU.mult, [bPS[b], bcs[ti % 2]], [br])
                        tt(P, "dve", r2, PS[b2][:, :], sn[ti % 2], ALU.mult, [bPS[b2], bcs[ti % 2]], [br])
                        tt(P, "pool", ob, r1, r2, ALU.add, [br], [bo])
                        dst = dq_d if fc < 18 else dk_d
                        r0 = (fc % 2) * 128
                        dma(P, SQ, dst[r0:r0 + 128, t0:t0 + 512], ob, [bo], [])
                for s in range(4):
                    b = next_bank()
                    for kc in range(8):
                        mm(P, PS[b][:, :], HN[:, kc, s * 128:(s + 1) * 128], WV[:, kc, :], kc == 0, kc == 7,
                           [bW, bhn[ti % 2]], [bPS[b]])
                    o, bo = next_st()
                    ob = o.bitcast(BF16)[:, 0:512]
                    cp(P, "act", ob, PS[b][:, :], [bPS[b]], [bo])
                    dma(P, SQ, vh_d[t0 + s * 128:t0 + (s + 1) * 128, :], ob[:, 0:256], [bo], [])
                    dma(P, SQ, vd_d[t0 + s * 128:t0 + (s + 1) * 128, :], ob[:, 256:512], [bo], [])
            P.barrier()

        def phase_lru(si, l):
            S = seq_lens[si]
            TT = min(1024, S // 4)
            NTT = S // TT
            GW = min(512, TT)
            AR.reset()
            pr = PR[l]
            cw, cb, bab, bxb, coef, bdb = pr["cw"], pr["cb"], pr["bab"], pr["bxb"], pr["coef"], pr["bdb"]
            LX = [AR.alloc(S + 4, F32) for _ in range(2)]
            GL = [AR.alloc(S, BF16) for _ in range(2)]
            OB = [AR.alloc(S, BF16) for _ in range(2)]
            bLX, bGL, bOB = P.bufs(2, "lx"), P.bufs(2, "gl"), P.bufs(2, "ob")
            U32 = AR.alloc(S, F32)
            UB = AR.alloc(S, BF16)
            HS = AR.alloc(S, F32)
            bU, bUB, bHS = P.bufs(NTT, "u32"), P.bufs(NTT, "ub"), P.bufs(NTT, "hs")
            A_ = [AR.alloc(TT, F32) for _ in range(3)]
            I_ = [AR.alloc(TT, F32) for _ in range(3)]
            T_ = [AR.alloc(TT, F32) for _ in range(2)]
            H2 = [AR.alloc(TT, F32) for _ in range(2)]
            bA, bI, bT, bH2 = P.bufs(3, "la"), P.bufs(3, "li"), P.bufs(2, "lt"), P.bufs(2, "lh")
            bank_i = [0]

            def nbank():
                b = bank_i[0] % 8
                bank_i[0] += 1
                return b

            descs = []
            for c in range(4):
                passes = [(0, True), (1, False)] if c % 2 == 0 else [(1, False), (0, True)]
                for pi, (d, asc) in enumerate(passes):
                    tiles = list(range(NTT)) if asc else list(range(NTT - 1, -1, -1))
                    for idx, tt_ in enumerate(tiles):
                        descs.append(dict(c=c, pi=pi, d=d, tt=tt_, first=(idx == 0), lastt=(idx == NTT - 1),
                                          k3=len(descs) % 3, k=len(descs) % 2, banks=[]))
            prevd = [None]

            def stage_a1(ds):
                c, pi, d, tt_ = ds["c"], ds["pi"], ds["d"], ds["tt"]
                lx, blx = LX[c % 2], bLX[c % 2]
                gl, bgl = GL[c % 2], bGL[c % 2]
                a0 = tt_ * TT
                u32 = U32[:, a0:a0 + TT]
                ub = UB[:, a0:a0 + TT]
                if ds["first"] and pi == 0:
                    mset(P, "pool", lx[:, 0:2], 0.0, [blx])
                    mset(P, "pool", lx[:, S + 2:S + 4], 0.0, [blx])
                    dma(P, LQ, lx[:, 2:S + 2], lx_d[c * 128:(c + 1) * 128, 0:S], [], [blx])
                    dma(P, LQ, gl, gl_d[c * 128:(c + 1) * 128, 0:S], [], [bgl])
                if pi == 0:
                    ts(P, "dve", u32, lx[:, a0:a0 + TT], cw[:, c, 0:1], cb[:, c:c + 1], ALU.mult, ALU.add,
                       [blx, bC], [bU[tt_]])
                    for j in range(1, 4):
                        stt(P, u32, lx[:, a0 + j:a0 + j + TT], cw[:, c, j:j + 1], u32, ALU.mult, ALU.add,
                            [blx, bC, bU[tt_]], [bU[tt_]])
                    cp(P, "act", ub, u32, [bU[tt_]], [bUB[tt_]])
                for t0 in range(0, TT, GW):
                    b1, b2 = nbank(), nbank()
                    ds["banks"].append((t0, b1, b2))
                    mm(P, PS[b1][:, 0:GW], bdb[:, 0, d, c, :], ub[:, t0:t0 + GW], True, True, [bC, bUB[tt_]],
                       [bPS[b1]])
                    mm(P, PS[b2][:, 0:GW], bdb[:, 1, d, c, :], ub[:, t0:t0 + GW], True, True, [bC, bUB[tt_]],
                       [bPS[b2]])

            def stage_a2(ds):
                c, d, k3 = ds["c"], ds["d"], ds["k3"]
                for (t0, b1, b2) in ds["banks"]:
                    act(P, A_[k3][:, t0:t0 + GW], PS[b1][:, 0:GW], AF.Sigmoid, [bPS[b1], bC], [bA[k3]],
                        bias=bab[:, d, c:c + 1])
                    act(P, I_[k3][:, t0:t0 + GW], PS[b2][:, 0:GW], AF.Sigmoid, [bPS[b2], bC], [bI[k3]],
                        bias=bxb[:, d, c:c + 1])

            def stage_b_act(ds):
                c, d, k3, k = ds["c"], ds["d"], ds["k3"], ds["k"]
                act(P, A_[k3], A_[k3], AF.Exp, [bA[k3], bC], [bA[k3]], scale=coef[:, d, c:c + 1])
                act(P, T_[k], A_[k3], AF.Square, [bA[k3]], [bT[k]])
                act(P, T_[k], T_[k], AF.Sqrt, [bT[k]], [bT[k]], scale=-1.0, bias=1.0)

            def stage_b_rest(ds):
                c, pi, d, tt_, k3, k = ds["c"], ds["pi"], ds["d"], ds["tt"], ds["k3"], ds["k"]
                gl, bgl = GL[c % 2], bGL[c % 2]
                ob, bob = OB[c % 2], bOB[c % 2]
                a0 = tt_ * TT
                u32 = U32[:, a0:a0 + TT]
                tt(P, "dve", I_[k3], I_[k3], u32, ALU.mult, [bI[k3], bU[tt_]], [bI[k3]])
                tt(P, "dve", I_[k3], I_[k3], T_[k], ALU.mult, [bI[k3], bT[k]], [bI[k3]])
                if pi == 0:
                    dest, bdest = HS[:, a0:a0 + TT], bHS[tt_]
                else:
                    dest, bdest = H2[k], bH2[k]
                rd = [bA[k3], bI[k3]]
                if ds["first"]:
                    init = 0.0
                else:
                    pdest, pb = prevd[0]
                    init = pdest[:, TT - 1:TT] if d == 0 else pdest[:, 0:1]
                    rd = rd + [pb]
                if d == 0:
                    P.op("dve", (lambda o_, a_, b_, i_: (lambda e: e.tensor_tensor_scan(
                        out=o_, data0=a_, data1=b_, initial=i_, op0=ALU.mult, op1=ALU.add)))(
                        dest, A_[k3], I_[k3], init), rd, [bdest])
                else:
                    P.op("dve", (lambda o_, a_, b_, i_: (lambda e: e.tensor_tensor_scan(
                        out=o_, data0=a_, data1=b_, initial=i_, op0=ALU.mult, op1=ALU.add)))(
                        dest[:, ::-1], A_[k3][:, ::-1], I_[k3][:, ::-1], init), rd, [bdest])
                prevd[0] = (dest, bdest)
                if pi == 1:
                    tt(P, "pool", T_[k], H2[k], HS[:, a0:a0 + TT], ALU.add, [bH2[k], bHS[tt_]], [bT[k]])
                    tt(P, "dve", ob[:, a0:a0 + TT], T_[k], gl[:, a0:a0 + TT], ALU.mult, [bT[k], bgl], [bob])
                    if ds["lastt"]:
                        dma(P, SQ, oT_d[256 + c * 128:256 + (c + 1) * 128, 0:S], ob, [bob], [])

            nd = len(descs)
            for j in range(min(2, nd)):
                stage_a1(descs[j])
                stage_a2(descs[j])
            for i in range(nd):
                if i + 2 < nd:
                    stage_a1(descs[i + 2])
                stage_b_act(descs[i])
                if i + 2 < nd:
                    stage_a2(descs[i + 2])
                stage_b_rest(descs[i])
            P.barrier()

        def lru_generator(si, l):
            S = seq_lens[si]
            TT = 512
            NTT = S // TT
            pr = PR[l]
            cw, cb, coef, coef2, bdb = pr["cw"], pr["cb"], pr["coef"], pr["coef2"], pr["bdb"]
            nbab, nbxb = pr["nbab"], pr["nbxb"]
            LX = AR.alloc(S + 4, F32)
            GL = AR.alloc(S, BF16)
            OB = AR.alloc(S, BF16)
            U32 = AR.alloc(S, F32)
            UB = AR.alloc(S, BF16)
            HS = AR.alloc(S, F32)
            bLX, bGL, bOB = P.buf("lx"), P.buf("gl"), P.buf("ob")
            bU, bUB, bHS = P.bufs(NTT, "u32"), P.bufs(NTT, "ub"), P.bufs(NTT, "hs")
            NS = 2
            ER = [AR.alloc(TT, F32) for _ in range(NS)]
            EI = [AR.alloc(TT, F32) for _ in range(NS)]
            A_ = [AR.alloc(TT, F32) for _ in range(NS)]
            T_ = [AR.alloc(TT, F32) for _ in range(NS)]
            H2 = [AR.alloc(TT, F32) for _ in range(NS)]
            bER, bEI, bA, bT, bH2 = (P.bufs(NS, "ler"), P.bufs(NS, "lei"), P.bufs(NS, "la"), P.bufs(NS, "lt"),
                                     P.bufs(NS, "lh"))
            descs = []
            for c in range(4):
                passes = [(0, True), (1, False)] if c % 2 == 0 else [(1, False), (0, True)]
                for pi, (d, asc) in enumerate(passes):
                    tiles = list(range(NTT)) if asc else list(range(NTT - 1, -1, -1))
                    for idx, tt_ in enumerate(tiles):
                        descs.append(dict(c=c, pi=pi, d=d, tt=tt_, first=(idx == 0), lastt=(idx == NTT - 1),
                                          k=len(descs) % NS))

            def load_lx(c):
                mset(P, "pool", LX[:, 0:2], 0.0, [bLX])
                mset(P, "pool", LX[:, S + 2:S + 4], 0.0, [bLX])
                dma(P, LQ, LX[:, 2:S + 2], lx_d[c * 128:(c + 1) * 128, 0:S], [], [bLX])

            def stage_a(ds):
                c, pi, d, tt_, k = ds["c"], ds["pi"], ds["d"], ds["tt"], ds["k"]
                a0 = tt_ * TT
                u32 = U32[:, a0:a0 + TT]
                ub = UB[:, a0:a0 + TT]
                if ds["first"] and pi == 0:
                    if c == 0:
                        load_lx(0)
                if ds["first"] and pi == 1 and c + 1 < 4:
                    load_lx(c + 1)
                if pi == 0:
                    ts(P, "dve", u32, LX[:, a0:a0 + TT], cw[:, c, 0:1], cb[:, c:c + 1], ALU.mult, ALU.add,
                       [bLX, bC], [bU[tt_]])
                    for j in range(1, 4):
                        stt(P, u32, LX[:, a0 + j:a0 + j + TT], cw[:, c, j:j + 1], u32, ALU.mult, ALU.add,
                            [bLX, bC, bU[tt_]], [bU[tt_]])
                    yield
                    cp(P, "act", ub, u32, [bU[tt_]], [bUB[tt_]])
                    yield
                mm(P, PS[6][:, :], bdb[:, 0, d, c, :], ub, True, True, [bC, bUB[tt_]], [bPS[6]])
                mm(P, PS[7][:, :], bdb[:, 1, d, c, :], ub, True, True, [bC, bUB[tt_]], [bPS[7]])
                yield
                act(P, ER[k], PS[6][:, :], AF.Exp, [bPS[6], bC], [bER[k]], bias=nbab[:, d, c:c + 1], scale=-1.0)
                act(P, EI[k], PS[7][:, :], AF.Exp, [bPS[7], bC], [bEI[k]], bias=nbxb[:, d, c:c + 1], scale=-1.0)
                yield
                act(P, ER[k], ER[k], AF.Ln, [bER[k]], [bER[k]], bias=1.0)
                act(P, EI[k], EI[k], AF.Ln, [bEI[k]], [bEI[k]], bias=1.0)
                yield
                act(P, ER[k], ER[k], AF.Exp, [bER[k]], [bER[k]], scale=-1.0)
                act(P, EI[k], EI[k], AF.Exp, [bEI[k]], [bEI[k]], scale=-1.0)
                yield

            prevd = [None]

            def stage_b(ds):
                c, pi, d, tt_, k = ds["c"], ds["pi"], ds["d"], ds["tt"], ds["k"]
                a0 = tt_ * TT
                u32 = U32[:, a0:a0 + TT]
                if pi == 0 and ds["lastt"]:
                    dma(P, LQ, GL, gl_d[c * 128:(c + 1) * 128, 0:S], [], [bGL])
                act(P, A_[k], ER[k], AF.Exp, [bER[k], bC], [bA[k]], scale=coef[:, d, c:c + 1])
                act(P, T_[k], ER[k], AF.Exp, [bER[k], bC], [bT[k]], scale=coef2[:, d, c:c + 1])
                yield
                act(P, T_[k], T_[k], AF.Ln, [bT[k]], [bT[k]], scale=-1.0, bias=1.0)
                act(P, T_[k], T_[k], AF.Exp, [bT[k]], [bT[k]], scale=0.5)
                tt(P, "dve", EI[k], EI[k], u32, ALU.mult, [bEI[k], bU[tt_]], [bEI[k]])
                yield
                tt(P, "dve", EI[k], EI[k], T_[k], ALU.mult, [bEI[k], bT[k]], [bEI[k]])
                if pi == 0:
                    dest, bdest = HS[:, a0:a0 + TT], bHS[tt_]
                else:
                    dest, bdest = H2[k], bH2[k]
                rd = [bA[k], bEI[k]]
                if ds["first"]:
                    init = 0.0
                else:
                    pdest, pb = prevd[0]
                    init = pdest[:, TT - 1:TT] if d == 0 else pdest[:, 0:1]
                    rd = rd + [pb]
                if d == 0:
                    P.op("dve", (lambda o_, a_, b_, i_: (lambda e: e.tensor_tensor_scan(
                        out=o_, data0=a_, data1=b_, initial=i_, op0=ALU.mult, op1=ALU.add)))(
                        dest, A_[k], EI[k], init), rd, [bdest])
                else:
                    P.op("dve", (lambda o_, a_, b_, i_: (lambda e: e.tensor_tensor_scan(
                        out=o_, data0=a_, data1=b_, initial=i_, op0=ALU.mult, op1=ALU.add)))(
                        dest[:, ::-1], A_[k][:, ::-1], EI[k][:, ::-1], init), rd, [bdest])
                prevd[0] = (dest, bdest)
                yield
                if pi == 1:
                    tt(P, "pool", T_[k], H2[k], HS[:, a0:a0 + TT], ALU.add, [bH2[k], bHS[tt_]], [bT[k]])
                    tt(P, "dve", OB[:, a0:a0 + TT], T_[k], GL[:, a0:a0 + TT], ALU.mult, [bT[k], bGL], [bOB])
                    if ds["lastt"]:
                        dma(P, SQ, oT_d[256 + c * 128:256 + (c + 1) * 128, 0:S], OB, [bOB], [])
                    yield

            nd = len(descs)
            for x in stage_a(descs[0]):
                yield
            for i in range(nd):
                if i + 1 < nd:
                    for x in stage_a(descs[i + 1]):
                        yield
                for x in stage_b(descs[i]):
                    yield

        def phase_da(si, l):
            S = seq_lens[si]
            NB = S // 128
            lam_init = 0.8 - 0.6 * float(np.exp(-0.3 * l))
            AR.reset()
            pr = PR[l]
            neglam, dn = pr["neglam"], pr["dn"]
            bsm = bC
            QT = AR.alloc(2 * S, BF16).rearrange("p (c t) -> p c t", c=2)
            KT = AR.alloc(2 * S, BF16).rearrange("p (c t) -> p c t", c=2)
            VA = AR.alloc(NB * 512, BF16).rearrange("p (n h f) -> p n h f", n=NB, h=4)
            bQ, bK, bV, bVA = P.bufs(4, "da")
            mset(P, "dve", VA[:, :, :, 64:128], 1.0, [bVA])
            for h_ in range(4):
                dma(P, LQ, VA[:, :, h_, 0:64],
                    vd_d[0:S, 64 * h_:64 * h_ + 64].rearrange("(n p) f -> p n f", p=128), [], [bVA])
            dma(P, LQ, QT, dq_d[:, 0:S].rearrange("(c p) t -> p c t", p=128), [], [bQ])
            dma(P, LQ, KT, dk_d[:, 0:S].rearrange("(c p) t -> p c t", p=128), [], [bK])
            lru_gen = lru_generator(si, l) if FUSE_LRU else None
            NPT = 4
            PT = [AR.alloc(512, BF16) for _ in range(NPT)]
            bPT = P.bufs(NPT, "pt")
            RZ = AR.alloc(512, F32)
            OS = AR.alloc(512, F32)
            ON = [AR.alloc(512, F32) for _ in range(2)]
            DD = AR.alloc(512, F32)
            D2 = AR.alloc(512, BF16)
            RS = AR.alloc(512, F32)
            GD = [AR.alloc(512, BF16) for _ in range(2)]
            OO = [AR.alloc(512, BF16) for _ in range(2)]
            bRZ, bOS, bDD, bD2, bRS = P.bufs(5, "dae")
            bON = P.bufs(2, "on")
            bGD = P.bufs(2, "gd")
            bOO = P.bufs(2, "oo")
            scale = 32.0 ** -0.5
            QM = [AR.alloc(512, BF16) for _ in range(3)]
            bQM = P.bufs(3, "qm")
            NQC = S // 512
            LOOK = 2
            steps = []
            for h in range(4):
                for qc in range(NQC):
                    for s_ in range(2):
                        for kb in range(NB):
                            steps.append((h, qc, s_, kb))
            n = len(steps)
            deferred = []
            gidx = {}

            def epilogue_a(h, qc, s_):
                acc = 3 + s_
                P.op("dve", (lambda a: (lambda e: e.reciprocal(out=RZ[64:128, :], in_=PS[a][64:128, :])))(acc),
                     [bPS[acc]], [bRZ])
                cp(P, "dve", OS[0:64, :], PS[acc][0:64, :], [bPS[acc]], [bOS])

            def epilogue_b1(h, qc, s_):
                mm(P, PS[5][0:64, :], IDENT[64:128, 64:128], RZ[64:128, :], True, True, [bC, bRZ], [bPS[5]])
                tt(P, "dve", ON[s_][0:64, :], OS[0:64, :], PS[5][0:64, :], ALU.mult, [bOS, bPS[5]], [bON[s_]])

            def epilogue_b2(h, qc, s_):
                stt(P, DD[0:64, :], ON[1][0:64, :], neglam, ON[0][0:64, :], ALU.mult, ALU.add,
                    [bON[0], bON[1], bsm], [bDD])
                tt(P, "dve", D2[0:64, :], DD[0:64, :], DD[0:64, :], ALU.mult, [bDD], [bD2])

            def epilogue_b3(h, qc, s_):
                mm(P, PS[5][0:64, :], ONESB[0:64, 0:64], D2[0:64, :], True, True, [bC, bD2], [bPS[5]])

            def epilogue_b4(h, qc, s_):
                q0 = qc * 512
                k = gidx[(h, qc)]
                gdt, bgd = GD[k % 2], bGD[k % 2]
                oot, boo = OO[k % 2], bOO[k % 2]
                rstd_from_ss(PS[5][0:64, :], 512, 64, RS[0:64, :], bRS, bPS[5], 1.0 / 64)
                stt(P, DD[0:64, :], DD[0:64, :], dn[0:64, :], RS[0:64, :], ALU.mult, ALU.mult, [bDD, bsm, bRS],
                    [bDD])
                tt(P, "dve", oot[0:64, :], DD[0:64, :], gdt[0:64, :], ALU.mult, [bDD, bgd], [boo])
                dma(P, SQ, oT_d[768 + 64 * h:768 + 64 * h + 64, q0:q0 + 512], oot[0:64, :], [boo], [])

            groups = []
            for h in range(4):
                for qc in range(NQC):
                    for s_ in range(2):
                        groups.append((h, qc, s_))
            qm_of = {}

            def make_qm(g):
                if g >= len(groups):
                    return
                h, qc, s_ = groups[g]
                hd = 2 * h + s_
                qm, bqm = QM[g % 3], bQM[g % 3]
                qm_of[(h, qc, s_)] = (qm, bqm)
                ts(P, "dve", qm, QT[:, hd // 4, qc * 512:(qc + 1) * 512], HM[:, hd % 4:hd % 4 + 1], None, ALU.mult,
                   None, [bQ, bC], [bqm])

            d1 = max(1, min(8, NB // 4))
            if lru_gen is not None:
                n_y = 4 * 2 * (S // 512) * 10 + 16
                stride = max(1, int(0.92 * n) // n_y)
            make_qm(0)
            for i in range(n + LOOK):
                if lru_gen is not None and i % stride == 0:
                    next(lru_gen, None)
                if i < n:
                    h, qc, s_, kb = steps[i]
                    hd = 2 * h + s_
                    ch = hd // 4
                    q0 = qc * 512
                    if kb == 0:
                        if s_ == 0:
                            k = len(gidx)
                            gidx[(h, qc)] = k
                            dma(P, LQ, GD[k % 2][0:64, :], gd_d[64 * h:64 * h + 64, q0:q0 + 512], [], [bGD[k % 2]])
                        make_qm(i // NB + 1)
                    qm, bqm = qm_of[(h, qc, s_)]
                    sb = i % 3
                    mm(P, PS[sb][:, :], KT[:, ch, kb * 128:(kb + 1) * 128], qm, True, True, [bK, bqm], [bPS[sb]])
                j = i - LOOK
                if j >= 0:
                    h, qc, s_, kb = steps[j]
                    sb = j % 3
                    acc = 3 + s_
                    pt, bpt = PT[j % NPT], bPT[j % NPT]
                    act(P, pt, PS[sb][:, :], AF.Exp, [bPS[sb]], [bpt], scale=scale)
                    mm(P, PS[acc][:, :], VA[:, kb, h, :], pt, kb == 0, kb == NB - 1, [bVA, bpt], [bPS[acc]])
                    deferred.sort(key=lambda t: t[0])
                    while deferred and deferred[0][0] <= j:
                        _, fn_, args = deferred.pop(0)
                        fn_(*args)
                    if kb == NB - 1:
                        epilogue_a(h, qc, s_)
                        deferred.append((j + d1, epilogue_b1, (h, qc, s_)))
                        if s_ == 1:
                            deferred.append((j + 2 * d1, epilogue_b2, (h, qc, s_)))
                            deferred.append((j + 3 * d1, epilogue_b3, (h, qc, s_)))
                            deferred.append((j + 4 * d1, epilogue_b4, (h, qc, s_)))
            deferred.sort(key=lambda t: t[0])
            while deferred:
                _, fn_, args = deferred.pop(0)
                fn_(*args)
            if lru_gen is not None:
                for _ in lru_gen:
                    pass
            P.barrier()

        def phase_hg(si, l):
            S = seq_lens[si]
            NT = S // 128
            N = S // 32
            AR.reset()
            pr = PR[l]
            lbd, c1, c1n, gn = pr["lbd"], pr["c1"], pr["c1n"], pr["gn"]
            bsm = bC
            RM = AR.alloc(S + 32, BF16)
            bRM = P.buf("rm")
            mset(P, "dve", RM, 1.0, [bRM])
            mset(P, "dve", RM.rearrange("p (n j) -> p n j", j=32)[:, :, 0:1], 0.0, [bRM])
            RMf = RM[:, 0:S]
            RMb = RM[:, 1:S + 1]
            VH = AR.alloc(NT * 256, BF16).rearrange("p (n f) -> p n f", n=NT)
            bVH = P.buf("vh")
            dma(P, LQ, VH, vh_d[0:S, :].rearrange("(n p) f -> p n f", p=128), [], [bVH])
            B1 = AR.alloc(S, F32)
            B2 = AR.alloc(S, F32)
            B3 = AR.alloc(S, F32)
            B4 = AR.alloc(S, F32)
            B5 = AR.alloc(S, F32)
            TQ = min(1024, S)
            NTQ = S // TQ
            bB1, bB2, bB3, bB4, bB5 = [P.bufs(NTQ, f"hgB{k}") for k in range(5)]
            QH = AR.alloc(S, BF16)
            QTt = AR.alloc(S, BF16)
            KTt = AR.alloc(S, BF16)
            KH = QH
            KHT = B3.bitcast(BF16)[:, 0:S].rearrange("p (n f) -> p n f", n=NT)
            bQH, bQT, bKT = [P.bufs(NTQ, f"hgq{k}") for k in range(3)]
            bKH = bQH
            bKHT = bB3
            DEC = AR.alloc(N, F32)
            bDEC = P.buf("dec")
            b4b = B4.bitcast(BF16)
            assert (N + 1) * 64 <= 4 * S
            SS = bass.AP(b4b.tensor, b4b.offset, [list(b4b.ap[0]), [64, N + 1], [1, 64]])
            bSSl = bB4 + bB5
            OACC = AR.alloc(S, F32)
            bOA = P.buf("oacc")
            NG = NT // 4
            AT = [[AR.alloc(512, BF16) for _ in range(2)] for _ in range(NG)]
            bAT = [P.bufs(2, "at") for _ in range(NG)]
            OTMP = [AR.alloc(512, F32) for _ in range(2)]
            bOTMP = P.bufs(2, "otmp")
            GH = AR.alloc(512, BF16)
            SQo = AR.alloc(512, BF16)
            RSo = AR.alloc(512, F32)
            OOo = AR.alloc(512, BF16)
            bGH, bSQo, bRSo, bOOo = P.bufs(4, "hgo")
            U32 = bass.AP(B1.tensor, B1.offset, [list(B1.ap[0]), [64, N], [1, 64]]) if N * 64 <= 2 * S else None
            assert U32 is not None
            bU = bB1 + bB2
            for hp in range(2):
                for d in range(2):
                    sg_d = sgf_d if d == 0 else sgb_d
                    RMd = RMf if d == 0 else RMb
                    mask = MASKF if d == 0 else MASKB
                    NCQ = TQ // 32
                    last = 31 if d == 0 else 0
                    sls = [slice(q * TQ, (q + 1) * TQ) for q in range(NTQ)]
                    QS = list(enumerate(sls))
                    for q, sl in QS:
                        dma(P, LQ, B1[:, sl], sg_d[hp * 128:(hp + 1) * 128, sl], [], [bB1[q]])
                        dma(P, LQ, QH[:, sl], qh_d[hp * 128:(hp + 1) * 128, sl], [], [bQH[q]])
                    for q, sl in QS:
                        ts(P, "dve", B2[:, sl], B1[:, sl], c1[:, d, hp:hp + 1], lbd[:, d, hp:hp + 1], ALU.mult,
                           ALU.add, [bB1[q], bsm], [bB2[q]])
                    for q, sl in QS:
                        act(P, B2[:, sl], B2[:, sl], AF.Ln, [bB2[q]], [bB2[q]])
                    for q, sl in QS:
                        act(P, B1[:, sl], B1[:, sl], AF.Identity, [bB1[q], bsm], [bB1[q]], bias=c1[:, d, hp:hp + 1],
                            scale=c1n[:, d, hp:hp + 1])
                    for q, sl in QS:
                        if d == 0:
                            scan(P, B3[:, sl], RMd[:, sl], B2[:, sl], [bRM, bB2[q]], [bB3[q]])
                        else:
                            scan(P, B3[:, sl][:, ::-1], RMd[:, sl][:, ::-1], B2[:, sl][:, ::-1], [bRM, bB2[q]],
                                 [bB3[q]])
                    for q, sl in QS:
                        act(P, B4[:, sl], B3[:, sl], AF.Exp, [bB3[q]], [bB4[q]])
                    for q, sl in QS:
                        act(P, B5[:, sl], B3[:, sl], AF.Exp, [bB3[q]], [bB5[q]], scale=-1.0)
                    for q, sl in QS:
                        tt(P, "dve", QTt[:, sl], QH[:, sl], B4[:, sl], ALU.mult, [bQH[q], bB4[q]], [bQT[q]])
                    for q, sl in QS:
                        tt(P, "dve", KTt[:, sl], B1[:, sl], B5[:, sl], ALU.mult, [bB1[q], bB5[q]], [bKT[q]])
                    for q, sl in QS:
                        clb = bass.AP(B3.tensor, B3.offset + q * TQ + last, [list(B3.ap[0]), [32, NCQ], [0, 32]])
                        tt(P, "dve", B5[:, sl].rearrange("p (n j) -> p n j", j=32), clb,
                           B3[:, sl].rearrange("p (n j) -> p n j", j=32), ALU.subtract, [bB3[q], bB5[q]], [bB5[q]])
                    for q, sl in QS:
                        act(P, B5[:, sl], B5[:, sl], AF.Exp, [bB5[q]], [bB5[q]])
                    for q, sl in QS:
                        tt(P, "dve", KH[:, sl], B1[:, sl], B5[:, sl], ALU.mult, [bB1[q], bB5[q]], [bKH[q]])
                    e4 = bass.AP(B4.tensor, B4.offset + last, [list(B4.ap[0]), [32, N]])
                    cp(P, "dve", DEC, e4, bB4, [bDEC])
                    if HG_STOP <= 1:
                        continue
                    for T0 in range(0, NT, 4):
                        b = (T0 // 4) % 2
                        psb = PS[b].bitcast(BF16)
                        for t4 in range(4):
                            T = T0 + t4
                            tr(P, psb[:, t4 * 128:(t4 + 1) * 128], KH[:, T * 128:(T + 1) * 128], IDENTB,
                               [bKH[(T * 128) // TQ], bC], [bPS[b]])
                        cp(P, "act", KHT[:, T0:T0 + 4, :], psb[:, 0:512].rearrange("p (t f) -> p t f", t=4),
                           [bPS[b]], bKHT)
                    if HG_STOP <= 2:
                        continue
                    TG = 8
                    for T0 in range(0, NT, TG):
                        nt = min(TG, NT - T0)
                        for tt_ in range(nt):
                            T = T0 + tt_
                            for m in range(4):
                                for e_ in range(2):
                                    mm(P, PS[2 + m][64 * e_:64 * e_ + 64, tt_ * 64:(tt_ + 1) * 64],
                                       KHT[32 * m:32 * m + 32, T, 64 * e_:64 * e_ + 64],
                                       VH[32 * m:32 * m + 32, T, 64 * (2 * hp + e_):64 * (2 * hp + e_) + 64],
                                       True, True, bKHT + [bVH], [bPS[2 + m]], tp=(32 * m, 64 * e_))
                        for m in range(4):
                            udst = bass.AP(B1.tensor, B1.offset + (4 * T0 + m) * 64,
                                           [list(B1.ap[0]), [256, nt], [1, 64]])
                            cp(P, "act", udst,
                               PS[2 + m][:, 0:nt * 64].rearrange("p (t v) -> p t v", v=64), [bPS[2 + m]], bU)
                    if HG_STOP <= 3:
                        continue
                    if d == 0:
                        mset(P, "pool", SS[:, 0, :], 0.0, bSSl)
                    else:
                        mset(P, "pool", SS[:, N, :], 0.0, bSSl)
                    def emit_scan(v):
                        u_v = bass.AP(B1.tensor, B1.offset + v, [list(B1.ap[0]), [64, N]])
                        if d == 0:
                            o_v = bass.AP(SS.tensor, SS.offset + 64 + v, [list(SS.ap[0]), [64, N]])
                            scan(P, o_v, DEC, u_v, [bDEC] + bU, bSSl)
                        else:
                            o_v = bass.AP(SS.tensor, SS.offset + v, [list(SS.ap[0]), [64, N]])
                            scan(P, o_v[:, ::-1], DEC[:, ::-1], u_v[:, ::-1], [bDEC] + bU, bSSl)

                    vper = (64 + NG - 1) // NG
                    vnext = 0
                    for G in range(NG):
                        for e_ in range(2):
                            sb = (2 * G + e_) % 4
                            for t4 in range(4):
                                T = G * 4 + t4
                                qi = (T * 128) // TQ
                                mm(P, PS[sb][:, t4 * 128:(t4 + 1) * 128], KTt[64 * e_:64 * e_ + 64, T * 128:(T + 1) * 128],
                                   QTt[64 * e_:64 * e_ + 64, T * 128:(T + 1) * 128], True, True, [bKT[qi], bQT[qi]],
                                   [bPS[sb]])
                        for _ in range(vper):
                            if vnext < 64:
                                emit_scan(vnext)
                                vnext += 1
                        for e_ in range(2):
                            sb = (2 * G + e_) % 4
                            at, bat = AT[G][e_], bAT[G][e_]
                            mb = bass.AP(mask.tensor, mask.offset, [list(mask.ap[0]), [0, 4], [1, 128]])
                            cp(P, "act", at, PS[sb][:, :], [bPS[sb]], [bat])
                            tt(P, "pool", at.rearrange("p (a i) -> p a i", a=4),
                               at.rearrange("p (a i) -> p a i", a=4), mb, ALU.mult, [bat, bC], [bat])
                    while vnext < 64:
                        emit_scan(vnext)
                        vnext += 1
                    for G in range(NG):
                        for e_ in range(2):
                            ob = 6 + e_
                            at, bat = AT[G][e_], bAT[G][e_]
                            for t4 in range(4):
                                T = G * 4 + t4
                                qi = (T * 128) // TQ
                                vcol = 64 * (2 * hp + e_)
                                mm(P, PS[ob][64 * e_:64 * e_ + 64, t4 * 128:(t4 + 1) * 128],
                                   VH[:, T, vcol:vcol + 64], at[:, t4 * 128:(t4 + 1) * 128], True, False,
                                   [bVH, bat], [bPS[ob]], tp=(0, 64 * e_))
                                for m in range(4):
                                    n = 4 * T + m
                                    sidx = n if d == 0 else n + 1
                                    mm(P, PS[ob][64 * e_:64 * e_ + 64, t4 * 128 + 32 * m:t4 * 128 + 32 * m + 32],
                                       SS[64 * e_:64 * e_ + 64, sidx, :],
                                       QTt[64 * e_:64 * e_ + 64, T * 128 + 32 * m:T * 128 + 32 * m + 32],
                                       False, m == 3, bSSl + [bQT[qi]], [bPS[ob]], tp=(64 * e_, 64 * e_))
                        for e_ in range(2):
                            ob = 6 + e_
                            oa = OACC[64 * e_:64 * e_ + 64, G * 512:(G + 1) * 512]
                            if d == 0:
                                cp(P, "act", oa, PS[ob][64 * e_:64 * e_ + 64, :], [bPS[ob]], [bOA])
                            else:
                                otmp, botmp = OTMP[e_], bOTMP[e_]
                                cp(P, "act", otmp[64 * e_:64 * e_ + 64, :], PS[ob][64 * e_:64 * e_ + 64, :], [bPS[ob]],
                                   [botmp])
                                tt(P, "pool", oa, oa, otmp[64 * e_:64 * e_ + 64, :], ALU.add, [bOA, botmp], [bOA])
                if HG_STOP <= 5:
                    continue
                for G in range(S // 512):
                    oa = OACC[:, G * 512:(G + 1) * 512]
                    dma(P, LQ, GH, gh_d[hp * 128:(hp + 1) * 128, G * 512:(G + 1) * 512], [], [bGH])
                    tt(P, "pool", SQo, oa, oa, ALU.mult, [bOA], [bSQo])
                    mm(P, PS[0][:, :], BONES, SQo, True, True, [bC, bSQo], [bPS[0]])
                    rstd_from_ss(PS[0][:, :], 512, 128, RSo, bRSo, bPS[0], 1.0 / 64)
                    stt(P, RSo, oa, gn[:, 0:1], RSo, ALU.mult, ALU.mult, [bOA, bsm, bRSo], [bRSo])
                    tt(P, "dve", OOo, RSo, GH, ALU.mult, [bRSo, bGH], [bOOo])
                    dma(P, SQ, oT_d[hp * 128:(hp + 1) * 128, G * 512:(G + 1) * 512], OOo, [bOOo], [])
            P.barrier()

        def phase3(si, l):
            S = seq_lens[si]
            last = (l == DEPTH - 1)
            NTI = S // 512
            AR.reset()
            WO = AR.alloc(8 * D, BF16).rearrange("p (k f) -> p k f", k=8)
            WGt = AR.alloc(8 * D, BF16).rearrange("p (k f) -> p k f", k=8)
            WP = AR.alloc(2 * D, BF16).rearrange("p (k f) -> p k f", k=2)
            bW = P.buf("w3")
            bWG3 = P.buf("w3g")
            NH = 3
            ot = [AR.alloc(8 * 512, BF16) for _ in range(2)]
            ht = [AR.alloc(8 * 512, F32) for _ in range(NH)]
            pt = [AR.alloc(4 * PLE, F32) for _ in range(2)]
            bot = P.bufs(2, "ot")
            bht = [P.bufs(8, f"ht3_{k}") for k in range(NH)]
            bpt = P.bufs(2, "pt3")
            pTs = [AR.alloc(2 * 512, BF16).rearrange("p (k t) -> p k t", k=2) for _ in range(2)]
            bpT = P.bufs(2, "pT")
            hnbs = [AR.alloc(8 * 512, BF16).rearrange("p (c t) -> p c t", c=8) for _ in range(2)]
            bhnb = P.bufs(2, "hnb")
            gt = [AR.alloc(512, F32) for _ in range(2)]
            bgt = P.bufs(2, "gt")
            sq = AR.alloc(8 * 512, BF16).rearrange("p (c t) -> p c t", c=8) if last else None
            bsq = P.buf("sq3")
            rst = AR.alloc(512, F32)
            brst = P.buf("rst3")
            yt = AR.alloc(4 * D, F32).rearrange("p (s f) -> p s f", s=4) if last else None
            byt = P.buf("yt")
            bi = [0]

            def nb():
                b = 1 + bi[0] % 7
                bi[0] += 1
                return b

            def stage_loads(ti):
                t0 = ti * 512
                O = ot[ti % 2].rearrange("p (c t) -> p c t", c=8)
                H = ht[ti % NH].rearrange("p (c t) -> p c t", c=8)
                bH = bht[ti % NH]
                PT_ = pt[ti % 2].rearrange("p (s f) -> p s f", s=4)
                dma(P, LQ, PT_, ps_in[si][l, t0:t0 + 512, :].rearrange("(s p) f -> p s f", p=128), [],
                    [bpt[ti % 2]])
                dma(P, LQ, O, oT_d[:, t0:t0 + 512].rearrange("(c p) t -> p c t", p=128), [], [bot[ti % 2]])
                dma(P, LQ, H, hT_d[:, :, t0:t0 + 512].rearrange("c p t -> p c t"), [], bH)

            def stage_wo(ti):
                t0 = ti * 512
                O = ot[ti % 2].rearrange("p (c t) -> p c t", c=8)
                H = ht[ti % NH].rearrange("p (c t) -> p c t", c=8)
                bH = bht[ti % NH]
                PT_ = pt[ti % 2].rearrange("p (s f) -> p s f", s=4)
                pT = pTs[ti % 2]
                hnb = hnbs[ti % 2]
                for pc in range(2):
                    b = nb()
                    for s_ in range(4):
                        tr(P, PS[b][:, s_ * 128:(s_ + 1) * 128], PT_[:, s_, pc * 128:(pc + 1) * 128], IDENT,
                           [bpt[ti % 2], bC], [bPS[b]])
                    cp(P, "act", pT[:, pc, :], PS[b][:, :], [bPS[b]], [bpT[ti % 2]])
                for c in range(8):
                    b = nb()
                    for kc in range(8):
                        mm(P, PS[b][:, :], WO[:, kc, c * 128:(c + 1) * 128], O[:, kc, :], kc == 0, kc == 7,
                           [bW, bot[ti % 2]], [bPS[b]])
                    tt(P, "dve", H[:, c, :], H[:, c, :], PS[b][:, :], ALU.add, [bH[c], bPS[b]], [bH[c]])
                    cp(P, "act", hnb[:, c, :], H[:, c, :], [bH[c]], [bhnb[ti % 2]])

            def stage_gate(ti):
                t0 = ti * 512
                H = ht[ti % NH].rearrange("p (c t) -> p c t", c=8)
                bH = bht[ti % NH]
                pT = pTs[ti % 2]
                hnb = hnbs[ti % 2]
                for c in range(8):
                    b = nb()
                    for kc in range(8):
                        mm(P, PS[b][:, :], WGt[:, kc, c * 128:(c + 1) * 128], hnb[:, kc, :], kc == 0, kc == 7,
                           [bWG3, bhnb[ti % 2]], [bPS[b]])
                    g, bg = gt[c % 2], bgt[c % 2]
                    act(P, g, PS[b][:, :], AF.Sigmoid, [bPS[b]], [bg])
                    b2 = nb()
                    for pc in range(2):
                        mm(P, PS[b2][:, :], WP[:, pc, c * 128:(c + 1) * 128], pT[:, pc, :], pc == 0, pc == 1,
                           [bWG3, bpT[ti % 2]], [bPS[b2]])
                    tt(P, "dve", g, g, PS[b2][:, :], ALU.mult, [bg, bPS[b2]], [bg])
                    tt(P, "dve", H[:, c, :], H[:, c, :], g, ALU.add, [bH[c], bg], [bH[c]])
                if not last:
                    dma(P, SQ, hT_d[:, :, t0:t0 + 512].rearrange("c p t -> p c t"), H, bH, [])

            def stage_fina(ti):
                H = ht[ti % NH].rearrange("p (c t) -> p c t", c=8)
                bH = bht[ti % NH]
                tt(P, "pool", sq, H, H, ALU.mult, bH, [bsq])
                for c in range(8):
                    mm(P, PS[0][:, :], ONESB, sq[:, c, :], c == 0, c == 7, [bC, bsq], [bPS[0]])
                rstd_from_ss(PS[0][:, :], 512, 128, rst, brst, bPS[0], 1.0 / D)
                for c in range(8):
                    stt(P, H[:, c, :], H[:, c, :], fg[:, c:c + 1], rst, ALU.mult, ALU.mult,
                        [bH[c], bC, brst], [bH[c]])

            def stage_finb(ti):
                t0 = ti * 512
                H = ht[ti % NH].rearrange("p (c t) -> p c t", c=8)
                bH = bht[ti % NH]
                for s_ in range(4):
                    for half in range(2):
                        b = nb()
                        for cc in range(4):
                            c = half * 4 + cc
                            tr(P, PS[b][:, cc * 128:(cc + 1) * 128], H[:, c, s_ * 128:(s_ + 1) * 128], IDENT,
                               [bH[c], bC], [bPS[b]])
                        cp(P, "act" if half == 0 else "dve", yt[:, s_, half * 512:(half + 1) * 512], PS[b][:, :],
                           [bPS[b]], [byt])
                dma(P, SQ, ys[si][t0:t0 + 512, :].rearrange("(s p) f -> p s f", p=128), yt, [byt], [])

            stage_loads(0)
            dma(P, LQ, WO, WOUT_d[l], [], [bW])
            for k in range(NTI + 2):
                if last and 0 <= k - 2 < NTI:
                    stage_fina(k - 2)
                if k < NTI:
                    stage_wo(k)
                if k == 0:
                    dma(P, LQ, WGt, WG_d[l], [], [bWG3])
                    dma(P, LQ, WP, WPLE_d[l], [], [bWG3])
                if last and 0 <= k - 2 < NTI:
                    stage_finb(k - 2)
                if k + 1 < NTI:
                    stage_loads(k + 1)
                if 0 <= k - 1 < NTI:
                    stage_gate(k - 1)
            P.barrier()

        ph = PHASES
        if "w" in ph:
            prep_weights()
        for si in range(NSEQ):
            if "0" in ph:
                phase0(si)
            for l in range(DEPTH if "L2" in ph else 1):
                if "1" in ph:
                    phase1(si, l)
                if "h" in ph:
                    phase_hg(si, l)
                if "l" in ph and not FUSE_LRU:
                    phase_lru(si, l)
                if "d" in ph:
                    phase_da(si, l)
                if "3" in ph:
                    phase3(si, l)
        P.emit()
    return nc, P


def _consts(smax):
    ident = np.eye(128, dtype=np.float32)
    half = 4
    inv = (500000.0 ** (-np.arange(half, dtype=np.float32) * 2.0 / 8)).astype(np.float32)
    pos = np.arange(smax, dtype=np.float32)
    ang = pos[None, :] * inv[:, None]
    cos = np.ones((128, smax), np.float32)
    sin = np.zeros((128, smax), np.float32)
    for p in range(128):
        d = p % 32
        if d < 8:
            cos[p] = np.cos(ang[d % 4])
            sin[p] = np.sin(ang[d % 4])
    j = np.arange(128)[:, None]
    i = np.arange(128)[None, :]
    same = (j // 32) == (i // 32)
    maskf = (same & (j <= i)).astype(np.float32)
    maskb = (same & (j >= i)).astype(np.float32)
    bones = ((j // 64) == (i // 64)).astype(np.float32)
    return {"c_ident": ident, "c_cos": cos, "c_sin": sin, "c_maskf": maskf, "c_maskb": maskb, "c_bones": bones}


_WNAMES = ("norm_g", "w_in", "w_out", "hg_lb", "hg_norm", "lru_conv_w", "lru_conv_b", "lru_wa", "lru_ba", "lru_wx",
           "lru_bx", "lru_lam", "da_lq1", "da_lk1", "da_lq2", "da_lk2", "da_norm", "ple_w", "ple_gate_w",
           "final_norm")


def run_cores(seq_lens, per_core_x, per_core_p, weights, debug=False):
    nc, P = build_program(seq_lens, debug=debug)
    consts = _consts(max(seq_lens))
    in_maps = []
    for c in range(len(per_core_x)):
        m = {}
        for i in range(len(seq_lens)):
            m[f"x{i}"] = np.ascontiguousarray(per_core_x[c][i], dtype=np.float32)
            m[f"p{i}"] = np.ascontiguousarray(per_core_p[c][i], dtype=np.float32)
        for k in _WNAMES:
            m[k] = np.ascontiguousarray(weights[k], dtype=np.float32)
        m.update(consts)
        in_maps.append(m)
    res = run_bass_kernel_spmd(nc, in_maps, core_ids=list(range(len(per_core_x))))
    return res.results


def kernel(**inputs):
    n = 8
    xp = np.asarray(inputs["x_prompt"])
    xsm = np.asarray(inputs["x_sample"])
    pp = np.asarray(inputs["p_prompt"])
    psm = np.asarray(inputs["p_sample"])
    B, S1, _ = xp.shape
    B2, S2, _ = xsm.shape
    bp, bs = B // n, B2 // n
    seq_lens = [S1] * bp + [S2] * bs
    pcx, pcp = [], []
    for c in range(n):
        x_list = [xp[c * bp + i] for i in range(bp)] + [xsm[c * bs + i] for i in range(bs)]
        p_list = [pp[:, c * bp + i] for i in range(bp)] + [psm[:, c * bs + i] for i in range(bs)]
        pcx.append(x_list)
        pcp.append(p_list)
    weights = {k: np.asarray(inputs[k]) for k in _WNAMES}
    results = run_cores(seq_lens, pcx, pcp, weights)
    yp = np.empty((B, S1, D), np.float32)
    ysm = np.empty((B2, S2, D), np.float32)
    for c in range(n):
        r = results[c]
        for i in range(bp):
            yp[c * bp + i] = r[f"y{i}"]
        for i in range(bs):
            ysm[c * bs + i] = r[f"y{bp + i}"]
    return (yp, ysm)
```

```python
import numpy as np
import ml_dtypes
from contextlib import ExitStack
import concourse.bass as bass
import concourse.mybir as mybir
from concourse.bass_utils import run_bass_kernel_spmd

F32 = mybir.dt.float32
BF16 = mybir.dt.bfloat16
AF = mybir.ActivationFunctionType
ALU = mybir.AluOpType

D = 1024
DIN = 3328
PLE = 256
DEPTH = 2
NFM = 26
EPS = 1e-6
ENGS = ("pe", "act", "dve", "pool", "sp")
N_DMA_SEMS = 12
PHASES = ("w", "0", "1", "h", "l", "d", "3", "L2")
FUSE_LRU = False
HG_STOP = 99


class Buf:
    __slots__ = ("name", "w", "rs", "rd")

    def __init__(self, name=""):
        self.name = name
        self.w = None
        self.rs = {}
        self.rd = []


class Ev:
    __slots__ = ("eng", "fn", "deps", "need_inc", "semkey", "semval", "is_dma", "prev_dma")

    def __init__(self, eng, fn, is_dma=False):
        self.eng = eng
        self.fn = fn
        self.deps = ()
        self.need_inc = False
        self.semkey = None
        self.semval = 0
        self.is_dma = is_dma
        self.prev_dma = None


class Prog:
    def __init__(self, nc):
        self.nc = nc
        self.q = {e: [] for e in ENGS}
        self.dma_rr = {e: 0 for e in ENGS}
        self.dma_cnt = {}
        self.dma_last = {}
        self.all_bufs = []
        self.last_ev = {e: None for e in ENGS}

    def buf(self, name=""):
        b = Buf(name)
        self.all_bufs.append(b)
        return b

    def bufs(self, n, name=""):
        return [self.buf(f"{name}{i}") for i in range(n)]

    def _record(self, ev, reads, writes):
        deps = set()
        for b in reads:
            if b.w is not None:
                deps.add(b.w)
        for b in writes:
            if b.w is not None:
                deps.add(b.w)
            deps.update(b.rs.values())
            deps.update(b.rd)
        deps.discard(ev)
        ev.deps = tuple(deps)
        for b in reads:
            if ev.is_dma:
                b.rd.append(ev)
            else:
                b.rs[ev.eng] = ev
        for b in writes:
            b.w = ev
            b.rs = {}
            b.rd = []
        self.q[ev.eng].append(ev)
        if not ev.is_dma:
            self.last_ev[ev.eng] = ev

    def op(self, eng, fn, reads=(), writes=()):
        ev = Ev(eng, fn)
        self._record(ev, reads, writes)
        return ev

    def dma(self, eng, fn, reads=(), writes=()):
        ev = Ev(eng, fn, is_dma=True)
        k = self.dma_rr[eng]
        self.dma_rr[eng] = (k + 1) % N_DMA_SEMS
        key = (eng, k)
        ev.semkey = key
        self.dma_cnt[key] = self.dma_cnt.get(key, 0) + 16
        ev.semval = self.dma_cnt[key]
        ev.prev_dma = self.dma_last.get(key)
        self.dma_last[key] = ev
        self._record(ev, reads, writes)
        return ev

    def barrier(self):
        evs = [e for e in self.last_ev.values() if e is not None]
        evs += list(self.dma_last.values())
        for eng in ENGS:
            ev = Ev(eng, None)
            ev.deps = tuple(evs)
            self.q[eng].append(ev)
        for b in self.all_bufs:
            b.w = None
            b.rs = {}
            b.rd = []

    def emit(self):
        nc = self.nc
        for e in ENGS:
            for ev in self.q[e]:
                for d in ev.deps:
                    if d.is_dma or d.fn is None:
                        continue
                    if d.eng == "pe" and ev.eng == "pe" and not ev.is_dma and ev.fn is not None:
                        continue
                    d.need_inc = True
        tail = []
        for e in ENGS:
            evs = [x for x in self.q[e] if not x.is_dma and x.fn is not None]
            if evs:
                evs[-1].need_inc = True
                tail.append(evs[-1])
        tail += list(self.dma_last.values())
        counts = {}
        for e in ENGS:
            c = 0
            for ev in self.q[e]:
                if ev.is_dma or ev.fn is None:
                    continue
                if ev.need_inc:
                    c += 1
                    ev.semkey = e
                    ev.semval = c
            counts[e] = (c, len(self.q[e]))
        self.counts = counts
        with ExitStack() as es:
            sems = {}
            for e in ENGS:
                sems[e] = es.enter_context(nc.semaphore(f"s_{e}"))
            for key in sorted(self.dma_cnt.keys()):
                sems[key] = es.enter_context(nc.semaphore(f"d_{key[0]}{key[1]}"))
            block = es.enter_context(nc.Block())
            engmap = {"pe": "tensor", "act": "scalar", "dve": "vector", "pool": "gpsimd", "sp": "sync"}

            def run_queue(e, engine):
                seen = {}

                def wait_for(d):
                    if d.semkey is None or d.semval == 0:
                        return
                    if seen.get(d.semkey, 0) >= d.semval:
                        return
                    engine.wait_ge(sems[d.semkey], d.semval)
                    seen[d.semkey] = d.semval

                for ev in self.q[e]:
                    for d in ev.deps:
                        if (not d.is_dma) and d.eng == "pe" and e == "pe" and not ev.is_dma and ev.fn is not None:
                            continue
                        wait_for(d)
                    if ev.is_dma and ev.prev_dma is not None:
                        wait_for(ev.prev_dma)
                    if ev.fn is None:
                        continue
                    ins = ev.fn(engine)
                    if ev.is_dma:
                        ins.then_inc(sems[ev.semkey], 16)
                    elif ev.need_inc:
                        ins.then_inc(sems[ev.semkey], 1)
                if e == "sp":
                    for d in tail:
                        wait_for(d)

            for e in ENGS:
                dec = getattr(block, engmap[e])

                def mk(e=e):
                    def f(engine):
                        run_queue(e, engine)
                    return f
                dec(mk())


def mm(P, out, lhsT, rhs, start, stop, reads, writes, tp=None):
    if tp is None:
        return P.op("pe", lambda e: e.matmul(out, lhsT=lhsT, rhs=rhs, start=start, stop=stop), reads, writes)
    return P.op("pe", lambda e: e.matmul(out, lhsT=lhsT, rhs=rhs, start=start, stop=stop, tile_position=tp),
                reads, writes)


def tr(P, out, in_, ident, reads, writes):
    return P.op("pe", lambda e: e.transpose(out, in_, ident), reads, writes)


def act(P, out, in_, func, reads, writes, bias=None, scale=None):
    kw = {}
    if bias is not None:
        kw["bias"] = bias
    if scale is not None:
        kw["scale"] = scale
    return P.op("act", lambda e: e.activation(out=out, in_=in_, func=func, **kw), reads, writes)


def tt(P, eng, out, in0, in1, op, reads, writes):
    return P.op(eng, lambda e: e.tensor_tensor(out=out, in0=in0, in1=in1, op=op), reads, writes)


def ts(P, eng, out, in0, s1, s2, op0, op1, reads, writes):
    if op1 is None:
        return P.op(eng, lambda e: e.tensor_scalar(out=out, in0=in0, scalar1=s1, scalar2=None, op0=op0),
                    reads, writes)
    return P.op(eng, lambda e: e.tensor_scalar(out=out, in0=in0, scalar1=s1, scalar2=s2, op0=op0, op1=op1),
                reads, writes)


def stt(P, out, in0, scalar, in1, op0, op1, reads, writes):
    return P.op("dve", lambda e: e.scalar_tensor_tensor(out=out, in0=in0, scalar=scalar, in1=in1, op0=op0, op1=op1),
                reads, writes)


def cp(P, eng, out, in_, reads, writes):
    if eng == "act":
        return P.op("act", lambda e: e.copy(out=out, in_=in_), reads, writes)
    return P.op(eng, lambda e: e.tensor_copy(out=out, in_=in_), reads, writes)


def mset(P, eng, ap, val, writes):
    return P.op(eng, lambda e: e.memset(ap, val), (), writes)


def scan(P, out, d0, d1, reads, writes):
    return P.op("dve", lambda e: e.tensor_tensor_scan(out=out, data0=d0, data1=d1, initial=0.0,
                                                      op0=ALU.mult, op1=ALU.add), reads, writes)


def dma(P, q, out, in_, reads, writes):
    return P.dma(q, lambda e: e.dma_start(out=out, in_=in_), reads, writes)


class Arena:
    def __init__(self, t, nwords):
        self.t = t
        self.n = nwords
        self.off = 0

    def reset(self):
        self.off = 0

    def alloc(self, nelem, dtype):
        words = nelem if dtype == F32 else (nelem + 1) // 2
        a = self.t[:, self.off:self.off + words]
        self.off += words
        assert self.off <= self.n, f"arena overflow {self.off} > {self.n}"
        return a if dtype == F32 else a.bitcast(BF16)


LQ = "sp"
SQ = "pool"


def build_program(seq_lens, debug=False):
    nc = bass.Bass("TRN2", target_bir_lowering=False)
    SMAX = max(seq_lens)
    NSEQ = len(seq_lens)
    dk = "ExternalOutput" if debug else "Internal"

    def din(name, shape, dt=F32):
        return nc.dram_tensor(name, list(shape), dt, kind="ExternalInput").ap()

    def dscr(name, shape, dt, kind=None):
        return nc.dram_tensor(name, list(shape), dt, kind=kind or "Internal").ap()

    xs = [din(f"x{i}", [S, D]) for i, S in enumerate(seq_lens)]
    ps_in = [din(f"p{i}", [DEPTH, S, PLE]) for i, S in enumerate(seq_lens)]
    ys = [nc.dram_tensor(f"y{i}", [S, D], F32, kind="ExternalOutput").ap() for i, S in enumerate(seq_lens)]
    W = {}
    for name, shape in [("norm_g", [DEPTH, D]), ("w_in", [DEPTH, D, DIN]), ("w_out", [DEPTH, D, D]),
                        ("hg_lb", [DEPTH, 2, 256]), ("hg_norm", [DEPTH, 64]), ("lru_conv_w", [DEPTH, 4, 512]),
                        ("lru_conv_b", [DEPTH, 512]), ("lru_wa", [DEPTH, 2, 8, 64, 64]), ("lru_ba", [DEPTH, 2, 512]),
                        ("lru_wx", [DEPTH, 2, 8, 64, 64]), ("lru_bx", [DEPTH, 2, 512]), ("lru_lam", [DEPTH, 2, 512]),
                        ("da_lq1", [DEPTH, 32]), ("da_lk1", [DEPTH, 32]), ("da_lq2", [DEPTH, 32]),
                        ("da_lk2", [DEPTH, 32]), ("da_norm", [DEPTH, 64]), ("ple_w", [DEPTH, PLE, D]),
                        ("ple_gate_w", [DEPTH, D, D]), ("final_norm", [D])]:
        W[name] = din(name, shape)
    c_ident = din("c_ident", [128, 128])
    c_cos = din("c_cos", [128, SMAX])
    c_sin = din("c_sin", [128, SMAX])
    c_maskf = din("c_maskf", [128, 128])
    c_maskb = din("c_maskb", [128, 128])
    c_bones = din("c_bones", [128, 128])

    WIN_d = dscr("WIN_d", [DEPTH, 128, 8, NFM * 128], BF16)
    WV_d = dscr("WV_d", [DEPTH, 128, 8, 512], BF16)
    WOUT_d = dscr("WOUT_d", [DEPTH, 128, 8, D], BF16)
    WG_d = dscr("WG_d", [DEPTH, 128, 8, D], BF16)
    WPLE_d = dscr("WPLE_d", [DEPTH, 128, 2, D], BF16)
    hT_d = dscr("hT_d", [8, 128, SMAX], F32, dk)
    qh_d = dscr("qh_d", [256, SMAX], BF16, dk)
    sgf_d = dscr("sgf_d", [256, SMAX], F32, dk)
    sgb_d = dscr("sgb_d", [256, SMAX], F32, dk)
    gh_d = dscr("gh_d", [256, SMAX], BF16, dk)
    lx_d = dscr("lx_d", [512, SMAX], F32, dk)
    gl_d = dscr("gl_d", [512, SMAX], BF16, dk)
    dq_d = dscr("dq_d", [256, SMAX], BF16, dk)
    dk_d = dscr("dk_d", [256, SMAX], BF16, dk)
    gd_d = dscr("gd_d", [256, SMAX], BF16, dk)
    vh_d = dscr("vh_d", [SMAX, 256], BF16, dk)
    vd_d = dscr("vd_d", [SMAX, 256], BF16, dk)
    oT_d = dscr("oT_d", [D, SMAX], BF16, dk)

    P = Prog(nc)
    AR_WORDS = 46 * 1024
    CA_WORDS = 4608
    with ExitStack() as es:
        arena_t = es.enter_context(nc.sbuf_tensor("arena", [128, AR_WORDS], F32))
        cst_t = es.enter_context(nc.sbuf_tensor("cst", [128, CA_WORDS], F32))
        PS = [es.enter_context(nc.psum_tensor(f"ps{i}", [128, 512], F32)) for i in range(8)]
        bPS = P.bufs(8, "ps")
        AR = Arena(arena_t, AR_WORDS)
        CA = Arena(cst_t, CA_WORDS)
        bC = P.buf("consts")
        IDENT = CA.alloc(128, F32)
        IDENTB = CA.alloc(128, BF16)
        ONESB = CA.alloc(128, BF16)
        BONES = CA.alloc(128, BF16)
        MASKF = CA.alloc(128, BF16)
        MASKB = CA.alloc(128, BF16)
        NEGHALF = CA.alloc(512, F32)
        HALF = CA.alloc(512, F32)
        CTMP = CA.alloc(128, F32)
        EPSB = CA.alloc(1, F32)
        HM = CA.alloc(4, F32)

        dma(P, LQ, IDENT, c_ident, [], [bC])
        cp(P, "dve", IDENTB, IDENT, [bC], [bC])
        mset(P, "dve", ONESB, 1.0, [bC])
        mset(P, "dve", NEGHALF, -0.5, [bC])
        mset(P, "dve", HALF, 0.5, [bC])
        mset(P, "dve", EPSB, EPS, [bC])
        mset(P, "dve", HM, 0.0, [bC])
        for j in range(4):
            mset(P, "dve", HM[32 * j:32 * j + 32, j:j + 1], 1.0, [bC])
        for src, dst in ((c_bones, BONES), (c_maskf, MASKF), (c_maskb, MASKB)):
            dma(P, LQ, CTMP, src, [bC], [bC])
            cp(P, "dve", dst, CTMP, [bC], [bC])

        SB = []
        bCw = P.buf("cw")

        def nb_():
            b = P.buf("sm")
            SB.append(b)
            return [b]

        def load_cols(dst, src1d):
            C = dst.shape[1]
            for c in range(C):
                dma(P, LQ, dst[:, c:c + 1], src1d[c * 128:(c + 1) * 128].rearrange("(p o) -> p o", o=1), [], nb_())

        PR = []
        fg = CA.alloc(8, F32)
        load_cols(fg, W["final_norm"])
        AR.reset()
        bdfs = [AR.alloc(16 * 128, F32).rearrange("p (g d c e) -> p g d c e", g=2, d=2, c=4) for _ in range(DEPTH)]
        lqks = [AR.alloc(128, F32).rearrange("p (a d) -> p a d", a=4) for _ in range(DEPTH)]
        bBDFm = P.bufs(DEPTH, "bdfm")
        lbr = AR.alloc(8, F32).rearrange("p (l d c) -> p l d c", l=DEPTH, d=2)
        for ll in range(DEPTH):
            for d in range(2):
                load_cols(lbr[:, ll, d, :], W["hg_lb"][ll, d])
        for l in range(DEPTH):
            pr = {}
            bdf, lqk = bdfs[l], lqks[l]
            pr["gcol"] = CA.alloc(8, F32)
            load_cols(pr["gcol"], W["norm_g"][l])
            pr["gneg"] = CA.alloc(8, F32)
            ts(P, "dve", pr["gneg"], pr["gcol"], -1.0, None, ALU.mult, None, SB + [bCw], [bCw])
            cw = CA.alloc(16, F32).rearrange("p (c j) -> p c j", c=4)
            for j in range(4):
                for c in range(4):
                    dma(P, LQ, cw[:, c, j:j + 1],
                        W["lru_conv_w"][l, j, c * 128:(c + 1) * 128].rearrange("(p o) -> p o", o=1), [], nb_())
            pr["cw"] = cw
            pr["cb"] = CA.alloc(4, F32)
            load_cols(pr["cb"], W["lru_conv_b"][l])
            for nm, key in (("lru_ba", "bab"), ("lru_bx", "bxb"), ("lru_lam", "coef")):
                t = CA.alloc(8, F32).rearrange("p (d c) -> p d c", d=2)
                for d in range(2):
                    load_cols(t[:, d, :], W[nm][l, d])
                pr[key] = t
            cf = pr["coef"].rearrange("p d c -> p (d c)")
            act(P, cf, cf, AF.Exp, SB + [bCw], [bCw], scale=-1.0)
            act(P, cf, cf, AF.Ln, SB + [bCw], [bCw], bias=1.0)
            ts(P, "dve", cf, cf, -8.0, None, ALU.mult, None, SB + [bCw], [bCw])
            for key, src in (("coef2", "coef"), ("nbab", "bab"), ("nbxb", "bxb")):
                t = CA.alloc(8, F32).rearrange("p (d c) -> p d c", d=2)
                ts(P, "dve", t.rearrange("p d c -> p (d c)"), pr[src].rearrange("p d c -> p (d c)"),
                   2.0 if key == "coef2" else -1.0, None, ALU.mult, None, SB + [bCw], [bCw])
                pr[key] = t
            lbd = CA.alloc(4, F32).rearrange("p (d c) -> p d c", d=2)
            c1 = CA.alloc(4, F32).rearrange("p (d c) -> p d c", d=2)
            c1n = CA.alloc(4, F32).rearrange("p (d c) -> p d c", d=2)
            if l == 0:
                mset(P, "dve", lbd, 0.0, [bCw])
            else:
                tt(P, "dve", lbd, lbr[:, 0], lbr[:, 1], ALU.subtract, SB + [bCw], [bCw])
                act(P, lbd, lbd, AF.Exp, SB + [bCw], [bCw])
                ts(P, "dve", lbd, lbd, 1.0, None, ALU.add, None, SB + [bCw], [bCw])
                P.op("dve", (lambda a: (lambda e: e.reciprocal(out=a, in_=a)))(lbd), SB + [bCw], [bCw])
            ts(P, "dve", c1, lbd, -1.0, 1.0, ALU.mult, ALU.add, SB + [bCw], [bCw])
            ts(P, "dve", c1n, c1, -1.0, None, ALU.mult, None, SB + [bCw], [bCw])
            pr["lbd"], pr["c1"], pr["c1n"] = lbd, c1, c1n
            gn = CA.alloc(1, F32)
            dma(P, LQ, gn[0:64, :], W["hg_norm"][l].rearrange("(p o) -> p o", o=1), [], nb_())
            dma(P, LQ, gn[64:128, :], W["hg_norm"][l].rearrange("(p o) -> p o", o=1), [], nb_())
            pr["gn"] = gn
            lam_init = 0.8 - 0.6 * float(np.exp(-0.3 * l))
            lsm = CA.alloc(4, F32)
            for a, nm in enumerate(("da_lq1", "da_lk1", "da_lq2", "da_lk2")):
                dma(P, LQ, lqk[:, a, :], W[nm][l:l + 1, :].partition_broadcast(128), [], nb_())
            tt(P, "dve", lqk[:, 0, :], lqk[:, 0, :], lqk[:, 1, :], ALU.mult, SB + [bCw], [bCw])
            tt(P, "dve", lqk[:, 2, :], lqk[:, 2, :], lqk[:, 3, :], ALU.mult, SB + [bCw], [bCw])
            P.op("dve", (lambda o, i: (lambda e: e.reduce_sum(out=o, in_=i, axis=mybir.AxisListType.X)))(
                lsm[:, 0:1], lqk[:, 0, :]), SB + [bCw], [bCw])
            P.op("dve", (lambda o, i: (lambda e: e.reduce_sum(out=o, in_=i, axis=mybir.AxisListType.X)))(
                lsm[:, 1:2], lqk[:, 2, :]), SB + [bCw], [bCw])
            act(P, lsm[:, 0:2], lsm[:, 0:2], AF.Exp, SB + [bCw], [bCw])
            tt(P, "dve", lsm[:, 2:3], lsm[:, 1:2], lsm[:, 0:1], ALU.subtract, SB + [bCw], [bCw])
            ts(P, "dve", lsm[:, 3:4], lsm[:, 2:3], -lam_init, None, ALU.add, None, SB + [bCw], [bCw])
            pr["neglam"] = lsm[0:64, 3:4]
            dn = CA.alloc(1, F32)
            dma(P, LQ, dn[0:64, :], W["da_norm"][l].rearrange("(p o) -> p o", o=1), [], nb_())
            ts(P, "dve", dn[0:64, :], dn[0:64, :], 1.0 - lam_init, None, ALU.mult, None, SB + [bCw], [bCw])
            pr["dn"] = dn
            bdb = CA.alloc(16 * 128, BF16).rearrange("p (g d c e) -> p g d c e", g=2, d=2, c=4)
            mset(P, "pool", bdf, 0.0, [bBDFm[l]])
            for g, wname in enumerate(("lru_wa", "lru_wx")):
                for d in range(2):
                    for b in range(2):
                        src = W[wname][l, d].rearrange("(c b) ci e -> b ci c e", b=2)[b]
                        dma(P, LQ, bdf[64 * b:64 * b + 64, g, d, :, 64 * b:64 * b + 64], src, [bBDFm[l]], nb_())
            cp(P, "pool", bdb, bdf, SB + [bCw], [bCw])
            pr["bdb"] = bdb
            PR.append(pr)
        P.barrier()

        def prep_weights():
            AR.reset()
            stg = [AR.alloc(DIN, F32) for _ in range(2)]
            bstg = P.bufs(2, "wstg")
            ob = [AR.alloc(NFM * 128 + 512, BF16) for _ in range(2)]
            bob = P.bufs(2, "wob")
            bg = bC
            it = 0
            for l in range(DEPTH):
                for kc in range(8):
                    s = stg[it % 2]
                    o = ob[it % 2]
                    bs, bo = bstg[it % 2], bob[it % 2]
                    eng = "dve"
                    it += 1
                    dma(P, LQ, s, W["w_in"][l, kc * 128:(kc + 1) * 128, :], [], [bs])
                    g = PR[l]["gcol"][:, kc:kc + 1]
                    for (s0, s1, d0) in ((0, 768, 0), (1024, 1280, 768), (1280, 2304, 1024), (2304, 2816, 2048),
                                         (3072, 3328, 2560)):
                        ts(P, eng, o[:, d0:d0 + (s1 - s0)], s[:, s0:s1], g, None, ALU.mult, None, [bs, bg], [bo])
                    mset(P, eng, o[:, 2816:3328], 0.0, [bo])
                    sv = s[:, 2304:2816].rearrange("p (h d) -> p h d", d=32)
                    dv = o[:, 2816:3328].rearrange("p (h d) -> p h d", d=32)
                    ts(P, eng, dv[:, :, 0:4], sv[:, :, 4:8], PR[l]["gneg"][:, kc:kc + 1], None, ALU.mult, None, [bs, bg], [bo])
                    ts(P, eng, dv[:, :, 4:8], sv[:, :, 0:4], g, None, ALU.mult, None, [bs, bg], [bo])
                    ts(P, eng, o[:, 3328:3584], s[:, 768:1024], g, None, ALU.mult, None, [bs, bg], [bo])
                    ts(P, eng, o[:, 3584:3840], s[:, 2816:3072], g, None, ALU.mult, None, [bs, bg], [bo])
                    dma(P, SQ, WIN_d[l, :, kc, :], o[:, 0:3328], [bo], [])
                    dma(P, SQ, WV_d[l, :, kc, :], o[:, 3328:3840], [bo], [])
            for l in range(DEPTH):
                for (src, dst, nk) in ((W["w_out"], WOUT_d, 8), (W["ple_gate_w"], WG_d, 8), (W["ple_w"], WPLE_d, 2)):
                    for kc in range(nk):
                        s = stg[it % 2]
                        o = ob[it % 2]
                        bs, bo = bstg[it % 2], bob[it % 2]
                        eng = "dve" if it % 2 == 0 else "act"
                        it += 1
                        dma(P, LQ, s[:, 0:D], src[l, kc * 128:(kc + 1) * 128, :], [], [bs])
                        cp(P, eng, o[:, 0:D], s[:, 0:D], [bs], [bo])
                        dma(P, SQ, dst[l, :, kc, :], o[:, 0:D], [bo], [])
            P.barrier()

        def rstd_from_ss(ss_ps, n, npart, rst, brst, bss, inv_n):
            act(P, rst, ss_ps, AF.Ln, [bss], [brst], bias=EPSB[0:npart, :], scale=inv_n)
            act(P, rst, rst, AF.Exp, [brst], [brst], scale=-0.5)

        def phase0(si):
            S = seq_lens[si]
            AR.reset()
            xt = [AR.alloc(4 * D, F32) for _ in range(2)]
            bx = P.bufs(2, "xt")
            ht = [AR.alloc(8 * 512, F32) for _ in range(2)]
            bh = P.bufs(2, "ht")
            for ti in range(S // 512):
                X = xt[ti % 2].rearrange("p (s f) -> p s f", s=4)
                H = ht[ti % 2].rearrange("p (c t) -> p c t", c=8)
                dma(P, LQ, X, xs[si][ti * 512:(ti + 1) * 512, :].rearrange("(s p) f -> p s f", p=128), [],
                    [bx[ti % 2]])
                for c in range(8):
                    bank = c
                    for s in range(4):
                        tr(P, PS[bank][:, s * 128:(s + 1) * 128], X[:, s, c * 128:(c + 1) * 128], IDENT,
                           [bx[ti % 2], bC], [bPS[bank]])
                    cp(P, "dve" if c % 2 == 0 else "act", H[:, c, :], PS[bank][:, :], [bPS[bank]], [bh[ti % 2]])
                dma(P, SQ, hT_d[:, :, ti * 512:(ti + 1) * 512].rearrange("c p t -> p c t"), H, [bh[ti % 2]], [])
            P.barrier()

        def phase1(si, l):
            S = seq_lens[si]
            AR.reset()
            WIN = AR.alloc(8 * NFM * 128, BF16).rearrange("p (k f) -> p k f", k=8)
            WV = AR.alloc(8 * 512, BF16).rearrange("p (k f) -> p k f", k=8)
            bWg = P.bufs(NFM, "win")
            bW = P.buf("wv")

            def load_w1():
                for (f0, f1) in ((0, 2), (2, 6), (6, 12), (12, 18), (18, 26)):
                    dma(P, LQ, WIN[:, :, f0 * 128:f1 * 128], WIN_d[l, :, :, f0 * 128:f1 * 128], [], bWg[f0:f1])
                dma(P, LQ, WV, WV_d[l], [], [bW])
            ht = [AR.alloc(8 * 512, F32) for _ in range(2)]
            bh = P.bufs(2, "ht")
            sq = AR.alloc(8 * 512, BF16)
            bsq = P.buf("sq")
            hn = [AR.alloc(8 * 512, BF16) for _ in range(2)]
            bhn = P.bufs(2, "hn")
            rst = AR.alloc(512, F32)
            brst = P.buf("rst")
            cs = [AR.alloc(512, F32) for _ in range(2)]
            sn = [AR.alloc(512, F32) for _ in range(2)]
            bcs = P.bufs(2, "cs")
            NST = 6
            st = [AR.alloc(512, F32) for _ in range(NST)]
            bst = P.bufs(NST, "st")
            r1 = AR.alloc(512, F32)
            r2 = AR.alloc(512, F32)
            br = P.buf("r")
            sti = [0]
            bank_i = [0]

            def next_bank():
                b = 1 + bank_i[0] % 7
                bank_i[0] += 1
                return b

            def next_st():
                k = sti[0] % NST
                sti[0] += 1
                return st[k], bst[k]

            NTI = S // 512

            def pre(ti):
                t0 = ti * 512
                H = ht[ti % 2].rearrange("p (c t) -> p c t", c=8)
                HN = hn[ti % 2].rearrange("p (c t) -> p c t", c=8)
                SQv = sq.rearrange("p (c t) -> p c t", c=8)
                dma(P, LQ, H, hT_d[:, :, t0:t0 + 512].rearrange("c p t -> p c t"), [], [bh[ti % 2]])
                dma(P, LQ, cs[ti % 2], c_cos[:, t0:t0 + 512], [], [bcs[ti % 2]])
                dma(P, LQ, sn[ti % 2], c_sin[:, t0:t0 + 512], [], [bcs[ti % 2]])
                tt(P, "pool", SQv, H, H, ALU.mult, [bh[ti % 2]], [bsq])

            def pre_b(ti):
                H = ht[ti % 2].rearrange("p (c t) -> p c t", c=8)
                HN = hn[ti % 2].rearrange("p (c t) -> p c t", c=8)
                SQv = sq.rearrange("p (c t) -> p c t", c=8)
                for c in range(8):
                    mm(P, PS[0][:, :], ONESB, SQv[:, c, :], c == 0, c == 7, [bC, bsq], [bPS[0]])
                rstd_from_ss(PS[0][:, :], 512, 128, rst, brst, bPS[0], 1.0 / D)
                rb = bass.AP(rst.tensor, rst.offset, [list(rst.ap[0]), [0, 8], [1, 512]])
                tt(P, "dve", HN, H, rb, ALU.mult, [bh[ti % 2], brst], [bhn[ti % 2]])

            pre(0)
            pre_b(0)
            load_w1()
            for ti in range(NTI):
                t0 = ti * 512
                HN = hn[ti % 2].rearrange("p (c t) -> p c t", c=8)
                for fc in list(range(0, 22)):
                    if fc == 2 and ti + 1 < NTI:
                        pre(ti + 1)
                    if fc == 9 and ti + 1 < NTI:
                        pre_b(ti + 1)
                    b = next_bank()
                    for kc in range(8):
                        mm(P, PS[b][:, :], WIN[:, kc, fc * 128:(fc + 1) * 128], HN[:, kc, :], kc == 0, kc == 7,
                           [bWg[fc], bhn[ti % 2]], [bPS[b]])
                    if fc in (0, 1):
                        o, bo = next_st()
                        ob = o.bitcast(BF16)[:, 0:512]
                        act(P, ob, PS[b][:, :], AF.Silu, [bPS[b]], [bo])
                        dma(P, SQ, qh_d[fc * 128:(fc + 1) * 128, t0:t0 + 512], ob, [bo], [])
                    elif fc in (2, 3, 4, 5):
                        o, bo = next_st()
                        act(P, o, PS[b][:, :], AF.Sigmoid, [bPS[b]], [bo])
                        dst = sgf_d if fc < 4 else sgb_d
                        r0 = (fc % 2) * 128
                        dma(P, SQ, dst[r0:r0 + 128, t0:t0 + 512], o, [bo], [])
                    elif fc in (6, 7) or 12 <= fc <= 15 or fc in (20, 21):
                        o, bo = next_st()
                        ob = o.bitcast(BF16)[:, 0:512]
                        act(P, ob, PS[b][:, :], AF.Silu, [bPS[b]], [bo])
                        if fc in (6, 7):
                            dst, r0 = gh_d, (fc - 6) * 128
                        elif fc in (20, 21):
                            dst, r0 = gd_d, (fc - 20) * 128
                        else:
                            dst, r0 = gl_d, (fc - 12) * 128
                        dma(P, SQ, dst[r0:r0 + 128, t0:t0 + 512], ob, [bo], [])
                    elif 8 <= fc <= 11:
                        o, bo = next_st()
                        cp(P, "dve", o, PS[b][:, :], [bPS[b]], [bo])
                        dma(P, SQ, lx_d[(fc - 8) * 128:(fc - 7) * 128, t0:t0 + 512], o, [bo], [])
                    else:
                        b2 = next_bank()
                        for kc in range(8):
                            mm(P, PS[b2][:, :], WIN[:, kc, (fc + 6) * 128:(fc + 7) * 128], HN[:, kc, :], kc == 0,
                               kc == 7, [bWg[fc + 6], bhn[ti % 2]], [bPS[b2]])
                        o, bo = next_st()
                        ob = o.bitcast(BF16)[:, 0:512]
                        tt(P, "dve", r1, PS[b][:, :], cs[ti % 2], ALU.mult, [bPS[b], bcs[ti % 2]], [br])
                        tt(P, "dve", r2, PS[b2][:, :], sn[ti % 2], ALU.mult, [bPS[b2], bcs[ti % 2]], [br])
                        tt(P, "pool", ob, r1, r2, ALU.add, [br], [bo])
                        dst = dq_d if fc < 18 else dk_d
                        r0 = (fc % 2) * 128
                        dma(P, SQ, dst[r0:r0 + 128, t0:t0 + 512], ob, [bo], [])
                for s in range(4):
                    b = next_bank()
                    for kc in range(8):
                        mm(P, PS[b][:, :], HN[:, kc, s * 128:(s + 1) * 128], WV[:, kc, :], kc == 0, kc == 7,
                           [bW, bhn[ti % 2]], [bPS[b]])
                    o, bo = next_st()
                    ob = o.bitcast(BF16)[:, 0:512]
                    cp(P, "act", ob, PS[b][:, :], [bPS[b]], [bo])
                    dma(P, SQ, vh_d[t0 + s * 128:t0 + (s + 1) * 128, :], ob[:, 0:256], [bo], [])
                    dma(P, SQ, vd_d[t0 + s * 128:t0 + (s + 1) * 128, :], ob[:, 256:512], [bo], [])
            P.barrier()

        def phase_lru(si, l):
            S = seq_lens[si]
            TT = min(1024, S // 4)
            NTT = S // TT
            GW = min(512, TT)
            AR.reset()
            pr = PR[l]
            cw, cb, bab, bxb, coef, bdb = pr["cw"], pr["cb"], pr["bab"], pr["bxb"], pr["coef"], pr["bdb"]
            LX = [AR.alloc(S + 4, F32) for _ in range(2)]
            GL = [AR.alloc(S, BF16) for _ in range(2)]
            OB = [AR.alloc(S, BF16) for _ in range(2)]
            bLX, bGL, bOB = P.bufs(2, "lx"), P.bufs(2, "gl"), P.bufs(2, "ob")
            U32 = AR.alloc(S, F32)
            UB = AR.alloc(S, BF16)
            HS = AR.alloc(S, F32)
            bU, bUB, bHS = P.bufs(NTT, "u32"), P.bufs(NTT, "ub"), P.bufs(NTT, "hs")
            A_ = [AR.alloc(TT, F32) for _ in range(3)]
            I_ = [AR.alloc(TT, F32) for _ in range(3)]
            T_ = [AR.alloc(TT, F32) for _ in range(2)]
            H2 = [AR.alloc(TT, F32) for _ in range(2)]
            bA, bI, bT, bH2 = P.bufs(3, "la"), P.bufs(3, "li"), P.bufs(2, "lt"), P.bufs(2, "lh")
            bank_i = [0]

            def nbank():
                b = bank_i[0] % 8
                bank_i[0] += 1
                return b

            descs = []
            for c in range(4):
                passes = [(0, True), (1, False)] if c % 2 == 0 else [(1, False), (0, True)]
                for pi, (d, asc) in enumerate(passes):
                    tiles = list(range(NTT)) if asc else list(range(NTT - 1, -1, -1))
                    for idx, tt_ in enumerate(tiles):
                        descs.append(dict(c=c, pi=pi, d=d, tt=tt_, first=(idx == 0), lastt=(idx == NTT - 1),
                                          k3=len(descs) % 3, k=len(descs) % 2, banks=[]))
            prevd = [None]

            def stage_a1(ds):
                c, pi, d, tt_ = ds["c"], ds["pi"], ds["d"], ds["tt"]
                lx, blx = LX[c % 2], bLX[c % 2]
                gl, bgl = GL[c % 2], bGL[c % 2]
                a0 = tt_ * TT
                u32 = U32[:, a0:a0 + TT]
                ub = UB[:, a0:a0 + TT]
                if ds["first"] and pi == 0:
                    mset(P, "pool", lx[:, 0:2], 0.0, [blx])
                    mset(P, "pool", lx[:, S + 2:S + 4], 0.0, [blx])
                    dma(P, LQ, lx[:, 2:S + 2], lx_d[c * 128:(c + 1) * 128, 0:S], [], [blx])
                    dma(P, LQ, gl, gl_d[c * 128:(c + 1) * 128, 0:S], [], [bgl])
                if pi == 0:
                    ts(P, "dve", u32, lx[:, a0:a0 + TT], cw[:, c, 0:1], cb[:, c:c + 1], ALU.mult, ALU.add,
                       [blx, bC], [bU[tt_]])
                    for j in range(1, 4):
                        stt(P, u32, lx[:, a0 + j:a0 + j + TT], cw[:, c, j:j + 1], u32, ALU.mult, ALU.add,
                            [blx, bC, bU[tt_]], [bU[tt_]])
                    cp(P, "act", ub, u32, [bU[tt_]], [bUB[tt_]])
                for t0 in range(0, TT, GW):
                    b1, b2 = nbank(), nbank()
                    ds["banks"].append((t0, b1, b2))
                    mm(P, PS[b1][:, 0:GW], bdb[:, 0, d, c, :], ub[:, t0:t0 + GW], True, True, [bC, bUB[tt_]],
                       [bPS[b1]])
                    mm(P, PS[b2][:, 0:GW], bdb[:, 1, d, c, :], ub[:, t0:t0 + GW], True, True, [bC, bUB[tt_]],
                       [bPS[b2]])

            def stage_a2(ds):
                c, d, k3 = ds["c"], ds["d"], ds["k3"]
                for (t0, b1, b2) in ds["banks"]:
                    act(P, A_[k3][:, t0:t0 + GW], PS[b1][:, 0:GW], AF.Sigmoid, [bPS[b1], bC], [bA[k3]],
                        bias=bab[:, d, c:c + 1])
                    act(P, I_[k3][:, t0:t0 + GW], PS[b2][:, 0:GW], AF.Sigmoid, [bPS[b2], bC], [bI[k3]],
                        bias=bxb[:, d, c:c + 1])

            def stage_b_act(ds):
                c, d, k3, k = ds["c"], ds["d"], ds["k3"], ds["k"]
                act(P, A_[k3], A_[k3], AF.Exp, [bA[k3], bC], [bA[k3]], scale=coef[:, d, c:c + 1])
                act(P, T_[k], A_[k3], AF.Square, [bA[k3]], [bT[k]])
                act(P, T_[k], T_[k], AF.Sqrt, [bT[k]], [bT[k]], scale=-1.0, bias=1.0)

            def stage_b_rest(ds):
                c, pi, d, tt_, k3, k = ds["c"], ds["pi"], ds["d"], ds["tt"], ds["k3"], ds["k"]
                gl, bgl = GL[c % 2], bGL[c % 2]
                ob, bob = OB[c % 2], bOB[c % 2]
                a0 = tt_ * TT
                u32 = U32[:, a0:a0 + TT]
                tt(P, "dve", I_[k3], I_[k3], u32, ALU.mult, [bI[k3], bU[tt_]], [bI[k3]])
                tt(P, "dve", I_[k3], I_[k3], T_[k], ALU.mult, [bI[k3], bT[k]], [bI[k3]])
                if pi == 0:
                    dest, bdest = HS[:, a0:a0 + TT], bHS[tt_]
                else:
                    dest, bdest = H2[k], bH2[k]
                rd = [bA[k3], bI[k3]]
                if ds["first"]:
                    init = 0.0
                else:
                    pdest, pb = prevd[0]
                    init = pdest[:, TT - 1:TT] if d == 0 else pdest[:, 0:1]
                    rd = rd + [pb]
                if d == 0:
                    P.op("dve", (lambda o_, a_, b_, i_: (lambda e: e.tensor_tensor_scan(
                        out=o_, data0=a_, data1=b_, initial=i_, op0=ALU.mult, op1=ALU.add)))(
                        dest, A_[k3], I_[k3], init), rd, [bdest])
                else:
                    P.op("dve", (lambda o_, a_, b_, i_: (lambda e: e.tensor_tensor_scan(
                        out=o_, data0=a_, data1=b_, initial=i_, op0=ALU.mult, op1=ALU.add)))(
                        dest[:, ::-1], A_[k3][:, ::-1], I_[k3][:, ::-1], init), rd, [bdest])
                prevd[0] = (dest, bdest)
                if pi == 1:
                    tt(P, "pool", T_[k], H2[k], HS[:, a0:a0 + TT], ALU.add, [bH2[k], bHS[tt_]], [bT[k]])
                    tt(P, "dve", ob[:, a0:a0 + TT], T_[k], gl[:, a0:a0 + TT], ALU.mult, [bT[k], bgl], [bob])
                    if ds["lastt"]:
                        dma(P, SQ, oT_d[256 + c * 128:256 + (c + 1) * 128, 0:S], ob, [bob], [])

            nd = len(descs)
            for j in range(min(2, nd)):
                stage_a1(descs[j])
                stage_a2(descs[j])
            for i in range(nd):
                if i + 2 < nd:
                    stage_a1(descs[i + 2])
                stage_b_act(descs[i])
                if i + 2 < nd:
                    stage_a2(descs[i + 2])
                stage_b_rest(descs[i])
            P.barrier()

        def lru_generator(si, l):
            S = seq_lens[si]
            TT = 512
            NTT = S // TT
            pr = PR[l]
            cw, cb, coef, coef2, bdb = pr["cw"], pr["cb"], pr["coef"], pr["coef2"], pr["bdb"]
            nbab, nbxb = pr["nbab"], pr["nbxb"]
            LX = AR.alloc(S + 4, F32)
            GL = AR.alloc(S, BF16)
            OB = AR.alloc(S, BF16)
            U32 = AR.alloc(S, F32)
            UB = AR.alloc(S, BF16)
            HS = AR.alloc(S, F32)
            bLX, bGL, bOB = P.buf("lx"), P.buf("gl"), P.buf("ob")
            bU, bUB, bHS = P.bufs(NTT, "u32"), P.bufs(NTT, "ub"), P.bufs(NTT, "hs")
            NS = 2
            ER = [AR.alloc(TT, F32) for _ in range(NS)]
            EI = [AR.alloc(TT, F32) for _ in range(NS)]
            A_ = [AR.alloc(TT, F32) for _ in range(NS)]
            T_ = [AR.alloc(TT, F32) for _ in range(NS)]
            H2 = [AR.alloc(TT, F32) for _ in range(NS)]
            bER, bEI, bA, bT, bH2 = (P.bufs(NS, "ler"), P.bufs(NS, "lei"), P.bufs(NS, "la"), P.bufs(NS, "lt"),
                                     P.bufs(NS, "lh"))
            descs = []
            for c in range(4):
                passes = [(0, True), (1, False)] if c % 2 == 0 else [(1, False), (0, True)]
                for pi, (d, asc) in enumerate(passes):
                    tiles = list(range(NTT)) if asc else list(range(NTT - 1, -1, -1))
                    for idx, tt_ in enumerate(tiles):
                        descs.append(dict(c=c, pi=pi, d=d, tt=tt_, first=(idx == 0), lastt=(idx == NTT - 1),
                                          k=len(descs) % NS))

            def load_lx(c):
                mset(P, "pool", LX[:, 0:2], 0.0, [bLX])
                mset(P, "pool", LX[:, S + 2:S + 4], 0.0, [bLX])
                dma(P, LQ, LX[:, 2:S + 2], lx_d[c * 128:(c + 1) * 128, 0:S], [], [bLX])

            def stage_a(ds):
                c, pi, d, tt_, k = ds["c"], ds["pi"], ds["d"], ds["tt"], ds["k"]
                a0 = tt_ * TT
                u32 = U32[:, a0:a0 + TT]
                ub = UB[:, a0:a0 + TT]
                if ds["first"] and pi == 0:
                    if c == 0:
                        load_lx(0)
                if ds["first"] and pi == 1 and c + 1 < 4:
                    load_lx(c + 1)
                if pi == 0:
                    ts(P, "dve", u32, LX[:, a0:a0 + TT], cw[:, c, 0:1], cb[:, c:c + 1], ALU.mult, ALU.add,
                       [bLX, bC], [bU[tt_]])
                    for j in range(1, 4):
                        stt(P, u32, LX[:, a0 + j:a0 + j + TT], cw[:, c, j:j + 1], u32, ALU.mult, ALU.add,
                            [bLX, bC, bU[tt_]], [bU[tt_]])
                    yield
                    cp(P, "act", ub, u32, [bU[tt_]], [bUB[tt_]])
                    yield
                mm(P, PS[6][:, :], bdb[:, 0, d, c, :], ub, True, True, [bC, bUB[tt_]], [bPS[6]])
                mm(P, PS[7][:, :], bdb[:, 1, d, c, :], ub, True, True, [bC, bUB[tt_]], [bPS[7]])
                yield
                act(P, ER[k], PS[6][:, :], AF.Exp, [bPS[6], bC], [bER[k]], bias=nbab[:, d, c:c + 1], scale=-1.0)
                act(P, EI[k], PS[7][:, :], AF.Exp, [bPS[7], bC], [bEI[k]], bias=nbxb[:, d, c:c + 1], scale=-1.0)
                yield
                act(P, ER[k], ER[k], AF.Ln, [bER[k]], [bER[k]], bias=1.0)
                act(P, EI[k], EI[k], AF.Ln, [bEI[k]], [bEI[k]], bias=1.0)
                yield
                act(P, ER[k], ER[k], AF.Exp, [bER[k]], [bER[k]], scale=-1.0)
                act(P, EI[k], EI[k], AF.Exp, [bEI[k]], [bEI[k]], scale=-1.0)
                yield

            prevd = [None]

            def stage_b(ds):
                c, pi, d, tt_, k = ds["c"], ds["pi"], ds["d"], ds["tt"], ds["k"]
                a0 = tt_ * TT
                u32 = U32[:, a0:a0 + TT]
                if pi == 0 and ds["lastt"]:
                    dma(P, LQ, GL, gl_d[c * 128:(c + 1) * 128, 0:S], [], [bGL])
                act(P, A_[k], ER[k], AF.Exp, [bER[k], bC], [bA[k]], scale=coef[:, d, c:c + 1])
                act(P, T_[k], ER[k], AF.Exp, [bER[k], bC], [bT[k]], scale=coef2[:, d, c:c + 1])
                yield
                act(P, T_[k], T_[k], AF.Ln, [bT[k]], [bT[k]], scale=-1.0, bias=1.0)
                act(P, T_[k], T_[k], AF.Exp, [bT[k]], [bT[k]], scale=0.5)
                tt(P, "dve", EI[k], EI[k], u32, ALU.mult, [bEI[k], bU[tt_]], [bEI[k]])
                yield
                tt(P, "dve", EI[k], EI[k], T_[k], ALU.mult, [bEI[k], bT[k]], [bEI[k]])
                if pi == 0:
                    dest, bdest = HS[:, a0:a0 + TT], bHS[tt_]
                else:
                    dest, bdest = H2[k], bH2[k]
                rd = [bA[k], bEI[k]]
                if ds["first"]:
                    init = 0.0
                else:
                    pdest, pb = prevd[0]
                    init = pdest[:, TT - 1:TT] if d == 0 else pdest[:, 0:1]
                    rd = rd + [pb]
                if d == 0:
                    P.op("dve", (lambda o_, a_, b_, i_: (lambda e: e.tensor_tensor_scan(
                        out=o_, data0=a_, data1=b_, initial=i_, op0=ALU.mult, op1=ALU.add)))(
                        dest, A_[k], EI[k], init), rd, [bdest])
                else:
                    P.op("dve", (lambda o_, a_, b_, i_: (lambda e: e.tensor_tensor_scan(
                        out=o_, data0=a_, data1=b_, initial=i_, op0=ALU.mult, op1=ALU.add)))(
                        dest[:, ::-1], A_[k][:, ::-1], EI[k][:, ::-1], init), rd, [bdest])
                prevd[0] = (dest, bdest)
                yield
                if pi == 1:
                    tt(P, "pool", T_[k], H2[k], HS[:, a0:a0 + TT], ALU.add, [bH2[k], bHS[tt_]], [bT[k]])
                    tt(P, "dve", OB[:, a0:a0 + TT], T_[k], GL[:, a0:a0 + TT], ALU.mult, [bT[k], bGL], [bOB])
                    if ds["lastt"]:
                        dma(P, SQ, oT_d[256 + c * 128:256 + (c + 1) * 128, 0:S], OB, [bOB], [])
                    yield

            nd = len(descs)
            for x in stage_a(descs[0]):
                yield
            for i in range(nd):
                if i + 1 < nd:
                    for x in stage_a(descs[i + 1]):
                        yield
                for x in stage_b(descs[i]):
                    yield

        def phase_da(si, l):
            S = seq_lens[si]
            NB = S // 128
            lam_init = 0.8 - 0.6 * float(np.exp(-0.3 * l))
            AR.reset()
            pr = PR[l]
            neglam, dn = pr["neglam"], pr["dn"]
            bsm = bC
            QT = AR.alloc(2 * S, BF16).rearrange("p (c t) -> p c t", c=2)
            KT = AR.alloc(2 * S, BF16).rearrange("p (c t) -> p c t", c=2)
            VA = AR.alloc(NB * 512, BF16).rearrange("p (n h f) -> p n h f", n=NB, h=4)
            bQ, bK, bV, bVA = P.bufs(4, "da")
            mset(P, "dve", VA[:, :, :, 64:128], 1.0, [bVA])
            for h_ in range(4):
                dma(P, LQ, VA[:, :, h_, 0:64],
                    vd_d[0:S, 64 * h_:64 * h_ + 64].rearrange("(n p) f -> p n f", p=128), [], [bVA])
            dma(P, LQ, QT, dq_d[:, 0:S].rearrange("(c p) t -> p c t", p=128), [], [bQ])
            dma(P, LQ, KT, dk_d[:, 0:S].rearrange("(c p) t -> p c t", p=128), [], [bK])
            lru_gen = lru_generator(si, l) if FUSE_LRU else None
            NPT = 4
            PT = [AR.alloc(512, BF16) for _ in range(NPT)]
            bPT = P.bufs(NPT, "pt")
            RZ = AR.alloc(512, F32)
            OS = AR.alloc(512, F32)
            ON = [AR.alloc(512, F32) for _ in range(2)]
            DD = AR.alloc(512, F32)
            D2 = AR.alloc(512, BF16)
            RS = AR.alloc(512, F32)
            GD = [AR.alloc(512, BF16) for _ in range(2)]
            OO = [AR.alloc(512, BF16) for _ in range(2)]
            bRZ, bOS, bDD, bD2, bRS = P.bufs(5, "dae")
            bON = P.bufs(2, "on")
            bGD = P.bufs(2, "gd")
            bOO = P.bufs(2, "oo")
            scale = 32.0 ** -0.5
            QM = [AR.alloc(512, BF16) for _ in range(3)]
            bQM = P.bufs(3, "qm")
            NQC = S // 512
            LOOK = 2
            steps = []
            for h in range(4):
                for qc in range(NQC):
                    for s_ in range(2):
                        for kb in range(NB):
                            steps.append((h, qc, s_, kb))
            n = len(steps)
            deferred = []
            gidx = {}

            def epilogue_a(h, qc, s_):
                acc = 3 + s_
                P.op("dve", (lambda a: (lambda e: e.reciprocal(out=RZ[64:128, :], in_=PS[a][64:128, :])))(acc),
                     [bPS[acc]], [bRZ])
                cp(P, "dve", OS[0:64, :], PS[acc][0:64, :], [bPS[acc]], [bOS])

            def epilogue_b1(h, qc, s_):
                mm(P, PS[5][0:64, :], IDENT[64:128, 64:128], RZ[64:128, :], True, True, [bC, bRZ], [bPS[5]])
                tt(P, "dve", ON[s_][0:64, :], OS[0:64, :], PS[5][0:64, :], ALU.mult, [bOS, bPS[5]], [bON[s_]])

            def epilogue_b2(h, qc, s_):
                stt(P, DD[0:64, :], ON[1][0:64, :], neglam, ON[0][0:64, :], ALU.mult, ALU.add,
                    [bON[0], bON[1], bsm], [bDD])
                tt(P, "dve", D2[0:64, :], DD[0:64, :], DD[0:64, :], ALU.mult, [bDD], [bD2])

            def epilogue_b3(h, qc, s_):
                mm(P, PS[5][0:64, :], ONESB[0:64, 0:64], D2[0:64, :], True, True, [bC, bD2], [bPS[5]])

            def epilogue_b4(h, qc, s_):
                q0 = qc * 512
                k = gidx[(h, qc)]
                gdt, bgd = GD[k % 2], bGD[k % 2]
                oot, boo = OO[k % 2], bOO[k % 2]
                rstd_from_ss(PS[5][0:64, :], 512, 64, RS[0:64, :], bRS, bPS[5], 1.0 / 64)
                stt(P, DD[0:64, :], DD[0:64, :], dn[0:64, :], RS[0:64, :], ALU.mult, ALU.mult, [bDD, bsm, bRS],
                    [bDD])
                tt(P, "dve", oot[0:64, :], DD[0:64, :], gdt[0:64, :], ALU.mult, [bDD, bgd], [boo])
                dma(P, SQ, oT_d[768 + 64 * h:768 + 64 * h + 64, q0:q0 + 512], oot[0:64, :], [boo], [])

            groups = []
            for h in range(4):
                for qc in range(NQC):
                    for s_ in range(2):
                        groups.append((h, qc, s_))
            qm_of = {}

            def make_qm(g):
                if g >= len(groups):
                    return
                h, qc, s_ = groups[g]
                hd = 2 * h + s_
                qm, bqm = QM[g % 3], bQM[g % 3]
                qm_of[(h, qc, s_)] = (qm, bqm)
                ts(P, "dve", qm, QT[:, hd // 4, qc * 512:(qc + 1) * 512], HM[:, hd % 4:hd % 4 + 1], None, ALU.mult,
                   None, [bQ, bC], [bqm])

            d1 = max(1, min(8, NB // 4))
            if lru_gen is not None:
                n_y = 4 * 2 * (S // 512) * 10 + 16
                stride = max(1, int(0.92 * n) // n_y)
            make_qm(0)
            for i in range(n + LOOK):
                if lru_gen is not None and i % stride == 0:
                    next(lru_gen, None)
                if i < n:
                    h, qc, s_, kb = steps[i]
                    hd = 2 * h + s_
                    ch = hd // 4
                    q0 = qc * 512
                    if kb == 0:
                        if s_ == 0:
                            k = len(gidx)
                            gidx[(h, qc)] = k
                            dma(P, LQ, GD[k % 2][0:64, :], gd_d[64 * h:64 * h + 64, q0:q0 + 512], [], [bGD[k % 2]])
                        make_qm(i // NB + 1)
                    qm, bqm = qm_of[(h, qc, s_)]
                    sb = i % 3
                    mm(P, PS[sb][:, :], KT[:, ch, kb * 128:(kb + 1) * 128], qm, True, True, [bK, bqm], [bPS[sb]])
                j = i - LOOK
                if j >= 0:
                    h, qc, s_, kb = steps[j]
                    sb = j % 3
                    acc = 3 + s_
                    pt, bpt = PT[j % NPT], bPT[j % NPT]
                    act(P, pt, PS[sb][:, :], AF.Exp, [bPS[sb]], [bpt], scale=scale)
                    mm(P, PS[acc][:, :], VA[:, kb, h, :], pt, kb == 0, kb == NB - 1, [bVA, bpt], [bPS[acc]])
                    deferred.sort(key=lambda t: t[0])
                    while deferred and deferred[0][0] <= j:
                        _, fn_, args = deferred.pop(0)
                        fn_(*args)
                    if kb == NB - 1:
                        epilogue_a(h, qc, s_)
                        deferred.append((j + d1, epilogue_b1, (h, qc, s_)))
                        if s_ == 1:
                            deferred.append((j + 2 * d1, epilogue_b2, (h, qc, s_)))
                            deferred.append((j + 3 * d1, epilogue_b3, (h, qc, s_)))
                            deferred.append((j + 4 * d1, epilogue_b4, (h, qc, s_)))
            deferred.sort(key=lambda t: t[0])
            while deferred:
                _, fn_, args = deferred.pop(0)
                fn_(*args)
            if lru_gen is not None:
                for _ in lru_gen:
                    pass
            P.barrier()

        def phase_hg(si, l):
            S = seq_lens[si]
            NT = S // 128
            N = S // 32
            AR.reset()
            pr = PR[l]
            lbd, c1, c1n, gn = pr["lbd"], pr["c1"], pr["c1n"], pr["gn"]
            bsm = bC
            RM = AR.alloc(S + 32, BF16)
            bRM = P.buf("rm")
            mset(P, "dve", RM, 1.0, [bRM])
            mset(P, "dve", RM.rearrange("p (n j) -> p n j", j=32)[:, :, 0:1], 0.0, [bRM])
            RMf = RM[:, 0:S]
            RMb = RM[:, 1:S + 1]
            VH = AR.alloc(NT * 256, BF16).rearrange("p (n f) -> p n f", n=NT)
            bVH = P.buf("vh")
            dma(P, LQ, VH, vh_d[0:S, :].rearrange("(n p) f -> p n f", p=128), [], [bVH])
            B1 = AR.alloc(S, F32)
            B2 = AR.alloc(S, F32)
            B3 = AR.alloc(S, F32)
            B4 = AR.alloc(S, F32)
            B5 = AR.alloc(S, F32)
            TQ = min(1024, S)
            NTQ = S // TQ
            bB1, bB2, bB3, bB4, bB5 = [P.bufs(NTQ, f"hgB{k}") for k in range(5)]
            QH = AR.alloc(S, BF16)
            QTt = AR.alloc(S, BF16)
            KTt = AR.alloc(S, BF16)
            KH = QH
            KHT = B3.bitcast(BF16)[:, 0:S].rearrange("p (n f) -> p n f", n=NT)
            bQH, bQT, bKT = [P.bufs(NTQ, f"hgq{k}") for k in range(3)]
            bKH = bQH
            bKHT = bB3
            DEC = AR.alloc(N, F32)
            bDEC = P.buf("dec")
            b4b = B4.bitcast(BF16)
            assert (N + 1) * 64 <= 4 * S
            bSSl = bB4 + bB5
            DECR8 = AR.alloc(8 * N, F32)
            bD8 = P.buf("decr8")
            OACC = AR.alloc(S, F32)
            bOA = P.buf("oacc")
            NG = NT // 4
            AT = [[AR.alloc(512, BF16) for _ in range(2)] for _ in range(NG)]
            bAT = [P.bufs(2, "at") for _ in range(NG)]
            OTMP = [AR.alloc(512, F32) for _ in range(2)]
            bOTMP = P.bufs(2, "otmp")
            GH = AR.alloc(512, BF16)
            SQo = AR.alloc(512, BF16)
            RSo = AR.alloc(512, F32)
            OOo = AR.alloc(512, BF16)
            bGH, bSQo, bRSo, bOOo = P.bufs(4, "hgo")
            U32 = bass.AP(B1.tensor, B1.offset, [list(B1.ap[0]), [64, N], [1, 64]]) if N * 64 <= 2 * S else None
            assert U32 is not None
            bU = bB1 + bB2
            for hp in range(2):
                for d in range(2):
                    sg_d = sgf_d if d == 0 else sgb_d
                    RMd = RMf if d == 0 else RMb
                    mask = MASKF if d == 0 else MASKB
                    NCQ = TQ // 32
                    last = 31 if d == 0 else 0
                    sls = [slice(q * TQ, (q + 1) * TQ) for q in range(NTQ)]
                    QS = list(enumerate(sls))
                    for q, sl in QS:
                        dma(P, LQ, B1[:, sl], sg_d[hp * 128:(hp + 1) * 128, sl], [], [bB1[q]])
                        dma(P, LQ, QH[:, sl], qh_d[hp * 128:(hp + 1) * 128, sl], [], [bQH[q]])
                    for q, sl in QS:
                        ts(P, "dve", B2[:, sl], B1[:, sl], c1[:, d, hp:hp + 1], lbd[:, d, hp:hp + 1], ALU.mult,
                           ALU.add, [bB1[q], bsm], [bB2[q]])
                    for q, sl in QS:
                        act(P, B2[:, sl], B2[:, sl], AF.Ln, [bB2[q]], [bB2[q]])
                    for q, sl in QS:
                        act(P, B1[:, sl], B1[:, sl], AF.Identity, [bB1[q], bsm], [bB1[q]], bias=c1[:, d, hp:hp + 1],
                            scale=c1n[:, d, hp:hp + 1])
                    for q, sl in QS:
                        if d == 0:
                            scan(P, B3[:, sl], RMd[:, sl], B2[:, sl], [bRM, bB2[q]], [bB3[q]])
                        else:
                            scan(P, B3[:, sl][:, ::-1], RMd[:, sl][:, ::-1], B2[:, sl][:, ::-1], [bRM, bB2[q]],
                                 [bB3[q]])
                    for q, sl in QS:
                        act(P, B4[:, sl], B3[:, sl], AF.Exp, [bB3[q]], [bB4[q]])
                    for q, sl in QS:
                        act(P, B5[:, sl], B3[:, sl], AF.Exp, [bB3[q]], [bB5[q]], scale=-1.0)
                    for q, sl in QS:
                        tt(P, "dve", QTt[:, sl], QH[:, sl], B4[:, sl], ALU.mult, [bQH[q], bB4[q]], [bQT[q]])
                    for q, sl in QS:
                        tt(P, "dve", KTt[:, sl], B1[:, sl], B5[:, sl], ALU.mult, [bB1[q], bB5[q]], [bKT[q]])
                    for q, sl in QS:
                        clb = bass.AP(B3.tensor, B3.offset + q * TQ + last, [list(B3.ap[0]), [32, NCQ], [0, 32]])
                        tt(P, "dve", B5[:, sl].rearrange("p (n j) -> p n j", j=32), clb,
                           B3[:, sl].rearrange("p (n j) -> p n j", j=32), ALU.subtract, [bB3[q], bB5[q]], [bB5[q]])
                    for q, sl in QS:
                        act(P, B5[:, sl], B5[:, sl], AF.Exp, [bB5[q]], [bB5[q]])
                    for q, sl in QS:
                        tt(P, "dve", KH[:, sl], B1[:, sl], B5[:, sl], ALU.mult, [bB1[q], bB5[q]], [bKH[q]])
                    e4 = bass.AP(B4.tensor, B4.offset + last, [list(B4.ap[0]), [32, N]])
                    cp(P, "dve", DEC, e4, bB4, [bDEC])
                    if HG_STOP <= 1:
                        continue
                    for T0 in range(0, NT, 4):
                        b = (T0 // 4) % 2
                        psb = PS[b].bitcast(BF16)
                        for t4 in range(4):
                            T = T0 + t4
                            tr(P, psb[:, t4 * 128:(t4 + 1) * 128], KH[:, T * 128:(T + 1) * 128], IDENTB,
                               [bKH[(T * 128) // TQ], bC], [bPS[b]])
                        cp(P, "act", KHT[:, T0:T0 + 4, :], psb[:, 0:512].rearrange("p (t f) -> p t f", t=4),
                           [bPS[b]], bKHT)
                    if HG_STOP <= 2:
                        continue
                    TG = 8
                    for T0 in range(0, NT, TG):
                        nt = min(TG, NT - T0)
                        for tt_ in range(nt):
                            T = T0 + tt_
                            for m in range(4):
                                for e_ in range(2):
                                    mm(P, PS[2 + m][64 * e_:64 * e_ + 64, tt_ * 64:(tt_ + 1) * 64],
                                       KHT[32 * m:32 * m + 32, T, 64 * e_:64 * e_ + 64],
                                       VH[32 * m:32 * m + 32, T, 64 * (2 * hp + e_):64 * (2 * hp + e_) + 64],
                                       True, True, bKHT + [bVH], [bPS[2 + m]], tp=(32 * m, 64 * e_))
                        for m in range(4):
                            udst = bass.AP(B1.tensor, B1.offset + (4 * T0 + m),
                                           [list(B1.ap[0]), [4, nt], [N, 64]])
                            cp(P, "act", udst,
                               PS[2 + m][:, 0:nt * 64].rearrange("p (t v) -> p t v", v=64), [bPS[2 + m]], bU)
                    if HG_STOP <= 3:
                        continue
                    decb = bass.AP(DEC.tensor, DEC.offset, [list(DEC.ap[0]), [0, 8], [1, N]])
                    d8v = DECR8.rearrange("p (g n) -> p g n", g=8)
                    cp(P, "act", d8v, decb, [bDEC], [bD8])
                    mset(P, "pool", d8v[:, :, 0:1] if d == 0 else d8v[:, :, N - 1:N], 0.0, [bD8])
                    def emit_scan(g):
                        u_g = bass.AP(B1.tensor, B1.offset + 8 * g * N, [list(B1.ap[0]), [1, 8 * N]])
                        o_g = bass.AP(b4b.tensor, b4b.offset + 8 * g * N, [list(b4b.ap[0]), [1, 8 * N]])
                        if d == 0:
                            scan(P, o_g, DECR8, u_g, [bD8] + bU, bSSl)
                        else:
                            scan(P, o_g[:, ::-1], DECR8[:, ::-1], u_g[:, ::-1], [bD8] + bU, bSSl)

                    vper = (8 + NG - 1) // NG
                    vnext = 0
                    for G in range(NG):
                        for e_ in range(2):
                            sb = (2 * G + e_) % 4
                            for t4 in range(4):
                                T = G * 4 + t4
                                qi = (T * 128) // TQ
                                mm(P, PS[sb][:, t4 * 128:(t4 + 1) * 128], KTt[64 * e_:64 * e_ + 64, T * 128:(T + 1) * 128],
                                   QTt[64 * e_:64 * e_ + 64, T * 128:(T + 1) * 128], True, True, [bKT[qi], bQT[qi]],
                                   [bPS[sb]])
                        for _ in range(vper):
                            if vnext < 8:
                                emit_scan(vnext)
                                vnext += 1
                        for e_ in range(2):
                            sb = (2 * G + e_) % 4
                            at, bat = AT[G][e_], bAT[G][e_]
                            mb = bass.AP(mask.tensor, mask.offset, [list(mask.ap[0]), [0, 4], [1, 128]])
                            cp(P, "act", at, PS[sb][:, :], [bPS[sb]], [bat])
                            tt(P, "pool", at.rearrange("p (a i) -> p a i", a=4),
                               at.rearrange("p (a i) -> p a i", a=4), mb, ALU.mult, [bat, bC], [bat])
                    while vnext < 8:
                        emit_scan(vnext)
                        vnext += 1
                    for G in range(NG):
                        for e_ in range(2):
                            ob = 6 + e_
                            at, bat = AT[G][e_], bAT[G][e_]
                            for t4 in range(4):
                                T = G * 4 + t4
                                qi = (T * 128) // TQ
                                vcol = 64 * (2 * hp + e_)
                                mm(P, PS[ob][64 * e_:64 * e_ + 64, t4 * 128:(t4 + 1) * 128],
                                   VH[:, T, vcol:vcol + 64], at[:, t4 * 128:(t4 + 1) * 128], True, False,
                                   [bVH, bat], [bPS[ob]], tp=(0, 64 * e_))
                                ms = []
                                for m in range(4):
                                    n = 4 * T + m
                                    sidx = n - 1 if d == 0 else n + 1
                                    if 0 <= sidx < N:
                                        ms.append((m, sidx))
                                for (m, sidx) in ms:
                                    pst = b4b.ap[0][0]
                                    lw = bass.AP(b4b.tensor, b4b.offset + 64 * e_ * pst + sidx, [[pst, 64], [N, 64]])
                                    mm(P, PS[ob][64 * e_:64 * e_ + 64, t4 * 128 + 32 * m:t4 * 128 + 32 * m + 32],
                                       lw,
                                       QTt[64 * e_:64 * e_ + 64, T * 128 + 32 * m:T * 128 + 32 * m + 32],
                                       False, m == ms[-1][0], bSSl + [bQT[qi]], [bPS[ob]], tp=(64 * e_, 64 * e_))
                        for e_ in range(2):
                            ob = 6 + e_
                            oa = OACC[64 * e_:64 * e_ + 64, G * 512:(G + 1) * 512]
                            if d == 0:
                                cp(P, "act", oa, PS[ob][64 * e_:64 * e_ + 64, :], [bPS[ob]], [bOA])
                            else:
                                otmp, botmp = OTMP[e_], bOTMP[e_]
                                cp(P, "act", otmp[64 * e_:64 * e_ + 64, :], PS[ob][64 * e_:64 * e_ + 64, :], [bPS[ob]],
                                   [botmp])
                                tt(P, "pool", oa, oa, otmp[64 * e_:64 * e_ + 64, :], ALU.add, [bOA, botmp], [bOA])
                if HG_STOP <= 5:
                    continue
                for G in range(S // 512):
                    oa = OACC[:, G * 512:(G + 1) * 512]
                    dma(P, LQ, GH, gh_d[hp * 128:(hp + 1) * 128, G * 512:(G + 1) * 512], [], [bGH])
                    tt(P, "pool", SQo, oa, oa, ALU.mult, [bOA], [bSQo])
                    mm(P, PS[0][:, :], BONES, SQo, True, True, [bC, bSQo], [bPS[0]])
                    rstd_from_ss(PS[0][:, :], 512, 128, RSo, bRSo, bPS[0], 1.0 / 64)
                    stt(P, RSo, oa, gn[:, 0:1], RSo, ALU.mult, ALU.mult, [bOA, bsm, bRSo], [bRSo])
                    tt(P, "dve", OOo, RSo, GH, ALU.mult, [bRSo, bGH], [bOOo])
                    dma(P, SQ, oT_d[hp * 128:(hp + 1) * 128, G * 512:(G + 1) * 512], OOo, [bOOo], [])
            P.barrier()

        def phase3(si, l):
            S = seq_lens[si]
            last = (l == DEPTH - 1)
            NTI = S // 512
            AR.reset()
            WO = AR.alloc(8 * D, BF16).rearrange("p (k f) -> p k f", k=8)
            WGt = AR.alloc(8 * D, BF16).rearrange("p (k f) -> p k f", k=8)
            WP = AR.alloc(2 * D, BF16).rearrange("p (k f) -> p k f", k=2)
            bW = P.buf("w3")
            bWG3 = P.buf("w3g")
            NH = 3
            ot = [AR.alloc(8 * 512, BF16) for _ in range(2)]
            ht = [AR.alloc(8 * 512, F32) for _ in range(NH)]
            pt = [AR.alloc(4 * PLE, F32) for _ in range(2)]
            bot = P.bufs(2, "ot")
            bht = [P.bufs(8, f"ht3_{k}") for k in range(NH)]
            bpt = P.bufs(2, "pt3")
            pTs = [AR.alloc(2 * 512, BF16).rearrange("p (k t) -> p k t", k=2) for _ in range(2)]
            bpT = P.bufs(2, "pT")
            hnbs = [AR.alloc(8 * 512, BF16).rearrange("p (c t) -> p c t", c=8) for _ in range(2)]
            bhnb = P.bufs(2, "hnb")
            gt = [AR.alloc(512, F32) for _ in range(2)]
            bgt = P.bufs(2, "gt")
            sq = AR.alloc(8 * 512, BF16).rearrange("p (c t) -> p c t", c=8) if last else None
            bsq = P.buf("sq3")
            rst = AR.alloc(512, F32)
            brst = P.buf("rst3")
            yt = AR.alloc(4 * D, F32).rearrange("p (s f) -> p s f", s=4) if last else None
            byt = P.buf("yt")
            bi = [0]

            def nb():
                b = 1 + bi[0] % 7
                bi[0] += 1
                return b

            def stage_loads(ti):
                t0 = ti * 512
                O = ot[ti % 2].rearrange("p (c t) -> p c t", c=8)
                H = ht[ti % NH].rearrange("p (c t) -> p c t", c=8)
                bH = bht[ti % NH]
                PT_ = pt[ti % 2].rearrange("p (s f) -> p s f", s=4)
                dma(P, LQ, PT_, ps_in[si][l, t0:t0 + 512, :].rearrange("(s p) f -> p s f", p=128), [],
                    [bpt[ti % 2]])
                dma(P, LQ, O, oT_d[:, t0:t0 + 512].rearrange("(c p) t -> p c t", p=128), [], [bot[ti % 2]])
                dma(P, LQ, H, hT_d[:, :, t0:t0 + 512].rearrange("c p t -> p c t"), [], bH)

            def stage_wo(ti):
                t0 = ti * 512
                O = ot[ti % 2].rearrange("p (c t) -> p c t", c=8)
                H = ht[ti % NH].rearrange("p (c t) -> p c t", c=8)
                bH = bht[ti % NH]
                PT_ = pt[ti % 2].rearrange("p (s f) -> p s f", s=4)
                pT = pTs[ti % 2]
                hnb = hnbs[ti % 2]
                for pc in range(2):
                    b = nb()
                    for s_ in range(4):
                        tr(P, PS[b][:, s_ * 128:(s_ + 1) * 128], PT_[:, s_, pc * 128:(pc + 1) * 128], IDENT,
                           [bpt[ti % 2], bC], [bPS[b]])
                    cp(P, "act", pT[:, pc, :], PS[b][:, :], [bPS[b]], [bpT[ti % 2]])
                for c in range(8):
                    b = nb()
                    for kc in range(8):
                        mm(P, PS[b][:, :], WO[:, kc, c * 128:(c + 1) * 128], O[:, kc, :], kc == 0, kc == 7,
                           [bW, bot[ti % 2]], [bPS[b]])
                    tt(P, "dve", H[:, c, :], H[:, c, :], PS[b][:, :], ALU.add, [bH[c], bPS[b]], [bH[c]])
                    cp(P, "act", hnb[:, c, :], H[:, c, :], [bH[c]], [bhnb[ti % 2]])

            def stage_gate(ti):
                t0 = ti * 512
                H = ht[ti % NH].rearrange("p (c t) -> p c t", c=8)
                bH = bht[ti % NH]
                pT = pTs[ti % 2]
                hnb = hnbs[ti % 2]
                for c in range(8):
                    b = nb()
                    for kc in range(8):
                        mm(P, PS[b][:, :], WGt[:, kc, c * 128:(c + 1) * 128], hnb[:, kc, :], kc == 0, kc == 7,
                           [bWG3, bhnb[ti % 2]], [bPS[b]])
                    g, bg = gt[c % 2], bgt[c % 2]
                    act(P, g, PS[b][:, :], AF.Sigmoid, [bPS[b]], [bg])
                    b2 = nb()
                    for pc in range(2):
                        mm(P, PS[b2][:, :], WP[:, pc, c * 128:(c + 1) * 128], pT[:, pc, :], pc == 0, pc == 1,
                           [bWG3, bpT[ti % 2]], [bPS[b2]])
                    tt(P, "dve", g, g, PS[b2][:, :], ALU.mult, [bg, bPS[b2]], [bg])
                    tt(P, "dve", H[:, c, :], H[:, c, :], g, ALU.add, [bH[c], bg], [bH[c]])
                if not last:
                    dma(P, SQ, hT_d[:, :, t0:t0 + 512].rearrange("c p t -> p c t"), H, bH, [])

            def stage_fina(ti):
                H = ht[ti % NH].rearrange("p (c t) -> p c t", c=8)
                bH = bht[ti % NH]
                tt(P, "pool", sq, H, H, ALU.mult, bH, [bsq])
                for c in range(8):
                    mm(P, PS[0][:, :], ONESB, sq[:, c, :], c == 0, c == 7, [bC, bsq], [bPS[0]])
                rstd_from_ss(PS[0][:, :], 512, 128, rst, brst, bPS[0], 1.0 / D)
                for c in range(8):
                    stt(P, H[:, c, :], H[:, c, :], fg[:, c:c + 1], rst, ALU.mult, ALU.mult,
                        [bH[c], bC, brst], [bH[c]])

            def stage_finb(ti):
                t0 = ti * 512
                H = ht[ti % NH].rearrange("p (c t) -> p c t", c=8)
                bH = bht[ti % NH]
                for s_ in range(4):
                    for half in range(2):
                        b = nb()
                        for cc in range(4):
                            c = half * 4 + cc
                            tr(P, PS[b][:, cc * 128:(cc + 1) * 128], H[:, c, s_ * 128:(s_ + 1) * 128], IDENT,
                               [bH[c], bC], [bPS[b]])
                        cp(P, "act" if half == 0 else "dve", yt[:, s_, half * 512:(half + 1) * 512], PS[b][:, :],
                           [bPS[b]], [byt])
                dma(P, SQ, ys[si][t0:t0 + 512, :].rearrange("(s p) f -> p s f", p=128), yt, [byt], [])

            stage_loads(0)
            dma(P, LQ, WO, WOUT_d[l], [], [bW])
            for k in range(NTI + 2):
                if last and 0 <= k - 2 < NTI:
                    stage_fina(k - 2)
                if k < NTI:
                    stage_wo(k)
                if k == 0:
                    dma(P, LQ, WGt, WG_d[l], [], [bWG3])
                    dma(P, LQ, WP, WPLE_d[l], [], [bWG3])
                if last and 0 <= k - 2 < NTI:
                    stage_finb(k - 2)
                if k + 1 < NTI:
                    stage_loads(k + 1)
                if 0 <= k - 1 < NTI:
                    stage_gate(k - 1)
            P.barrier()

        ph = PHASES
        if "w" in ph:
            prep_weights()
        for si in range(NSEQ):
            if "0" in ph:
                phase0(si)
            for l in range(DEPTH if "L2" in ph else 1):
                if "1" in ph:
                    phase1(si, l)
                if "h" in ph:
                    phase_hg(si, l)
                if "l" in ph and not FUSE_LRU:
                    phase_lru(si, l)
                if "d" in ph:
                    phase_da(si, l)
                if "3" in ph:
                    phase3(si, l)
        P.emit()
    return nc, P


def _consts(smax):
    ident = np.eye(128, dtype=np.float32)
    half = 4
    inv = (500000.0 ** (-np.arange(half, dtype=np.float32) * 2.0 / 8)).astype(np.float32)
    pos = np.arange(smax, dtype=np.float32)
    ang = pos[None, :] * inv[:, None]
    cos = np.ones((128, smax), np.float32)
    sin = np.zeros((128, smax), np.float32)
    for p in range(128):
        d = p % 32
        if d < 8:
            cos[p] = np.cos(ang[d % 4])
            sin[p] = np.sin(ang[d % 4])
    j = np.arange(128)[:, None]
    i = np.arange(128)[None, :]
    same = (j // 32) == (i // 32)
    maskf = (same & (j <= i)).astype(np.float32)
    maskb = (same & (j >= i)).astype(np.float32)
    bones = ((j // 64) == (i // 64)).astype(np.float32)
    return {"c_ident": ident, "c_cos": cos, "c_sin": sin, "c_maskf": maskf, "c_maskb": maskb, "c_bones": bones}


_WNAMES = ("norm_g", "w_in", "w_out", "hg_lb", "hg_norm", "lru_conv_w", "lru_conv_b", "lru_wa", "lru_ba", "lru_wx",
           "lru_bx", "lru_lam", "da_lq1", "da_lk1", "da_lq2", "da_lk2", "da_norm", "ple_w", "ple_gate_w",
           "final_norm")


def run_cores(seq_lens, per_core_x, per_core_p, weights, debug=False):
    nc, P = build_program(seq_lens, debug=debug)
    consts = _consts(max(seq_lens))
    in_maps = []
    for c in range(len(per_core_x)):
        m = {}
        for i in range(len(seq_lens)):
            m[f"x{i}"] = np.ascontiguousarray(per_core_x[c][i], dtype=np.float32)
            m[f"p{i}"] = np.ascontiguousarray(per_core_p[c][i], dtype=np.float32)
        for k in _WNAMES:
            m[k] = np.ascontiguousarray(weights[k], dtype=np.float32)
        m.update(consts)
        in_maps.append(m)
    res = run_bass_kernel_spmd(nc, in_maps, core_ids=list(range(len(per_core_x))))
    return res.results


def kernel(**inputs):
    n = 8
    xp = np.asarray(inputs["x_prompt"])
    xsm = np.asarray(inputs["x_sample"])
    pp = np.asarray(inputs["p_prompt"])
    psm = np.asarray(inputs["p_sample"])
    B, S1, _ = xp.shape
    B2, S2, _ = xsm.shape
    bp, bs = B // n, B2 // n
    seq_lens = [S1] * bp + [S2] * bs
    pcx, pcp = [], []
    for c in range(n):
        x_list = [xp[c * bp + i] for i in range(bp)] + [xsm[c * bs + i] for i in range(bs)]
        p_list = [pp[:, c * bp + i] for i in range(bp)] + [psm[:, c * bs + i] for i in range(bs)]
        pcx.append(x_list)
        pcp.append(p_list)
    weights = {k: np.asarray(inputs[k]) for k in _WNAMES}
    results = run_cores(seq_lens, pcx, pcp, weights)
    yp = np.empty((B, S1, D), np.float32)
    ysm = np.empty((B2, S2, D), np.float32)
    for c in range(n):
        r = results[c]
        for i in range(bp):
            yp[c * bp + i] = r[f"y{i}"]
        for i in range(bs):
            ysm[c * bs + i] = r[f"y{bp + i}"]
    return (yp, ysm)
```
